# Optimizing a Trainium2 kernel written in Bass

```python
import math
import jax, jax.numpy as jnp
from jax import lax
import numpy as np

D_MODEL = 1024
BATCH = 8
SEQ = 4096
DEPTH = 2

MIX_WIDTH = D_MODEL
ATT_WIDTH = MIX_WIDTH // 2
FOURIER_WIDTH = MIX_WIDTH - ATT_WIDTH
N_HEADS = 8
HEAD_DIM = ATT_WIDTH // (2 * N_HEADS)
V_HEAD_DIM = 2 * HEAD_DIM
N_FGROUPS = 4
FGROUP_DIM = FOURIER_WIDTH // N_FGROUPS
IN_COLS = 3 * ATT_WIDTH + FOURIER_WIDTH
D_FF = ((8 * D_MODEL + 3 * 256 - 1) // (3 * 256)) * 256
Q_BLOCK = 128
LN_EPS = 1e-5
SUBLN_EPS = 1e-5
ALPHA = (2.0 * DEPTH) ** 0.25
BETA = (8.0 * DEPTH) ** -0.25
LAMBDA_STD = 0.1

kernel_name = 'hymba_diffattn_fnet_deepnorm_encoder'


def layer_norm(x, g, b):
    x32 = x.astype(jnp.float32)
    mu = jnp.mean(x32, axis=-1, keepdims=True)
    var = jnp.mean(jnp.square(x32 - mu), axis=-1, keepdims=True)
    return ((x32 - mu) * lax.rsqrt(var + LN_EPS) * g + b).astype(x.dtype)


def lambda_init_fn(layer_idx):
    return 0.8 - 0.6 * math.exp(-0.3 * layer_idx)


def diff_attention(q, k, v, lam, lam_init, subln_g):
    B, S = q.shape[0], q.shape[1]
    nb = S // Q_BLOCK
    slopes = jnp.exp2(-8.0 * jnp.arange(1, N_HEADS + 1, dtype=jnp.float32) / N_HEADS)
    kpos = jnp.arange(S, dtype=jnp.float32)
    scale = HEAD_DIM ** -0.5
    qb = q.reshape(B, nb, Q_BLOCK, N_HEADS, 2, HEAD_DIM).transpose(1, 0, 2, 3, 4, 5)
    starts = jnp.arange(nb, dtype=jnp.float32) * Q_BLOCK

    def block(args):
        qi, start = args
        s = jnp.einsum('bqhcd,bkhcd->bhcqk', qi, k).astype(jnp.float32) * scale
        qpos = start + jnp.arange(Q_BLOCK, dtype=jnp.float32)
        dist = jnp.abs(qpos[:, None] - kpos[None, :])
        s = s - slopes[:, None, None, None] * dist[None, None]
        p = jax.nn.softmax(s, axis=-1)
        a = p[:, :, 0] - lam * p[:, :, 1]
        o = jnp.einsum('bhqk,bkhd->bqhd', a.astype(v.dtype), v).astype(jnp.float32)
        o = o * lax.rsqrt(jnp.mean(jnp.square(o), axis=-1, keepdims=True) + SUBLN_EPS)
        o = o * subln_g * (1.0 - lam_init)
        return o.astype(v.dtype)

    out = lax.map(block, (qb, starts))
    return out.transpose(1, 0, 2, 3, 4).reshape(B, S, N_HEADS * V_HEAD_DIM)


def fourier_mix(u, w_f, b_f):
    B, S = u.shape[0], u.shape[1]
    ug = u.reshape(B, S, N_FGROUPS, FGROUP_DIM).astype(jnp.float32)
    f = jnp.fft.fft(ug, axis=3, norm='ortho')
    f = jnp.fft.fft(f, axis=1, norm='ortho').real.astype(u.dtype)
    y = jnp.einsum('bsgc,gcd->bsgd', f, w_f)
    return y.reshape(B, S, FOURIER_WIDTH) + b_f


def swiglu(h, w_gu, w_down):
    gu = jnp.einsum('bsd,df->bsf', h, w_gu)
    g, up = gu[..., :D_FF], gu[..., D_FF:]
    return jnp.einsum('bsf,fd->bsd', jax.nn.silu(g) * up, w_down)


def setup_inputs(seed: int = 0) -> dict:
    key = jax.random.key(seed)
    ks = jax.random.split(key, 16)
    f32 = jnp.float32
    nrm = lambda k, shape: jax.random.normal(k, shape, f32)
    return {
        'x': nrm(ks[0], (BATCH, SEQ, D_MODEL)),
        'ln_in_g': 1.0 + 0.01 * nrm(ks[1], (D_MODEL,)),
        'ln_in_b': 0.01 * nrm(ks[2], (D_MODEL,)),
        'w_in': nrm(ks[3], (DEPTH, D_MODEL, IN_COLS)) * D_MODEL ** -0.5,
        'lam_params': LAMBDA_STD * nrm(ks[4], (DEPTH, 4, HEAD_DIM)),
        'subln_g': 1.0 + 0.01 * nrm(ks[5], (DEPTH, V_HEAD_DIM)),
        'w_f': nrm(ks[6], (DEPTH, N_FGROUPS, FGROUP_DIM, FGROUP_DIM)) * FGROUP_DIM ** -0.5,
        'b_f': 0.01 * nrm(ks[7], (DEPTH, FOURIER_WIDTH)),
        'w_o': nrm(ks[8], (DEPTH, MIX_WIDTH, D_MODEL)) * (MIX_WIDTH ** -0.5 * BETA),
        'ln1_g': 1.0 + 0.01 * nrm(ks[9], (DEPTH, D_MODEL)),
        'ln1_b': 0.01 * nrm(ks[10], (DEPTH, D_MODEL)),
        'w_gu': nrm(ks[11], (DEPTH, D_MODEL, 2 * D_FF)) * D_MODEL ** -0.5,
        'w_down': nrm(ks[12], (DEPTH, D_FF, D_MODEL)) * (D_FF ** -0.5 * BETA),
        'ln2_g': 1.0 + 0.01 * nrm(ks[13], (DEPTH, D_MODEL)),
        'ln2_b': 0.01 * nrm(ks[14], (DEPTH, D_MODEL)),
    }


def reference(x, ln_in_g, ln_in_b, w_in, lam_params, subln_g, w_f, b_f, w_o,
              ln1_g, ln1_b, w_gu, w_down, ln2_g, ln2_b):
    B, S = x.shape[0], x.shape[1]
    h = layer_norm(x, ln_in_g, ln_in_b)
    for l in range(DEPTH):
        lam_init = lambda_init_fn(l)
        proj = jnp.einsum('bsd,de->bse', h, w_in[l])
        q = proj[..., :ATT_WIDTH].reshape(B, S, N_HEADS, 2, HEAD_DIM)
        k = proj[..., ATT_WIDTH:2 * ATT_WIDTH].reshape(B, S, N_HEADS, 2, HEAD_DIM)
        v = proj[..., 2 * ATT_WIDTH:3 * ATT_WIDTH].reshape(B, S, N_HEADS, V_HEAD_DIM)
        u = proj[..., 3 * ATT_WIDTH:]
        lp = lam_params[l].astype(jnp.float32)
        lam = jnp.exp(jnp.sum(lp[0] * lp[1])) - jnp.exp(jnp.sum(lp[2] * lp[3])) + lam_init
        a = diff_attention(q, k, v, lam, lam_init, subln_g[l])
        f = fourier_mix(u, w_f[l], b_f[l])
        mix = jnp.einsum('bse,ed->bsd', jnp.concatenate([a, f], axis=-1), w_o[l])
        h = layer_norm(ALPHA * h + mix, ln1_g[l], ln1_b[l])
        h = layer_norm(ALPHA * h + swiglu(h, w_gu[l], w_down[l]), ln2_g[l], ln2_b[l])
    return h
```

```python
import math
import contextlib
import numpy as np
import ml_dtypes
import concourse.bass as bass
import concourse.mybir as mybir
from concourse.bass_utils import run_bass_kernel_spmd

F32 = mybir.dt.float32
BF16 = mybir.dt.bfloat16
F32R = mybir.dt.float32r
AF = mybir.ActivationFunctionType
ALU = mybir.AluOpType

S = 4096
D = 1024
NT = 32
DEPTH = 2
DFF = 2816
NF = 22
NH = 8
ALPHA = (2.0 * DEPTH) ** 0.25
SCALE = 32 ** -0.5
LN_EPS = 1e-5
SUBLN_EPS = 1e-5
SKIP_BIAS = 64.0
SAME_ENGINE_SYNC = True


def lam_init_fn(l):
    return 0.8 - 0.6 * math.exp(-0.3 * l)


class Op:
    __slots__ = ("eng", "fn", "dma_key", "dma_cnt", "signal", "sig_idx", "waits")

    def __init__(self, eng, fn, dma_key):
        self.eng = eng
        self.fn = fn
        self.dma_key = dma_key
        self.dma_cnt = 0
        self.signal = False
        self.sig_idx = 0
        self.waits = []


class Prog:
    ENGS = ("pe", "act", "dve", "pool", "sp")

    def __init__(self, nc):
        self.nc = nc
        self.streams = {e: [] for e in self.ENGS}
        self.res = {}
        self.dma_counts = {}

    def add(self, eng, fn, r=(), w=(), dma=None):
        op = Op(eng, fn, dma)
        if dma is not None:
            c = self.dma_counts.get(dma, 0) + 1
            self.dma_counts[dma] = c
            op.dma_cnt = c
        deps = []
        for k in r:
            st = self.res.get(k)
            if st is None:
                st = [None, []]
                self.res[k] = st
            if st[0] is not None:
                deps.append(st[0])
            st[1].append(op)
        for k in w:
            st = self.res.get(k)
            if st is not None:
                if st[0] is not None:
                    deps.append(st[0])
                for rd in st[1]:
                    if rd is not op:
                        deps.append(rd)
            self.res[k] = [op, []]
        seen = set()
        for d in deps:
            if id(d) in seen or d is op:
                continue
            seen.add(id(d))
            if d.dma_key is None:
                if d.eng == eng and (eng == "pe" or not SAME_ENGINE_SYNC):
                    continue
                d.signal = True
            op.waits.append(d)
        self.streams[eng].append(op)
        return op

    def emit(self):
        nc = self.nc
        with contextlib.ExitStack() as es:
            esem = {e: es.enter_context(nc.semaphore("s_" + e)) for e in ("pe", "act", "dve", "pool")}
            dsem = {k: es.enter_context(nc.semaphore("d_%d" % i)) for i, k in enumerate(self.dma_counts)}
            for e, st in self.streams.items():
                c = 0
                for op in st:
                    if op.dma_key is None and op.signal:
                        c += 1
                        op.sig_idx = c
            block = es.enter_context(nc.Block())
            handles = {"pe": block.tensor, "act": block.scalar, "dve": block.vector,
                       "pool": block.gpsimd, "sp": block.sync}

            def mk(ename):
                st = self.streams[ename]

                def body(eng):
                    waited = {}
                    for op in st:
                        for d in op.waits:
                            if d.dma_key is not None:
                                key = ("d", d.dma_key)
                                val = d.dma_cnt * 16
                                sem = dsem[d.dma_key]
                            else:
                                key = ("e", d.eng)
                                val = d.sig_idx
                                sem = esem[d.eng]
                            if waited.get(key, 0) >= val:
                                continue
                            waited[key] = val
                            eng.wait_ge(sem, val)
                        inst = op.fn(eng)
                        if inst is None:
                            continue
                        if op.dma_key is not None:
                            inst.then_inc(dsem[op.dma_key], 16)
                        elif op.signal:
                            inst.then_inc(esem[op.eng], 1)

                return body

            for ename in self.ENGS:
                if self.streams[ename]:
                    handles[ename](mk(ename))


_CONSTS = None


def _consts():
    global _CONSTS
    if _CONSTS is not None:
        return _CONSTS
    bf = ml_dtypes.bfloat16
    c = {}
    c["idf"] = np.eye(128, dtype=np.float32)
    c["idb"] = np.eye(128, dtype=np.float32).astype(bf)
    ab = np.outer(np.arange(128), np.arange(128)) % 128
    ang = 2.0 * np.pi * ab / 128.0
    c["ccf"] = (np.cos(ang) / math.sqrt(128.0)).astype(np.float32)
    c["scn"] = (-np.sin(ang) / math.sqrt(128.0)).astype(np.float32)
    k = np.arange(S, dtype=np.int64)
    tab_c = (np.cos(2.0 * np.pi * k / S) / 64.0).astype(np.float32).astype(bf)
    tab_s = (np.sin(2.0 * np.pi * k / S) / 64.0).astype(np.float32).astype(bf)
    half = k[:S // 2]
    st = np.concatenate([np.outer(half, 2 * half), np.outer(half, 2 * half + 1)], axis=0)
    st = (st % S).astype(np.int32)
    c["cst"] = np.ascontiguousarray(np.stack([tab_c[st].reshape(S, 4, 512), tab_s[st].reshape(S, 4, 512)], axis=2))
    pos = np.arange(S)
    pa = (pos // 64).astype(np.float32) * 64.0
    pb = (pos % 64).astype(np.float32)
    qx = np.zeros((NH, 32, S), np.float32)
    klo = np.zeros((NH, 32, S), np.float32)
    for h in range(NH):
        sl = 2.0 ** -(h + 1)
        qx[h, 1] = -sl * pa
        qx[h, 2] = -sl * pb
        qx[h, 3] = 1.0
        qx[h, 4] = 1.0
        klo[h, 1] = 1.0
        klo[h, 2] = 1.0
        klo[h, 3] = sl * pa
        klo[h, 4] = sl * pb
    c["qx"] = qx.astype(bf)
    c["kxlo"] = klo.astype(bf)
    c["kxup"] = (-klo).astype(bf)
    c["kxdg"] = np.zeros((NH, 32, S), np.float32).astype(bf)
    dg = np.zeros((128, NH, 128), np.float32)
    pf = np.abs(np.arange(128)[None, :] - np.arange(128)[:, None]).astype(np.float32)
    for h in range(NH):
        dg[:, h, :] = -(2.0 ** -(h + 1)) * pf
    c["dg"] = dg.astype(bf)
    _CONSTS = c
    return c


def build(dbg=None, stop_after=None):
    dbg = dbg or set()
    nc = bass.Bass("TRN2", target_bir_lowering=False)

    def din(name, shape, dtype):
        return nc.dram_tensor(name, shape, dtype, kind="ExternalInput")

    def dscr(name, shape, dtype):
        return nc.dram_tensor(name, shape, dtype, kind="ExternalOutput" if name in dbg else "Internal")

    x_d = din("x", [S, D], F32).ap()
    ln_in_g = din("ln_in_g", [D], F32)
    ln_in_b = din("ln_in_b", [D], F32)
    w_in = din("w_in", [DEPTH, D, 2048], F32).ap()
    lam_p = din("lam_params", [DEPTH * 4 * 32], F32)
    subln = din("subln_g", [DEPTH, 64], F32).ap()
    w_f = din("w_f", [DEPTH, 4, 128, 128], F32).ap()
    b_f = din("b_f", [DEPTH, 512], F32).ap()
    w_o = din("w_o", [DEPTH, D, D], F32).ap()
    ln1_g = din("ln1_g", [DEPTH * D], F32)
    ln1_b = din("ln1_b", [DEPTH * D], F32)
    w_gu = din("w_gu", [DEPTH, D, 2 * DFF], F32).ap()
    w_dn = din("w_down", [DEPTH, DFF, D], F32).ap()
    ln2_g = din("ln2_g", [DEPTH * D], F32)
    ln2_b = din("ln2_b", [DEPTH * D], F32)
    idf_d = din("idf", [128, 128], F32).ap()
    idb_d = din("idb", [128, 128], BF16).ap()
    ccf_d = din("ccf", [128, 128], F32).ap()
    scn_d = din("scn", [128, 128], F32).ap()
    cst_d = din("cst", [S, 4, 2, 512], BF16).ap()
    qx_d = din("qx", [NH, 32, S], BF16).ap()
    kx_d = {"lo": din("kxlo", [NH, 32, S], BF16).ap(), "up": din("kxup", [NH, 32, S], BF16).ap(),
            "dg": din("kxdg", [NH, 32, S], BF16).ap()}
    dg_d = din("dg", [128, NH, 128], BF16).ap()
    out_d = nc.dram_tensor("out", [S, D], F32, kind="ExternalOutput").ap()

    hres = dscr("hres", [S, D], F32).ap()
    hTd = dscr("hTd", [1024, S], BF16).ap()
    qkT = dscr("qkT", [1024, S], BF16).ap()
    catT = dscr("catT", [1024, S], BF16).ap()

    with contextlib.ExitStack() as es:
        E = es.enter_context

        def sb(name, shape, dtype):
            return E(nc.sbuf_tensor("sb_" + name, shape, dtype))

        RA = sb("RA", [128, 8, S], BF16)
        RB = sb("RB", [128, NF, 1024], BF16)
        PS = [E(nc.psum_tensor("ps%d" % i, [128, 1024], F32)) for i in range(4)]
        idf = sb("idf", [128, 128], F32)
        idb = sb("idb", [128, 128], BF16)
        ccf = sb("ccf", [128, 128], F32)
        scn = sb("scn", [128, 128], F32)
        dgt = sb("dgt", [128, NH, 128], BF16)
        onesf = sb("onesf", [128, 72], F32)
        onesm = sb("onesm", [128, 72], F32)
        onesmB = sb("onesmB", [128, 72], BF16)
        gbt = sb("gbt", [128, 2, 1024], F32)
        lamt = sb("lamt", [128, 256], F32)
        lamw = sb("lamw", [128, 16], F32)
        neglam = sb("neglam", [128, DEPTH], F32)
        gcol = sb("gcol", [128, DEPTH], F32)
        bfc = sb("bfc", [128, DEPTH * 4], F32)
        wbuf = [sb("wbuf%d" % i, [128, 8, 512], BF16) for i in range(2)]
        catb = [sb("catb%d" % i, [128, 8, 512], BF16) for i in range(2)]
        hTc = [sb("hTc%d" % i, [128, 8, 512], BF16) for i in range(2)]
        hTs = sb("hTs", [128, 8, 512], BF16)
        stg = [sb("stg%d" % i, [128, 512], BF16) for i in range(3)]
        lnx = [sb("lnx%d" % i, [128, 1024], F32) for i in range(2)]
        lnr = [sb("lnr%d" % i, [128, 1024], F32) for i in range(2)]
        lnst = [sb("lnst%d" % i, [128, 12], F32) for i in range(2)]
        lnmv = [sb("lnmv%d" % i, [128, 8], F32) for i in range(2)]
        ptb = [sb("ptb%d" % i, [128, 1024], BF16) for i in range(3)]
        sgs = [sb("sgs%d" % i, [128, 512], F32) for i in range(2)]

        ocs, K_OCS = lnr[0], ("lnr", 0)
        rcs, K_RCS = lnr[1], ("lnr", 1)
        Rt = lnr[1][:, 1016:1024]
        Rt_b = bass.AP(lnr[1], 1016, [[1024, 128], [1, 8], [0, 65]])
        Rb65 = lnr[1][:, 0:520].rearrange("p (c m) -> p c m", c=8)
        t0s, t1s, K_T = lnx[0][:, 0:512], lnx[0][:, 512:1024], ("lnx", 0)
        sqs, rss, K_SQ = lnx[1][:, 0:512], lnx[1][:, 512:1024], ("lnx", 1)
        sqsB = lnx[1][:, 0:256].bitcast(BF16)
        wu32, K_WU = lnr[0][:, :].rearrange("p (k c) -> p k c", k=8), ("lnr", 0)
        wuT, K_WT = lnr[1], ("lnr", 1)
        m12, wf32, K_M = lnx[0][:, 0:256], lnx[0][:, 256:384], ("lnx", 0)

        RAf = RA[:, :, :].rearrange("p a b -> p (a b)")
        AB = RAf.rearrange("p (t c) -> p t c", t=NT)
        ACTT = RAf[:, 0:NF * 1024].rearrange("p (f t) -> p f t", f=NF)
        VAf = RB[:, :, :].rearrange("p a b -> p (a b)")
        VA = RB[:, :, :].rearrange("p a b -> p (a b)")[:, 0:NT * NH * 65].rearrange(
            "p (t h c) -> p t h c", t=NT, h=NH)

        p = Prog(nc)
        A = p.add
        cnt = {"ev": 0, "ps": 0, "stg": 0, "trbase": 2}

        def rak(ts):
            return [("RA", t) for t in ts]

        ALLRA = rak(range(8))

        for t, d_, nm in ((idf, idf_d, "idf"), (idb, idb_d, "idb"), (ccf, ccf_d, "ccf"), (scn, scn_d, "scn")):
            A("sp", lambda e, t=t, d_=d_: e.dma_start(out=t[:], in_=d_), w=[nm], dma=("const", nm))
        A("sp", lambda e: e.dma_start(out=dgt[:, :, :], in_=dg_d), w=["dgt"], dma=("const", "dgt"))
        A("sp", lambda e: e.dma_start(out=lamt[:], in_=bass.AP(lam_p, 0, [[0, 128], [1, 256]])), w=["lamt"], dma=("const", "lamt"))
        A("dve", lambda e: e.memset(gcol[:], 0.0), w=["gcol"])
        for l in range(DEPTH):
            A("sp", lambda e, l=l: e.dma_start(out=gcol[1:65, l:l + 1], in_=subln[l:l + 1, :].rearrange("a d -> d a")),
              w=["gcol"], dma=("const", "gcol", l))
            for g in range(4):
                A("sp", lambda e, l=l, g=g: e.dma_start(out=bfc[:, l * 4 + g:l * 4 + g + 1],
                                                        in_=b_f[l, g * 128:(g + 1) * 128].rearrange("(p a) -> p a", a=1)),
                  w=["bfc"], dma=("const", "bfc", l, g))
        A("dve", lambda e: e.memset(onesf[:], 1.0), w=["onesf"])
        A("dve", lambda e: e.memset(onesm[:], 1.0 / 64.0), w=["onesm"])
        A("dve", lambda e: e.memset(onesm[0:1, :], 0.0), w=["onesm"])
        A("dve", lambda e: e.tensor_copy(out=onesmB[:], in_=onesm[:]), r=["onesm"], w=["onesmB"])
        A("dve", lambda e: e.memset(RB[:, :, :], 1.0), w=["RB"])
        for l in range(DEPTH):
            b0 = l * 128
            A("dve", lambda e, b0=b0: e.tensor_tensor(out=lamt[:, b0:b0 + 32], in0=lamt[:, b0:b0 + 32],
                                                     in1=lamt[:, b0 + 32:b0 + 64], op=ALU.mult), r=["lamt"], w=["lamt"])
            A("dve", lambda e, b0=b0: e.tensor_tensor(out=lamt[:, b0 + 64:b0 + 96], in0=lamt[:, b0 + 64:b0 + 96],
                                                     in1=lamt[:, b0 + 96:b0 + 128], op=ALU.mult), r=["lamt"], w=["lamt"])
            A("dve", lambda e, b0=b0, l=l: e.tensor_reduce(out=lamw[:, 4 * l:4 * l + 1], in_=lamt[:, b0:b0 + 32],
                                                          axis=mybir.AxisListType.X, op=ALU.add), r=["lamt"], w=["lamw"])
            A("dve", lambda e, b0=b0, l=l: e.tensor_reduce(out=lamw[:, 4 * l + 1:4 * l + 2], in_=lamt[:, b0 + 64:b0 + 96],
                                                          axis=mybir.AxisListType.X, op=ALU.add), r=["lamt", "lamw"], w=["lamw"])
            A("act", lambda e, l=l: e.activation(out=lamw[:, 4 * l + 2:4 * l + 4], in_=lamw[:, 4 * l:4 * l + 2], func=AF.Exp),
              r=["lamw"], w=["lamw"])
            A("dve", lambda e, l=l: e.scalar_tensor_tensor(out=neglam[:, l:l + 1], in0=lamw[:, 4 * l + 3:4 * l + 4],
                                                          scalar=-lam_init_fn(l), in1=lamw[:, 4 * l + 2:4 * l + 3],
                                                          op0=ALU.add, op1=ALU.subtract), r=["lamw"], w=["neglam"])
            A("dve", lambda e, l=l: e.tensor_scalar(out=gcol[0:65, l:l + 1], in0=gcol[0:65, l:l + 1],
                                                   scalar1=1.0 - lam_init_fn(l), scalar2=None, op0=ALU.mult),
              r=["gcol"], w=["gcol"])

        def load_gb(g_h, b_h, off):
            A("sp", lambda e: e.dma_start(out=gbt[:, 0, :], in_=bass.AP(g_h, off, [[0, 128], [1, 1024]])), w=["gbt"], dma="gbt")
            A("sp", lambda e: e.dma_start(out=gbt[:, 1, :], in_=bass.AP(b_h, off, [[0, 128], [1, 1024]])), w=["gbt"], dma="gbt")

        def evac(out_ap, in_ap, r, w, scale=None, bias=None, force=None):
            cnt["ev"] += 1
            if bias is not None:
                r = list(r) + ["bfc"]
            eng = force or ("act" if cnt["ev"] % 2 else "dve")
            if eng == "act":
                if bias is not None:
                    A("act", lambda e: e.activation(out=out_ap, in_=in_ap, func=AF.Identity, bias=bias), r=r, w=w)
                elif scale is not None:
                    A("act", lambda e: e.activation(out=out_ap, in_=in_ap, func=AF.Copy, scale=scale), r=r, w=w)
                else:
                    A("act", lambda e: e.activation(out=out_ap, in_=in_ap, func=AF.Copy), r=r, w=w)
            else:
                if bias is not None:
                    A("dve", lambda e: e.tensor_scalar(out=out_ap, in0=in_ap, scalar1=bias, scalar2=None, op0=ALU.add), r=r, w=w)
                elif scale is not None:
                    A("dve", lambda e: e.tensor_scalar(out=out_ap, in0=in_ap, scalar1=scale, scalar2=None, op0=ALU.mult), r=r, w=w)
                else:
                    A("dve", lambda e: e.tensor_copy(out=out_ap, in_=in_ap), r=r, w=w)

        def psnext():
            i = cnt["ps"] % 8
            cnt["ps"] += 1
            return i // 2, i % 2

        def psap(pi, b, rows=128):
            return PS[pi][0:rows, b * 512:(b + 1) * 512]

        def ln_stats(src, src_key, wk):
            st_, mv_ = lnst[wk], lnmv[wk]
            A("dve", lambda e: (e.bn_stats(out=st_[:, 0:6], in_=src[:, 0:512]),
                                e.bn_stats(out=st_[:, 6:12], in_=src[:, 512:1024]))[-1], r=[src_key], w=[("lnst", wk)])
            A("dve", lambda e: e.bn_aggr(out=mv_[:, 0:2], in_=st_[:, 0:12]), r=[("lnst", wk)], w=[("lnmv", wk)])
            A("act", lambda e: e.activation(out=mv_[:, 2:3], in_=mv_[:, 1:2], func=AF.Ln, bias=LN_EPS, scale=1.0),
              r=[("lnmv", wk)], w=[("lnmv2", wk)])
            A("act", lambda e: e.activation(out=mv_[:, 3:4], in_=mv_[:, 2:3], func=AF.Exp, scale=-0.5),
              r=[("lnmv2", wk)], w=[("lnmv3", wk)])
            A("act", lambda e: e.mul(out=mv_[:, 4:5], in_=mv_[:, 0:1], mul=-1.0), r=[("lnmv", wk)], w=[("lnmv4", wk)])
            A("act", lambda e: e.activation(out=mv_[:, 5:6], in_=mv_[:, 4:5], func=AF.Copy, scale=mv_[:, 3:4]),
              r=[("lnmv3", wk), ("lnmv4", wk)], w=[("lnmv5", wk)])

        def ln_apply_a(src, src_key, wk, tt, to_hres, to_out, hook=None):
            rt = lnr[wk]
            rk = ("lnr", wk)
            mv_ = lnmv[wk]
            A("act", lambda e: e.activation(out=rt[:], in_=src[:], func=AF.Identity, bias=mv_[:, 5:6], scale=mv_[:, 3:4]),
              r=[src_key, ("lnmv3", wk), ("lnmv5", wk)], w=[rk])
            lnp.emit_stores()
            if hook is not None:
                hook()
            A("pool", lambda e: e.tensor_tensor(out=rt[:], in0=rt[:], in1=gbt[:, 0, :], op=ALU.mult), r=[rk, "gbt"], w=[rk])
            A("pool", lambda e: e.tensor_tensor(out=rt[:], in0=rt[:], in1=gbt[:, 1, :], op=ALU.add), r=[rk, "gbt"], w=[rk])
            rows = slice(tt * 128, (tt + 1) * 128)
            if to_hres:
                lnp.stores.append(lambda: A("act", lambda e: e.dma_start(out=hres[rows, :], in_=rt[:]), r=[rk],
                                            w=[("hres", tt)], dma=("hres_w", wk)))
            if to_out:
                lnp.stores.append(lambda: A("act", lambda e: e.dma_start(out=out_d[rows, :], in_=rt[:]), r=[rk],
                                            w=[("out", tt)], dma=("out_w", wk)))

        def ln_apply_b(wk, tt):
            rt = lnr[wk]
            rk = ("lnr", wk)
            pi = cnt["trbase"] + (cnt["ps"] % 2)
            cnt["ps"] += 1
            pst = PS[pi]

            def tr(e):
                for k in range(8):
                    i = e.transpose(out=pst[:, k * 128:(k + 1) * 128], in_=rt[:, k * 128:(k + 1) * 128], identity=idf[:])
                return i
            A("pe", tr, r=[rk, "idf"], w=[("ps", pi, 0), ("ps", pi, 1)])
            t4 = tt % 4
            evac(hTs[:, :, t4 * 128:(t4 + 1) * 128], pst[:, :].rearrange("p (a b) -> p a b", a=8),
                 [("ps", pi, 0), ("ps", pi, 1)], ["hTs"], force="dve")
            if t4 == 3:
                c = tt // 4
                lnp.stores.append(lambda: A("act", lambda e: e.dma_start(
                    out=hTd[:, c * 512:(c + 1) * 512].rearrange("(kc p) t -> p kc t", p=128), in_=hTs[:, :, :]),
                    r=["hTs"], w=[("hTd", c)], dma="hTs_w"))

        class LNPipe:
            def __init__(self):
                self.q = []
                self.stores = []

            def emit_stores(self):
                while self.stores:
                    self.stores.pop(0)()

            def pre(self):
                if len(self.q) >= 2:
                    ln_apply_b(*self.q.pop(0))

            def push(self, src, src_key, wk, tt, to_hres, to_out, to_hT, hook=None):
                ln_stats(src, src_key, wk % 2)
                ln_apply_a(src, src_key, wk, tt, to_hres, to_out, hook=hook)
                if to_hT:
                    self.q.append((wk, tt))

            def flush(self):
                while self.q:
                    ln_apply_b(*self.q.pop(0))
                self.emit_stores()

        lnp = LNPipe()

        def hk(i):
            return [("hTc", i, a) for a in range(4)]

        def load_hTc(i, c):
            A("sp", lambda e: e.dma_start(out=hTc[i][:, :, :],
                                          in_=hTd[:, c * 512:(c + 1) * 512].rearrange("(kc p) t -> p kc t", p=128)),
              r=[("hTd", c)], w=hk(i), dma=("hTc", i))

        load_gb(ln_in_g, ln_in_b, 0)
        def load_x(tt):
            if tt < NT:
                A("sp", lambda e: e.dma_start(out=lnx[tt % 2][:], in_=x_d[tt * 128:(tt + 1) * 128, :]),
                  w=[("lnx", tt % 2)], dma=("lnx", tt % 2))

        def load_h(tt, lim=NT):
            if tt < lim:
                A("sp", lambda e: e.dma_start(out=lnx[tt % 2][:], in_=hres[tt * 128:(tt + 1) * 128, :]),
                  r=[("hres", tt)], w=[("lnx", tt % 2)], dma=("lnx", tt % 2))

        load_x(0)
        load_x(1)
        for tt in range(NT):
            xi = lnx[tt % 2]
            lnp.pre()
            lnp.push(xi, ("lnx", tt % 2), tt % 2, tt, True, False, True, hook=(lambda tt=tt: load_x(tt + 2)))
        lnp.flush()

        def do_layer(l):
            if stop_after == ("ln0",):
                return True
            w_in_v = w_in[l].rearrange("(kc p) n -> p kc n", p=128)
            WAB = catb
            for g in range(4):
                A("sp", lambda e, g=g: e.dma_start(out=wu32, in_=w_in_v[:, :, 1536 + g * 128:1536 + (g + 1) * 128]),
                  w=[K_WU], dma="wu32")
                A("sp", lambda e, g=g: e.dma_start(out=wf32, in_=w_f[l, g]), w=[K_M], dma="wf32")
                pi, b = psnext()
                pi2, b2 = psnext()

                def trw(e, pi=pi, b=b, pi2=pi2, b2=b2):
                    for k in range(8):
                        tgt = psap(pi, b) if k < 4 else psap(pi2, b2)
                        i = e.transpose(out=tgt[:, (k % 4) * 128:(k % 4 + 1) * 128], in_=wu32[:, k, :], identity=idf[:])
                    return i
                A("pe", trw, r=[K_WU, "idf"], w=[("ps", pi, b), ("ps", pi2, b2)])
                evac(wuT[:, 0:512], psap(pi, b), [("ps", pi, b)], [K_WT])
                evac(wuT[:, 512:1024], psap(pi2, b2), [("ps", pi2, b2)], [K_WT])
                pi, b = psnext()

                def mm12(e, pi=pi, b=b):
                    e.matmul(psap(pi, b)[:, 0:128], lhsT=ccf[:], rhs=wf32, start=True, stop=True)
                    return e.matmul(psap(pi, b)[:, 128:256], lhsT=scn[:], rhs=wf32, start=True, stop=True)
                A("pe", mm12, r=["ccf", "scn", K_M], w=[("ps", pi, b)])
                evac(m12, psap(pi, b)[:, 0:256], [("ps", pi, b)], [K_M])
                for q4 in range(4):
                    pi, b = psnext()

                    def mmab3(e, pi=pi, b=b, q4=q4):
                        i = None
                        for k2 in range(2):
                            kc = q4 * 2 + k2
                            i = e.matmul(psap(pi, b)[:, k2 * 256:(k2 + 1) * 256], lhsT=wuT[:, kc * 128:(kc + 1) * 128],
                                         rhs=m12, start=True, stop=True)
                        return i
                    A("pe", mmab3, r=[K_WT, K_M], w=[("ps", pi, b)])
                    hf = g // 2
                    c0 = (g % 2) * 256
                    evac(WAB[hf][:, q4 * 2:q4 * 2 + 2, c0:c0 + 256],
                         psap(pi, b).rearrange("p (k c) -> p k c", k=2), [("ps", pi, b)], [("catb", hf)])

            for blk in range(2):
                A("pool", lambda e, blk=blk: e.dma_start(out=wbuf[blk][:, :, :], in_=w_in_v[:, :, blk * 512:(blk + 1) * 512]),
                  w=[("wbuf", blk, 0), ("wbuf", blk, 1)], dma=("wbuf", blk))
            for tc in range(8):
                load_hTc(tc % 2, tc)
                hc = hTc[tc % 2]
                for blk in range(2):
                    for m in range(4):
                        pi, b = psnext()

                        def mmqk(e, hc=hc, blk=blk, m=m, pi=pi, b=b):
                            for kc in range(8):
                                i = e.matmul(psap(pi, b), lhsT=wbuf[blk][:, kc, m * 128:(m + 1) * 128],
                                             rhs=hc[:, kc, :], start=(kc == 0), stop=(kc == 7))
                            return i
                        A("pe", mmqk, r=[("wbuf", blk, 0), ("wbuf", blk, 1)] + hk(tc % 2), w=[("ps", pi, b)])
                        si = cnt["stg"] % 3
                        cnt["stg"] += 1
                        evac(stg[si][:, :], psap(pi, b), [("ps", pi, b)], [("stg", si)],
                             scale=(SCALE if blk == 0 else None))
                        r0 = blk * 512 + m * 128
                        A("sp", lambda e, si=si, r0=r0, tc=tc: e.dma_start(out=qkT[r0:r0 + 128, tc * 512:(tc + 1) * 512],
                                                                         in_=stg[si][:, :]),
                          r=[("stg", si)], w=[("qkT", r0 // 128)], dma=("stg_w", si))
            A("pool", lambda e: e.dma_start(out=wbuf[0][:, :, :], in_=w_in_v[:, :, 1024:1536]),
              w=[("wbuf", 0, 0), ("wbuf", 0, 1)], dma=("wbuf", 0))
            for tc in range(8):
                load_hTc(tc % 2, tc)
                hc = hTc[tc % 2]
                for t4 in range(4):
                    tt = tc * 4 + t4
                    pi, b = psnext()

                    def mmv(e, hc=hc, t4=t4, pi=pi, b=b):
                        for kc in range(8):
                            i = e.matmul(psap(pi, b), lhsT=hc[:, kc, t4 * 128:(t4 + 1) * 128], rhs=wbuf[0][:, kc, :],
                                         start=(kc == 0), stop=(kc == 7))
                        return i
                    A("pe", mmv, r=[("wbuf", 0, 0), ("wbuf", 0, 1)] + hk(tc % 2), w=[("ps", pi, b)])
                    evac(VA[:, tt, :, 1:65], psap(pi, b).rearrange("p (h c) -> p h c", h=NH), [("ps", pi, b)], ["RB"])
                    for hf in range(2):
                        pi, b = psnext()

                        def mmab4(e, hc=hc, t4=t4, hf=hf, pi=pi, b=b):
                            for kc in range(8):
                                i = e.matmul(psap(pi, b), lhsT=hc[:, kc, t4 * 128:(t4 + 1) * 128], rhs=WAB[hf][:, kc, :],
                                             start=(kc == 0), stop=(kc == 7))
                            return i
                        A("pe", mmab4, r=[("catb", hf)] + hk(tc % 2), w=[("ps", pi, b)])
                        evac(AB[:, tt, hf * 512:(hf + 1) * 512], psap(pi, b), [("ps", pi, b)], [("RA", tt // 4)])
            for i_ in range(16):
                lo_, hi_ = AB[:, i_, :], AB[:, i_ + 16, :]
                A("dve", lambda e, lo_=lo_, hi_=hi_: e.tensor_tensor(out=hi_, in0=lo_, in1=hi_, op=ALU.subtract),
                  r=[("RA", i_ // 4), ("RA", (i_ + 16) // 4)], w=[("RA", (i_ + 16) // 4)])
                A("dve", lambda e, lo_=lo_, hi_=hi_: e.scalar_tensor_tensor(out=lo_, in0=lo_, scalar=2.0, in1=hi_,
                                                                           op0=ALU.mult, op1=ALU.subtract),
                  r=[("RA", i_ // 4), ("RA", (i_ + 16) // 4)], w=[("RA", i_ // 4)])
            if stop_after == ("A", l):
                return True

            ti = 0
            for tc in range(4):
                accE = [psnext() for _ in range(4)]
                accO = [psnext() for _ in range(4)]
                for st_ in range(NT):
                    sl_ = ti % 8
                    ti += 1
                    hb_, a_ = hTc[sl_ // 4], sl_ % 4
                    tk = ("hTc", sl_ // 4, a_)
                    tcs, tss = hb_[:, 2 * a_, :], hb_[:, 2 * a_ + 1, :]
                    A("sp", lambda e, hb_=hb_, a_=a_, st_=st_, tc=tc: e.dma_start(
                        out=hb_[:, 2 * a_:2 * a_ + 2, :], in_=cst_d[st_ * 128:(st_ + 1) * 128, tc, :, :]), w=[tk], dma=("tab", sl_))
                    accs = accE if st_ < 16 else accO

                    def mmf(e, tcs=tcs, tss=tss, st_=st_, accs=accs):
                        for g in range(4):
                            pi, b = accs[g]
                            e.matmul(psap(pi, b), lhsT=AB[:, st_, g * 256:g * 256 + 128], rhs=tcs,
                                     start=(st_ % 16 == 0), stop=False)
                            i = e.matmul(psap(pi, b), lhsT=AB[:, st_, g * 256 + 128:g * 256 + 256], rhs=tss,
                                         start=False, stop=(st_ % 16 == 15))
                        return i
                    A("pe", mmf, r=[tk, ("RA", st_ // 4)], w=[("ps",) + a_ for a_ in accs])
                for g in range(4):
                    pb_ = ptb[g % 3]
                    pkey = ("ptb", g % 3)
                    for par, acc in ((0, accE[g]), (1, accO[g])):
                        pi, b = acc
                        evac(pb_[:, par:1024:2], psap(pi, b), [("ps", pi, b)], [pkey], bias=bfc[:, l * 4 + g:l * 4 + g + 1])
                    r0 = 512 + g * 128
                    A("sp", lambda e, pb_=pb_, r0=r0, tc=tc: e.dma_start(out=catT[r0:r0 + 128, tc * 1024:(tc + 1) * 1024],
                                                                       in_=pb_[:, :]),
                      r=[pkey], w=[("catT", r0 // 128, 2 * tc), ("catT", r0 // 128, 2 * tc + 1)], dma=("ptb_w", g % 3))
            if stop_after == ("B2", l):
                return True

            deferred = []

            def run_deferred(n=1):
                for _ in range(n):
                    if deferred:
                        deferred.pop(0)()

            def _dmin(qc, j):
                if j * 128 + 127 < qc * 512:
                    return qc * 512 - (j * 128 + 127)
                if j * 128 > qc * 512 + 511:
                    return j * 128 - (qc * 512 + 511)
                return 0
            blocks = []
            gidx = 0
            for h in range(NH):
                for qc in range(8):
                    js = [j for j in range(NT) if (2.0 ** -(h + 1)) * _dmin(qc, j) <= SKIP_BIAS]
                    for j in js:
                        blocks.append((h, qc, j, gidx, j == js[0], j == js[-1]))
                    gidx += 1

            def aug(hh, t):
                return RA[:, (hh % 2) * 4 + t, :]

            def load_head(h):
                s_ = h % 2
                for c in range(2):
                    A("sp", lambda e, c=c: e.dma_start(out=aug(h, 0)[c * 64:c * 64 + 32, :],
                                                       in_=qkT[h * 64 + c * 32:h * 64 + c * 32 + 32, :]),
                      r=[("qkT", (h * 64) // 128)], w=rak([s_ * 4 + 0]), dma=("aug", s_, 0))
                    A("sp", lambda e, c=c: e.dma_start(out=aug(h, 0)[c * 64 + 32:c * 64 + 64, :], in_=qx_d[h]),
                      w=rak([s_ * 4 + 0]), dma=("aug", s_, 0))
                    for t, nm in ((1, "lo"), (2, "up"), (3, "dg")):
                        A("sp", lambda e, c=c, t=t: e.dma_start(out=aug(h, t)[c * 64:c * 64 + 32, :],
                                                                in_=qkT[512 + h * 64 + c * 32:512 + h * 64 + c * 32 + 32, :]),
                          r=[("qkT", (512 + h * 64) // 128)], w=rak([s_ * 4 + t]), dma=("aug", s_, t))
                        A("sp", lambda e, c=c, t=t, nm=nm: e.dma_start(out=aug(h, t)[c * 64 + 32:c * 64 + 64, :],
                                                                      in_=kx_d[nm][h]),
                          w=rak([s_ * 4 + t]), dma=("aug", s_, t))

            def emit_S(bi):
                h, qc, j = blocks[bi][0:3]
                si = bi % 2
                s_ = h % 2
                q0 = qc * 512
                Q = aug(h, 0)
                rel = j - 4 * qc

                def mms(e):
                    i = None
                    for c in range(2):
                        pr = slice(c * 64, c * 64 + 64)
                        ob = PS[si][:, c * 512:(c + 1) * 512]
                        kc_ = slice(j * 128, (j + 1) * 128)
                        if rel < 0 or rel > 3:
                            K = aug(h, 1 if rel < 0 else 2)
                            i = e.matmul(ob, lhsT=K[pr, kc_], rhs=Q[pr, q0:q0 + 512], start=True, stop=True)
                        else:
                            lo_c = 128 * rel
                            if lo_c > 0:
                                e.matmul(ob[:, 0:lo_c], lhsT=aug(h, 2)[pr, kc_], rhs=Q[pr, q0:q0 + lo_c],
                                         start=True, stop=True, skip_group_check=True)
                            e.matmul(ob[:, lo_c:lo_c + 128], lhsT=aug(h, 3)[pr, kc_], rhs=Q[pr, q0 + lo_c:q0 + lo_c + 128],
                                     start=True, stop=False, skip_group_check=True)
                            i = e.matmul(ob[:, lo_c:lo_c + 128], lhsT=idb[:, :], rhs=dgt[:, h, :], start=False, stop=True,
                                         skip_group_check=True)
                            if lo_c + 128 < 512:
                                i = e.matmul(ob[:, lo_c + 128:512], lhsT=aug(h, 1)[pr, kc_],
                                             rhs=Q[pr, q0 + lo_c + 128:q0 + 512], start=True, stop=True, skip_group_check=True)
                    return i
                A("pe", mms, r=rak([s_ * 4 + t for t in range(4)]) + ["idb", "dgt"], w=[("ps", si, 0), ("ps", si, 1)])

            def emit_EXP(bi):
                si = bi % 2
                pt = ptb[bi % 3]
                A("act", lambda e: e.activation(out=pt[:, :], in_=PS[si][:, :], func=AF.Exp),
                  r=[("ps", si, 0), ("ps", si, 1)], w=[("ptb", bi % 3)])

            def emit_PV(bi):
                h, qc, j, g_, first, last_ = blocks[bi]
                gi = g_ % 2
                pt = ptb[bi % 3]

                def mmpv(e):
                    for c in range(2):
                        i = e.matmul(PS[2 + gi][0:65, c * 512:(c + 1) * 512], lhsT=VA[:, j, h, :],
                                     rhs=pt[:, c * 512:(c + 1) * 512], start=first, stop=last_)
                    return i
                A("pe", mmpv, r=[("ptb", bi % 3), "RB"], w=[("ps", 2 + gi, 0), ("ps", 2 + gi, 1)])

            def post(h, qc, gi):
                pv = PS[2 + gi]
                pk = [("ps", 2 + gi, 0), ("ps", 2 + gi, 1)]

                def s1():
                    A("dve", lambda e: e.tensor_copy(out=ocs[0:65, :], in_=pv[0:65, :]), r=pk, w=[K_OCS])

                def s1b():
                    def trs(e):
                        for c8 in range(8):
                            i = e.transpose(out=pv[:, c8:c8 + 1], in_=ocs[0:1, c8 * 128:(c8 + 1) * 128], identity=idf[0:1, 0:1])
                        return i
                    A("pe", trs, r=[K_OCS, "idf"], w=[pk[0]])
                    A("dve", lambda e: e.reciprocal(out=Rt, in_=pv[:, 0:8]), r=[pk[0]], w=[K_RCS])
                    A("dve", lambda e: e.tensor_scalar(out=Rt[:, 4:8], in0=Rt[:, 4:8], scalar1=neglam[:, l:l + 1],
                                                       scalar2=None, op0=ALU.mult), r=[K_RCS, "neglam"], w=[K_RCS])
                    A("dve", lambda e: e.tensor_copy(out=Rb65, in_=Rt_b), r=[K_RCS], w=[K_RCS])

                def s2():
                    def mmb(e):
                        for c8 in range(8):
                            i = e.matmul(pv[0:65, c8 * 128:(c8 + 1) * 128], lhsT=Rb65[:, c8, :], rhs=idf[:, :],
                                         start=True, stop=True, skip_group_check=True)
                        return i
                    A("pe", mmb, r=[K_RCS, "idf"], w=pk)
                    A("dve", lambda e: e.tensor_tensor(out=t0s[0:65, :], in0=ocs[0:65, 0:512], in1=pv[0:65, 0:512], op=ALU.mult),
                      r=[K_OCS, pk[0]], w=[K_T])
                    A("dve", lambda e: e.tensor_tensor(out=t1s[0:65, :], in0=ocs[0:65, 512:1024], in1=pv[0:65, 512:1024],
                                                       op=ALU.mult), r=[K_OCS, pk[1]], w=[K_T])
                    A("dve", lambda e: e.tensor_tensor(out=t0s[0:65, :], in0=t0s[0:65, :], in1=t1s[0:65, :], op=ALU.add),
                      r=[K_T], w=[K_T])
                    A("dve", lambda e: e.tensor_tensor(out=sqsB[0:65, :], in0=t0s[0:65, :], in1=t0s[0:65, :], op=ALU.mult),
                      r=[K_T], w=[K_SQ])

                def s3():
                    A("pe", lambda e: e.matmul(pv[0:65, 0:512], lhsT=onesmB[0:65, 0:65], rhs=sqsB[0:65, :], start=True, stop=True),
                      r=[K_SQ, "onesmB"], w=[pk[0]])
                    A("act", lambda e: e.activation(out=rss[0:65, :], in_=pv[0:65, 0:512], func=AF.Ln, bias=SUBLN_EPS, scale=1.0),
                      r=[pk[0]], w=[K_SQ])
                    A("act", lambda e: e.activation(out=rss[0:65, :], in_=rss[0:65, :], func=AF.Exp, scale=-0.5),
                      r=[K_SQ], w=[K_SQ])
                    si = cnt["stg"] % 3
                    cnt["stg"] += 1
                    A("dve", lambda e: e.scalar_tensor_tensor(out=stg[si][0:65, :], in0=t0s[0:65, :],
                                                              scalar=gcol[0:65, l:l + 1], in1=rss[0:65, :],
                                                              op0=ALU.mult, op1=ALU.mult),
                      r=[K_T, K_SQ, "gcol"], w=[("stg", si)])
                    A("sp", lambda e: e.dma_start(out=catT[h * 64:(h + 1) * 64, qc * 512:(qc + 1) * 512], in_=stg[si][1:65, :]),
                      r=[("stg", si)], w=[("catT", "a", h, qc)], dma=("stg_w", si))
                return s1, [(2, s1b), (5, s2), (10, s3)]

            nb = len(blocks)
            load_head(0)
            emit_S(0)
            emit_S(1)
            pos = 0
            for bi in range(nb):
                h, qc, j, g_, first, last_ = blocks[bi]
                if first:
                    pos = 0
                if first and qc == 0 and h + 1 < NH:
                    load_head(h + 1)
                emit_EXP(bi)
                if bi + 2 < nb:
                    emit_S(bi + 2)
                emit_PV(bi)
                while deferred and deferred[0][0] <= pos:
                    deferred.pop(0)[1]()
                pos += 1
                if last_:
                    while deferred:
                        deferred.pop(0)[1]()
                    s1_, rest = post(h, qc, g_ % 2)
                    s1_()
                    deferred.extend(rest)
            while deferred:
                deferred.pop(0)[1]()
            if stop_after == ("B", l):
                return True

            w_o_v = w_o[l].rearrange("(kc p) n -> p kc n", p=128)
            for i in range(2):
                A("pool", lambda e, i=i: e.dma_start(out=wbuf[i][:, :, :], in_=w_o_v[:, :, i * 512:(i + 1) * 512]),
                  w=[("wbuf", i, 0), ("wbuf", i, 1)], dma=("wbuf", i))
            load_gb(ln1_g, ln1_b, l * D)
            cat_keys = [("catT", rr, tc_) for rr in range(4, 8) for tc_ in range(8)] + \
                       [("catT", "a", hh, qq) for hh in range(NH) for qq in range(8)]
            def load_cat(k):
                if k < 8:
                    A("sp", lambda e: e.dma_start(
                        out=catb[k % 2][:, :, :], in_=catT[:, k * 512:(k + 1) * 512].rearrange("(kc p) t -> p kc t", p=128)),
                      r=cat_keys, w=[("catb", k % 2)], dma=("catb", k % 2))

            load_cat(0)
            for tt in range(NT):
                if tt % 4 == 0:
                    cb = catb[(tt // 4) % 2]
                    ck = ("catb", (tt // 4) % 2)
                    load_cat(tt // 4 + 1)
                xi = lnx[tt % 2]
                if tt == 0:
                    load_h(0)
                    load_h(1)
                lnp.pre()
                pi = cnt["ps"] % 2
                cnt["ps"] += 1

                def mmo(e, cb=cb, tt=tt, pi=pi):
                    for cc in range(2):
                        for kc in range(8):
                            i = e.matmul(PS[pi][:, cc * 512:(cc + 1) * 512], lhsT=cb[:, kc, (tt % 4) * 128:(tt % 4 + 1) * 128],
                                         rhs=wbuf[cc][:, kc, :], start=(kc == 0), stop=(kc == 7))
                    return i
                A("pe", mmo, r=[ck] + [("wbuf", i, j_) for i in range(2) for j_ in range(2)], w=[("ps", pi, 0), ("ps", pi, 1)])
                A("dve", lambda e, xi=xi, pi=pi: e.scalar_tensor_tensor(out=xi[:], in0=xi[:], scalar=ALPHA, in1=PS[pi][:, :],
                                                                       op0=ALU.mult, op1=ALU.add),
                  r=[("lnx", tt % 2), ("ps", pi, 0), ("ps", pi, 1)], w=[("lnx", tt % 2)])
                lnp.push(xi, ("lnx", tt % 2), tt % 2, tt, True, False, True, hook=(lambda tt=tt: load_h(tt + 2)))
            lnp.flush()
            if stop_after == ("C", l):
                return True

            A("pool", lambda e: e.dma_start(out=RB[:, :, :], in_=w_dn[l].rearrange("(f p) n -> p f n", p=128)),
              w=["RB"], dma="RB")
            load_gb(ln2_g, ln2_b, l * D)
            cnt["trbase"] = 0
            w_gu_v = w_gu[l].rearrange("(kc p) n -> p kc n", p=128)
            last = (l == DEPTH - 1)
            gi_ = 0
            for tb in range(4):
                load_hTc(0, 2 * tb)
                load_hTc(1, 2 * tb + 1)
                for f in range(NF):
                    ws = gi_ % 4
                    gi_ += 1
                    wt_ = wbuf[ws // 2]
                    o_ = (ws % 2) * 256
                    wk_ = ("wbuf", ws // 2, ws % 2)
                    A("pool", lambda e, wt_=wt_, o_=o_, f=f: e.dma_start(out=wt_[:, :, o_:o_ + 128],
                                                                        in_=w_gu_v[:, :, f * 128:(f + 1) * 128]),
                      w=[wk_], dma=wk_)
                    A("pool", lambda e, wt_=wt_, o_=o_, f=f: e.dma_start(out=wt_[:, :, o_ + 128:o_ + 256],
                                                                        in_=w_gu_v[:, :, DFF + f * 128:DFF + (f + 1) * 128]),
                      w=[wk_], dma=wk_)
                    for tc in range(2):
                        pi = cnt["ps"] % 2
                        cnt["ps"] += 1

                        def mmgu(e, wt_=wt_, o_=o_, tc=tc, pi=pi):
                            for u_ in range(2):
                                for kc in range(8):
                                    i = e.matmul(PS[pi][:, u_ * 512:(u_ + 1) * 512],
                                                 lhsT=wt_[:, kc, o_ + u_ * 128:o_ + (u_ + 1) * 128],
                                                 rhs=hTc[tc][:, kc, :], start=(kc == 0), stop=(kc == 7))
                            return i
                        A("pe", mmgu, r=[wk_] + hk(tc), w=[("ps", pi, 0), ("ps", pi, 1)])
                        sgt = sgs[pi]
                        A("act", lambda e, sgt=sgt, pi=pi: e.activation(out=sgt[:, :], in_=PS[pi][:, 0:512], func=AF.Silu),
                          r=[("ps", pi, 0)], w=[("sgs", pi)])
                        A("dve", lambda e, sgt=sgt, pi=pi, f=f, tc=tc: e.tensor_tensor(
                            out=ACTT[:, f, tc * 512:(tc + 1) * 512], in0=sgt[:, :], in1=PS[pi][:, 512:1024], op=ALU.mult),
                          r=[("sgs", pi), ("ps", pi, 1)], w=[("RA", f // 4)])
                for t8 in range(8):
                    tt = tb * 8 + t8
                    xi = lnx[tt % 2]
                    if t8 == 0:
                        load_h(tt)
                        load_h(tt + 1)
                    lnp.pre()
                    pi = 2 + (cnt["ps"] % 2)
                    cnt["ps"] += 1

                    def mmd(e, t8=t8, pi=pi):
                        for cc in range(2):
                            for f in range(NF):
                                i = e.matmul(PS[pi][:, cc * 512:(cc + 1) * 512], lhsT=ACTT[:, f, t8 * 128:(t8 + 1) * 128],
                                             rhs=RB[:, f, cc * 512:(cc + 1) * 512], start=(f == 0), stop=(f == NF - 1))
                        return i
                    A("pe", mmd, r=rak(range(6)) + ["RB"], w=[("ps", pi, 0), ("ps", pi, 1)])
                    A("dve", lambda e, xi=xi, pi=pi: e.scalar_tensor_tensor(out=xi[:], in0=xi[:], scalar=ALPHA, in1=PS[pi][:, :],
                                                                           op0=ALU.mult, op1=ALU.add),
                      r=[("lnx", tt % 2), ("ps", pi, 0), ("ps", pi, 1)], w=[("lnx", tt % 2)])
                    lnp.push(xi, ("lnx", tt % 2), tt % 2, tt, not last, last, not last,
                             hook=(lambda tt=tt, tb=tb: load_h(tt + 2, (tb + 1) * 8)))
                lnp.flush()
            cnt["trbase"] = 2
            if not last:
                A("dve", lambda e: e.memset(RB[:, :, :], 1.0), w=["RB"])
            if stop_after == ("D", l):
                return True

            return False

        for l_ in range(DEPTH):
            if do_layer(l_):
                break

        fin_keys = [k for k in p.res.keys() if isinstance(k, tuple) and k[0] in ("out", "hres", "qkT", "catT", "hTd")]
        A("sp", lambda e: None, r=fin_keys)
        p.emit()
    return nc


_NC = {}


def _get_nc():
    if "nc" not in _NC:
        _NC["nc"] = build()
    return _NC["nc"]


def _in_maps(inputs):
    c = _consts()
    f32 = lambda a: np.ascontiguousarray(np.asarray(a, dtype=np.float32))
    shared = {
        "ln_in_g": f32(inputs["ln_in_g"]), "ln_in_b": f32(inputs["ln_in_b"]),
        "w_in": f32(inputs["w_in"]), "lam_params": f32(inputs["lam_params"]).reshape(-1),
        "subln_g": f32(inputs["subln_g"]), "w_f": f32(inputs["w_f"]), "b_f": f32(inputs["b_f"]),
        "w_o": f32(inputs["w_o"]), "ln1_g": f32(inputs["ln1_g"]).reshape(-1), "ln1_b": f32(inputs["ln1_b"]).reshape(-1),
        "w_gu": f32(inputs["w_gu"]), "w_down": f32(inputs["w_down"]),
        "ln2_g": f32(inputs["ln2_g"]).reshape(-1), "ln2_b": f32(inputs["ln2_b"]).reshape(-1),
    }
    shared.update(c)
    x = f32(inputs["x"])
    return [dict(shared, x=x[b]) for b in range(x.shape[0])]


def kernel(**inputs):
    nc = _get_nc()
    maps = _in_maps(inputs)
    res = run_bass_kernel_spmd(nc, maps, core_ids=list(range(8)))
    return np.stack([np.asarray(r["out"], dtype=np.float32) for r in res.results], axis=0)
```

```python
import math
import contextlib
import numpy as np
import ml_dtypes
import concourse.bass as bass
import concourse.mybir as mybir
from concourse.bass_utils import run_bass_kernel_spmd

F32 = mybir.dt.float32
BF16 = mybir.dt.bfloat16
F32R = mybir.dt.float32r
AF = mybir.ActivationFunctionType
ALU = mybir.AluOpType

S = 4096
D = 1024
NT = 32
DEPTH = 2
DFF = 2816
NF = 22
NH = 8
ALPHA = (2.0 * DEPTH) ** 0.25
SCALE = 32 ** -0.5
LN_EPS = 1e-5
SUBLN_EPS = 1e-5
SKIP_BIAS = 48.0
SAME_ENGINE_SYNC = True


def lam_init_fn(l):
    return 0.8 - 0.6 * math.exp(-0.3 * l)


class Op:
    __slots__ = ("eng", "fn", "dma_key", "dma_cnt", "signal", "sig_idx", "waits")

    def __init__(self, eng, fn, dma_key):
        self.eng = eng
        self.fn = fn
        self.dma_key = dma_key
        self.dma_cnt = 0
        self.signal = False
        self.sig_idx = 0
        self.waits = []


class Prog:
    ENGS = ("pe", "act", "dve", "pool", "sp")

    def __init__(self, nc):
        self.nc = nc
        self.streams = {e: [] for e in self.ENGS}
        self.res = {}
        self.dma_counts = {}

    def add(self, eng, fn, r=(), w=(), dma=None):
        op = Op(eng, fn, dma)
        if dma is not None:
            c = self.dma_counts.get(dma, 0) + 1
            self.dma_counts[dma] = c
            op.dma_cnt = c
        deps = []
        for k in r:
            st = self.res.get(k)
            if st is None:
                st = [None, []]
                self.res[k] = st
            if st[0] is not None:
                deps.append(st[0])
            st[1].append(op)
        for k in w:
            st = self.res.get(k)
            if st is not None:
                if st[0] is not None:
                    deps.append(st[0])
                for rd in st[1]:
                    if rd is not op:
                        deps.append(rd)
            self.res[k] = [op, []]
        seen = set()
        for d in deps:
            if id(d) in seen or d is op:
                continue
            seen.add(id(d))
            if d.dma_key is None:
                if d.eng == eng and (eng == "pe" or not SAME_ENGINE_SYNC):
                    continue
                d.signal = True
            op.waits.append(d)
        self.streams[eng].append(op)
        return op

    def emit(self):
        nc = self.nc
        with contextlib.ExitStack() as es:
            esem = {e: es.enter_context(nc.semaphore("s_" + e)) for e in ("pe", "act", "dve", "pool")}
            dsem = {k: es.enter_context(nc.semaphore("d_%d" % i)) for i, k in enumerate(self.dma_counts)}
            for e, st in self.streams.items():
                c = 0
                for op in st:
                    if op.dma_key is None and op.signal:
                        c += 1
                        op.sig_idx = c
            block = es.enter_context(nc.Block())
            handles = {"pe": block.tensor, "act": block.scalar, "dve": block.vector,
                       "pool": block.gpsimd, "sp": block.sync}

            def mk(ename):
                st = self.streams[ename]

                def body(eng):
                    waited = {}
                    for op in st:
                        for d in op.waits:
                            if d.dma_key is not None:
                                key = ("d", d.dma_key)
                                val = d.dma_cnt * 16
                                sem = dsem[d.dma_key]
                            else:
                                key = ("e", d.eng)
                                val = d.sig_idx
                                sem = esem[d.eng]
                            if waited.get(key, 0) >= val:
                                continue
                            waited[key] = val
                            eng.wait_ge(sem, val)
                        inst = op.fn(eng)
                        if inst is None:
                            continue
                        if op.dma_key is not None:
                            inst.then_inc(dsem[op.dma_key], 16)
                        elif op.signal:
                            inst.then_inc(esem[op.eng], 1)

                return body

            for ename in self.ENGS:
                if self.streams[ename]:
                    handles[ename](mk(ename))


_CONSTS = None


def _consts():
    global _CONSTS
    if _CONSTS is not None:
        return _CONSTS
    bf = ml_dtypes.bfloat16
    c = {}
    c["idf"] = np.eye(128, dtype=np.float32)
    c["idb"] = np.eye(128, dtype=np.float32).astype(bf)
    ab = np.outer(np.arange(128), np.arange(128)) % 128
    ang = 2.0 * np.pi * ab / 128.0
    c["ccf"] = (np.cos(ang) / math.sqrt(128.0)).astype(np.float32)
    c["scn"] = (-np.sin(ang) / math.sqrt(128.0)).astype(np.float32)
    k = np.arange(S, dtype=np.int64)
    tab_c = (np.cos(2.0 * np.pi * k / S) / 64.0).astype(np.float32).astype(bf)
    tab_s = (np.sin(2.0 * np.pi * k / S) / 64.0).astype(np.float32).astype(bf)
    half = k[:S // 2]
    st = np.concatenate([np.outer(half, 2 * half), np.outer(half, 2 * half + 1)], axis=0)
    st = (st % S).astype(np.int32)
    c["cst"] = np.ascontiguousarray(np.stack([tab_c[st].reshape(S, 4, 512), tab_s[st].reshape(S, 4, 512)], axis=2))
    pos = np.arange(S)
    pa = (pos // 64).astype(np.float32) * 64.0
    pb = (pos % 64).astype(np.float32)
    qx = np.zeros((NH, 32, S), np.float32)
    klo = np.zeros((NH, 32, S), np.float32)
    for h in range(NH):
        sl = 2.0 ** -(h + 1)
        qx[h, 1] = -sl * pa
        qx[h, 2] = -sl * pb
        qx[h, 3] = 1.0
        qx[h, 4] = 1.0
        klo[h, 1] = 1.0
        klo[h, 2] = 1.0
        klo[h, 3] = sl * pa
        klo[h, 4] = sl * pb
    c["qx"] = qx.astype(bf)
    c["kxlo"] = klo.astype(bf)
    c["kxup"] = (-klo).astype(bf)
    c["kxdg"] = np.zeros((NH, 32, S), np.float32).astype(bf)
    dg = np.zeros((128, NH, 128), np.float32)
    pf = np.abs(np.arange(128)[None, :] - np.arange(128)[:, None]).astype(np.float32)
    for h in range(NH):
        dg[:, h, :] = -(2.0 ** -(h + 1)) * pf
    c["dg"] = dg.astype(bf)
    _CONSTS = c
    return c


def build(dbg=None, stop_after=None):
    dbg = dbg or set()
    nc = bass.Bass("TRN2", target_bir_lowering=False)

    def din(name, shape, dtype):
        return nc.dram_tensor(name, shape, dtype, kind="ExternalInput")

    def dscr(name, shape, dtype):
        return nc.dram_tensor(name, shape, dtype, kind="ExternalOutput" if name in dbg else "Internal")

    x_d = din("x", [S, D], F32).ap()
    ln_in_g = din("ln_in_g", [D], F32)
    ln_in_b = din("ln_in_b", [D], F32)
    w_in = din("w_in", [DEPTH, D, 2048], F32).ap()
    lam_p = din("lam_params", [DEPTH * 4 * 32], F32)
    subln = din("subln_g", [DEPTH, 64], F32).ap()
    w_f = din("w_f", [DEPTH, 4, 128, 128], F32).ap()
    b_f = din("b_f", [DEPTH, 512], F32).ap()
    w_o = din("w_o", [DEPTH, D, D], F32).ap()
    ln1_g = din("ln1_g", [DEPTH * D], F32)
    ln1_b = din("ln1_b", [DEPTH * D], F32)
    w_gu = din("w_gu", [DEPTH, D, 2 * DFF], F32).ap()
    w_dn = din("w_down", [DEPTH, DFF, D], F32).ap()
    ln2_g = din("ln2_g", [DEPTH * D], F32)
    ln2_b = din("ln2_b", [DEPTH * D], F32)
    idf_d = din("idf", [128, 128], F32).ap()
    idb_d = din("idb", [128, 128], BF16).ap()
    ccf_d = din("ccf", [128, 128], F32).ap()
    scn_d = din("scn", [128, 128], F32).ap()
    cst_d = din("cst", [S, 4, 2, 512], BF16).ap()
    qx_d = din("qx", [NH, 32, S], BF16).ap()
    kx_d = {"lo": din("kxlo", [NH, 32, S], BF16).ap(), "up": din("kxup", [NH, 32, S], BF16).ap(),
            "dg": din("kxdg", [NH, 32, S], BF16).ap()}
    dg_d = din("dg", [128, NH, 128], BF16).ap()
    out_d = nc.dram_tensor("out", [S, D], F32, kind="ExternalOutput").ap()

    hres = dscr("hres", [S, D], F32).ap()
    hTd = dscr("hTd", [1024, S], BF16).ap()
    qkT = dscr("qkT", [1024, S], BF16).ap()
    catT = dscr("catT", [1024, S], BF16).ap()

    with contextlib.ExitStack() as es:
        E = es.enter_context

        def sb(name, shape, dtype):
            return E(nc.sbuf_tensor("sb_" + name, shape, dtype))

        RA = sb("RA", [128, 8, S], BF16)
        RB = sb("RB", [128, NF, 1024], BF16)
        PS = [E(nc.psum_tensor("ps%d" % i, [128, 1024], F32)) for i in range(4)]
        idf = sb("idf", [128, 128], F32)
        idb = sb("idb", [128, 128], BF16)
        ccf = sb("ccf", [128, 128], F32)
        scn = sb("scn", [128, 128], F32)
        dgt = sb("dgt", [128, NH, 128], BF16)
        onesf = sb("onesf", [128, 72], F32)
        onesm = sb("onesm", [128, 72], F32)
        onesmB = sb("onesmB", [128, 72], BF16)
        gbt = sb("gbt", [128, 2, 1024], F32)
        lamt = sb("lamt", [128, 256], F32)
        lamw = sb("lamw", [128, 16], F32)
        neglam = sb("neglam", [128, DEPTH], F32)
        gcol = sb("gcol", [128, DEPTH], F32)
        bfc = sb("bfc", [128, DEPTH * 4], F32)
        wbuf = [sb("wbuf%d" % i, [128, 8, 512], BF16) for i in range(2)]
        catb = [sb("catb%d" % i, [128, 8, 512], BF16) for i in range(2)]
        hTc = [sb("hTc%d" % i, [128, 8, 512], BF16) for i in range(2)]
        hTs = sb("hTs", [128, 8, 512], BF16)
        stg = [sb("stg%d" % i, [128, 512], BF16) for i in range(3)]
        lnx = [sb("lnx%d" % i, [128, 1024], F32) for i in range(2)]
        lnr = [sb("lnr%d" % i, [128, 1024], F32) for i in range(2)]
        lnst = [sb("lnst%d" % i, [128, 12], F32) for i in range(2)]
        lnmv = [sb("lnmv%d" % i, [128, 8], F32) for i in range(2)]
        ptb = [sb("ptb%d" % i, [128, 1024], BF16) for i in range(3)]
        sgs = [sb("sgs%d" % i, [128, 512], F32) for i in range(2)]

        ocs, K_OCS = lnr[0], ("lnr", 0)
        rcs, K_RCS = lnr[1], ("lnr", 1)
        Rt = lnr[1][:, 1016:1024]
        Rt_b = bass.AP(lnr[1], 1016, [[1024, 128], [1, 8], [0, 65]])
        Rb65 = lnr[1][:, 0:520].rearrange("p (c m) -> p c m", c=8)
        t0s, t1s, K_T = lnx[0][:, 0:512], lnx[0][:, 512:1024], ("lnx", 0)
        sqs, rss, K_SQ = lnx[1][:, 0:512], lnx[1][:, 512:1024], ("lnx", 1)
        sqsB = lnx[1][:, 0:256].bitcast(BF16)
        wu32, K_WU = lnr[0][:, :].rearrange("p (k c) -> p k c", k=8), ("lnr", 0)
        wuT, K_WT = lnr[1], ("lnr", 1)
        m12, wf32, K_M = lnx[0][:, 0:256], lnx[0][:, 256:384], ("lnx", 0)

        RAf = RA[:, :, :].rearrange("p a b -> p (a b)")
        AB = RAf.rearrange("p (t c) -> p t c", t=NT)
        ACTT = RAf[:, 0:NF * 1024].rearrange("p (f t) -> p f t", f=NF)
        VAf = RB[:, :, :].rearrange("p a b -> p (a b)")
        VA = RB[:, :, :].rearrange("p a b -> p (a b)")[:, 0:NT * NH * 65].rearrange(
            "p (t h c) -> p t h c", t=NT, h=NH)

        p = Prog(nc)
        A = p.add
        cnt = {"ev": 0, "ps": 0, "stg": 0, "trbase": 2}

        def rak(ts):
            return [("RA", t) for t in ts]

        ALLRA = rak(range(8))

        for t, d_, nm in ((idf, idf_d, "idf"), (idb, idb_d, "idb"), (ccf, ccf_d, "ccf"), (scn, scn_d, "scn")):
            A("sp", lambda e, t=t, d_=d_: e.dma_start(out=t[:], in_=d_), w=[nm], dma=("const", nm))
        A("sp", lambda e: e.dma_start(out=dgt[:, :, :], in_=dg_d), w=["dgt"], dma=("const", "dgt"))
        A("sp", lambda e: e.dma_start(out=lamt[:], in_=bass.AP(lam_p, 0, [[0, 128], [1, 256]])), w=["lamt"], dma=("const", "lamt"))
        A("dve", lambda e: e.memset(gcol[:], 0.0), w=["gcol"])
        for l in range(DEPTH):
            A("sp", lambda e, l=l: e.dma_start(out=gcol[1:65, l:l + 1], in_=subln[l:l + 1, :].rearrange("a d -> d a")),
              w=["gcol"], dma=("const", "gcol", l))
            for g in range(4):
                A("sp", lambda e, l=l, g=g: e.dma_start(out=bfc[:, l * 4 + g:l * 4 + g + 1],
                                                        in_=b_f[l, g * 128:(g + 1) * 128].rearrange("(p a) -> p a", a=1)),
                  w=["bfc"], dma=("const", "bfc", l, g))
        A("dve", lambda e: e.memset(onesf[:], 1.0), w=["onesf"])
        A("dve", lambda e: e.memset(onesm[:], 1.0 / 64.0), w=["onesm"])
        A("dve", lambda e: e.memset(onesm[0:1, :], 0.0), w=["onesm"])
        A("dve", lambda e: e.tensor_copy(out=onesmB[:], in_=onesm[:]), r=["onesm"], w=["onesmB"])
        A("dve", lambda e: e.memset(RB[:, :, :], 1.0), w=["RB"])
        for l in range(DEPTH):
            b0 = l * 128
            A("dve", lambda e, b0=b0: e.tensor_tensor(out=lamt[:, b0:b0 + 32], in0=lamt[:, b0:b0 + 32],
                                                     in1=lamt[:, b0 + 32:b0 + 64], op=ALU.mult), r=["lamt"], w=["lamt"])
            A("dve", lambda e, b0=b0: e.tensor_tensor(out=lamt[:, b0 + 64:b0 + 96], in0=lamt[:, b0 + 64:b0 + 96],
                                                     in1=lamt[:, b0 + 96:b0 + 128], op=ALU.mult), r=["lamt"], w=["lamt"])
            A("dve", lambda e, b0=b0, l=l: e.tensor_reduce(out=lamw[:, 4 * l:4 * l + 1], in_=lamt[:, b0:b0 + 32],
                                                          axis=mybir.AxisListType.X, op=ALU.add), r=["lamt"], w=["lamw"])
            A("dve", lambda e, b0=b0, l=l: e.tensor_reduce(out=lamw[:, 4 * l + 1:4 * l + 2], in_=lamt[:, b0 + 64:b0 + 96],
                                                          axis=mybir.AxisListType.X, op=ALU.add), r=["lamt", "lamw"], w=["lamw"])
            A("act", lambda e, l=l: e.activation(out=lamw[:, 4 * l + 2:4 * l + 4], in_=lamw[:, 4 * l:4 * l + 2], func=AF.Exp),
              r=["lamw"], w=["lamw"])
            A("dve", lambda e, l=l: e.scalar_tensor_tensor(out=neglam[:, l:l + 1], in0=lamw[:, 4 * l + 3:4 * l + 4],
                                                          scalar=-lam_init_fn(l), in1=lamw[:, 4 * l + 2:4 * l + 3],
                                                          op0=ALU.add, op1=ALU.subtract), r=["lamw"], w=["neglam"])
            A("dve", lambda e, l=l: e.tensor_scalar(out=gcol[0:65, l:l + 1], in0=gcol[0:65, l:l + 1],
                                                   scalar1=1.0 - lam_init_fn(l), scalar2=None, op0=ALU.mult),
              r=["gcol"], w=["gcol"])

        def load_gb(g_h, b_h, off):
            A("sp", lambda e: e.dma_start(out=gbt[:, 0, :], in_=bass.AP(g_h, off, [[0, 128], [1, 1024]])), w=["gbt"], dma="gbt")
            A("sp", lambda e: e.dma_start(out=gbt[:, 1, :], in_=bass.AP(b_h, off, [[0, 128], [1, 1024]])), w=["gbt"], dma="gbt")

        def evac(out_ap, in_ap, r, w, scale=None, bias=None, force=None):
            cnt["ev"] += 1
            if bias is not None:
                r = list(r) + ["bfc"]
            eng = force or ("act" if cnt["ev"] % 2 else "dve")
            if eng == "act":
                if bias is not None:
                    A("act", lambda e: e.activation(out=out_ap, in_=in_ap, func=AF.Identity, bias=bias), r=r, w=w)
                elif scale is not None:
                    A("act", lambda e: e.activation(out=out_ap, in_=in_ap, func=AF.Copy, scale=scale), r=r, w=w)
                else:
                    A("act", lambda e: e.activation(out=out_ap, in_=in_ap, func=AF.Copy), r=r, w=w)
            else:
                if bias is not None:
                    A("dve", lambda e: e.tensor_scalar(out=out_ap, in0=in_ap, scalar1=bias, scalar2=None, op0=ALU.add), r=r, w=w)
                elif scale is not None:
                    A("dve", lambda e: e.tensor_scalar(out=out_ap, in0=in_ap, scalar1=scale, scalar2=None, op0=ALU.mult), r=r, w=w)
                else:
                    A("dve", lambda e: e.tensor_copy(out=out_ap, in_=in_ap), r=r, w=w)

        def psnext():
            i = cnt["ps"] % 8
            cnt["ps"] += 1
            return i // 2, i % 2

        def psap(pi, b, rows=128):
            return PS[pi][0:rows, b * 512:(b + 1) * 512]

        def ln_stats(src, src_key, wk):
            st_, mv_ = lnst[wk], lnmv[wk]
            A("dve", lambda e: (e.bn_stats(out=st_[:, 0:6], in_=src[:, 0:512]),
                                e.bn_stats(out=st_[:, 6:12], in_=src[:, 512:1024]))[-1], r=[src_key], w=[("lnst", wk)])
            A("dve", lambda e: e.bn_aggr(out=mv_[:, 0:2], in_=st_[:, 0:12]), r=[("lnst", wk)], w=[("lnmv", wk)])
            A("act", lambda e: e.activation(out=mv_[:, 2:3], in_=mv_[:, 1:2], func=AF.Ln, bias=LN_EPS, scale=1.0),
              r=[("lnmv", wk)], w=[("lnmv2", wk)])
            A("act", lambda e: e.activation(out=mv_[:, 3:4], in_=mv_[:, 2:3], func=AF.Exp, scale=-0.5),
              r=[("lnmv2", wk)], w=[("lnmv3", wk)])
            A("act", lambda e: e.mul(out=mv_[:, 4:5], in_=mv_[:, 0:1], mul=-1.0), r=[("lnmv", wk)], w=[("lnmv4", wk)])
            A("act", lambda e: e.activation(out=mv_[:, 5:6], in_=mv_[:, 4:5], func=AF.Copy, scale=mv_[:, 3:4]),
              r=[("lnmv3", wk), ("lnmv4", wk)], w=[("lnmv5", wk)])

        def ln_apply_a(src, src_key, wk, tt, to_hres, to_out, hook=None):
            rt = lnr[wk]
            rk = ("lnr", wk)
            mv_ = lnmv[wk]
            A("act", lambda e: e.activation(out=rt[:], in_=src[:], func=AF.Identity, bias=mv_[:, 5:6], scale=mv_[:, 3:4]),
              r=[src_key, ("lnmv3", wk), ("lnmv5", wk)], w=[rk])
            lnp.emit_stores()
            if hook is not None:
                hook()
            A("pool", lambda e: e.tensor_tensor(out=rt[:], in0=rt[:], in1=gbt[:, 0, :], op=ALU.mult), r=[rk, "gbt"], w=[rk])
            A("pool", lambda e: e.tensor_tensor(out=rt[:], in0=rt[:], in1=gbt[:, 1, :], op=ALU.add), r=[rk, "gbt"], w=[rk])
            rows = slice(tt * 128, (tt + 1) * 128)
            if to_hres:
                lnp.stores.append(lambda: A("act", lambda e: e.dma_start(out=hres[rows, :], in_=rt[:]), r=[rk],
                                            w=[("hres", tt)], dma=("hres_w", wk)))
            if to_out:
                lnp.stores.append(lambda: A("act", lambda e: e.dma_start(out=out_d[rows, :], in_=rt[:]), r=[rk],
                                            w=[("out", tt)], dma=("out_w", wk)))

        def ln_apply_b(wk, tt):
            rt = lnr[wk]
            rk = ("lnr", wk)
            pi = cnt["trbase"] + (cnt["ps"] % 2)
            cnt["ps"] += 1
            pst = PS[pi]

            def tr(e):
                for k in range(8):
                    i = e.transpose(out=pst[:, k * 128:(k + 1) * 128], in_=rt[:, k * 128:(k + 1) * 128], identity=idf[:])
                return i
            A("pe", tr, r=[rk, "idf"], w=[("ps", pi, 0), ("ps", pi, 1)])
            t4 = tt % 4
            evac(hTs[:, :, t4 * 128:(t4 + 1) * 128], pst[:, :].rearrange("p (a b) -> p a b", a=8),
                 [("ps", pi, 0), ("ps", pi, 1)], ["hTs"], force="dve")
            if t4 == 3:
                c = tt // 4
                lnp.stores.append(lambda: A("act", lambda e: e.dma_start(
                    out=hTd[:, c * 512:(c + 1) * 512].rearrange("(kc p) t -> p kc t", p=128), in_=hTs[:, :, :]),
                    r=["hTs"], w=[("hTd", c)], dma="hTs_w"))

        class LNPipe:
            def __init__(self):
                self.q = []
                self.stores = []

            def emit_stores(self):
                while self.stores:
                    self.stores.pop(0)()

            def pre(self):
                if len(self.q) >= 2:
                    ln_apply_b(*self.q.pop(0))

            def push(self, src, src_key, wk, tt, to_hres, to_out, to_hT, hook=None):
                ln_stats(src, src_key, wk % 2)
                ln_apply_a(src, src_key, wk, tt, to_hres, to_out, hook=hook)
                if to_hT:
                    self.q.append((wk, tt))

            def flush(self):
                while self.q:
                    ln_apply_b(*self.q.pop(0))
                self.emit_stores()

        lnp = LNPipe()

        def hk(i):
            return [("hTc", i, a) for a in range(4)]

        def load_hTc(i, c):
            A("sp", lambda e: e.dma_start(out=hTc[i][:, :, :],
                                          in_=hTd[:, c * 512:(c + 1) * 512].rearrange("(kc p) t -> p kc t", p=128)),
              r=[("hTd", c)], w=hk(i), dma=("hTc", i))

        load_gb(ln_in_g, ln_in_b, 0)
        def load_x(tt):
            if tt < NT:
                A("sp", lambda e: e.dma_start(out=lnx[tt % 2][:], in_=x_d[tt * 128:(tt + 1) * 128, :]),
                  w=[("lnx", tt % 2)], dma=("lnx", tt % 2))

        def load_h(tt, lim=NT):
            if tt < lim:
                A("sp", lambda e: e.dma_start(out=lnx[tt % 2][:], in_=hres[tt * 128:(tt + 1) * 128, :]),
                  r=[("hres", tt)], w=[("lnx", tt % 2)], dma=("lnx", tt % 2))

        load_x(0)
        load_x(1)
        for tt in range(NT):
            xi = lnx[tt % 2]
            lnp.pre()
            lnp.push(xi, ("lnx", tt % 2), tt % 2, tt, True, False, True, hook=(lambda tt=tt: load_x(tt + 2)))
        lnp.flush()

        def do_layer(l):
            if stop_after == ("ln0",):
                return True
            w_in_v = w_in[l].rearrange("(kc p) n -> p kc n", p=128)
            WAB = catb
            for g in range(4):
                A("sp", lambda e, g=g: e.dma_start(out=wu32, in_=w_in_v[:, :, 1536 + g * 128:1536 + (g + 1) * 128]),
                  w=[K_WU], dma="wu32")
                A("sp", lambda e, g=g: e.dma_start(out=wf32, in_=w_f[l, g]), w=[K_M], dma="wf32")
                pi, b = psnext()
                pi2, b2 = psnext()

                def trw(e, pi=pi, b=b, pi2=pi2, b2=b2):
                    for k in range(8):
                        tgt = psap(pi, b) if k < 4 else psap(pi2, b2)
                        i = e.transpose(out=tgt[:, (k % 4) * 128:(k % 4 + 1) * 128], in_=wu32[:, k, :], identity=idf[:])
                    return i
                A("pe", trw, r=[K_WU, "idf"], w=[("ps", pi, b), ("ps", pi2, b2)])
                evac(wuT[:, 0:512], psap(pi, b), [("ps", pi, b)], [K_WT])
                evac(wuT[:, 512:1024], psap(pi2, b2), [("ps", pi2, b2)], [K_WT])
                pi, b = psnext()

                def mm12(e, pi=pi, b=b):
                    e.matmul(psap(pi, b)[:, 0:128], lhsT=ccf[:], rhs=wf32, start=True, stop=True)
                    return e.matmul(psap(pi, b)[:, 128:256], lhsT=scn[:], rhs=wf32, start=True, stop=True)
                A("pe", mm12, r=["ccf", "scn", K_M], w=[("ps", pi, b)])
                evac(m12, psap(pi, b)[:, 0:256], [("ps", pi, b)], [K_M])
                for q4 in range(4):
                    pi, b = psnext()

                    def mmab3(e, pi=pi, b=b, q4=q4):
                        i = None
                        for k2 in range(2):
                            kc = q4 * 2 + k2
                            i = e.matmul(psap(pi, b)[:, k2 * 256:(k2 + 1) * 256], lhsT=wuT[:, kc * 128:(kc + 1) * 128],
                                         rhs=m12, start=True, stop=True)
                        return i
                    A("pe", mmab3, r=[K_WT, K_M], w=[("ps", pi, b)])
                    hf = g // 2
                    c0 = (g % 2) * 256
                    evac(WAB[hf][:, q4 * 2:q4 * 2 + 2, c0:c0 + 256],
                         psap(pi, b).rearrange("p (k c) -> p k c", k=2), [("ps", pi, b)], [("catb", hf)])

            for blk in range(2):
                A("pool", lambda e, blk=blk: e.dma_start(out=wbuf[blk][:, :, :], in_=w_in_v[:, :, blk * 512:(blk + 1) * 512]),
                  w=[("wbuf", blk, 0), ("wbuf", blk, 1)], dma=("wbuf", blk))
            for tc in range(8):
                load_hTc(tc % 2, tc)
                hc = hTc[tc % 2]
                for blk in range(2):
                    for m in range(4):
                        pi, b = psnext()

                        def mmqk(e, hc=hc, blk=blk, m=m, pi=pi, b=b):
                            for kc in range(8):
                                i = e.matmul(psap(pi, b), lhsT=wbuf[blk][:, kc, m * 128:(m + 1) * 128],
                                             rhs=hc[:, kc, :], start=(kc == 0), stop=(kc == 7))
                            return i
                        A("pe", mmqk, r=[("wbuf", blk, 0), ("wbuf", blk, 1)] + hk(tc % 2), w=[("ps", pi, b)])
                        si = cnt["stg"] % 3
                        cnt["stg"] += 1
                        evac(stg[si][:, :], psap(pi, b), [("ps", pi, b)], [("stg", si)],
                             scale=(SCALE if blk == 0 else None))
                        r0 = blk * 512 + m * 128
                        A("sp", lambda e, si=si, r0=r0, tc=tc: e.dma_start(out=qkT[r0:r0 + 128, tc * 512:(tc + 1) * 512],
                                                                         in_=stg[si][:, :]),
                          r=[("stg", si)], w=[("qkT", r0 // 128)], dma=("stg_w", si))
            A("pool", lambda e: e.dma_start(out=wbuf[0][:, :, :], in_=w_in_v[:, :, 1024:1536]),
              w=[("wbuf", 0, 0), ("wbuf", 0, 1)], dma=("wbuf", 0))
            for tc in range(8):
                load_hTc(tc % 2, tc)
                hc = hTc[tc % 2]
                for t4 in range(4):
                    tt = tc * 4 + t4
                    pi, b = psnext()

                    def mmv(e, hc=hc, t4=t4, pi=pi, b=b):
                        for kc in range(8):
                            i = e.matmul(psap(pi, b), lhsT=hc[:, kc, t4 * 128:(t4 + 1) * 128], rhs=wbuf[0][:, kc, :],
                                         start=(kc == 0), stop=(kc == 7))
                        return i
                    A("pe", mmv, r=[("wbuf", 0, 0), ("wbuf", 0, 1)] + hk(tc % 2), w=[("ps", pi, b)])
                    evac(VA[:, tt, :, 1:65], psap(pi, b).rearrange("p (h c) -> p h c", h=NH), [("ps", pi, b)], ["RB"])
                    for hf in range(2):
                        pi, b = psnext()

                        def mmab4(e, hc=hc, t4=t4, hf=hf, pi=pi, b=b):
                            for kc in range(8):
                                i = e.matmul(psap(pi, b), lhsT=hc[:, kc, t4 * 128:(t4 + 1) * 128], rhs=WAB[hf][:, kc, :],
                                             start=(kc == 0), stop=(kc == 7))
                            return i
                        A("pe", mmab4, r=[("catb", hf)] + hk(tc % 2), w=[("ps", pi, b)])
                        evac(AB[:, tt, hf * 512:(hf + 1) * 512], psap(pi, b), [("ps", pi, b)], [("RA", tt // 4)])
            for i_ in range(16):
                lo_, hi_ = AB[:, i_, :], AB[:, i_ + 16, :]
                A("dve", lambda e, lo_=lo_, hi_=hi_: e.tensor_tensor(out=hi_, in0=lo_, in1=hi_, op=ALU.subtract),
                  r=[("RA", i_ // 4), ("RA", (i_ + 16) // 4)], w=[("RA", (i_ + 16) // 4)])
                A("dve", lambda e, lo_=lo_, hi_=hi_: e.scalar_tensor_tensor(out=lo_, in0=lo_, scalar=2.0, in1=hi_,
                                                                           op0=ALU.mult, op1=ALU.subtract),
                  r=[("RA", i_ // 4), ("RA", (i_ + 16) // 4)], w=[("RA", i_ // 4)])
            if stop_after == ("A", l):
                return True

            ti = 0
            for tc in range(4):
                accE = [psnext() for _ in range(4)]
                accO = [psnext() for _ in range(4)]
                for st_ in range(NT):
                    sl_ = ti % 8
                    ti += 1
                    hb_, a_ = hTc[sl_ // 4], sl_ % 4
                    tk = ("hTc", sl_ // 4, a_)
                    tcs, tss = hb_[:, 2 * a_, :], hb_[:, 2 * a_ + 1, :]
                    A("sp", lambda e, hb_=hb_, a_=a_, st_=st_, tc=tc: e.dma_start(
                        out=hb_[:, 2 * a_:2 * a_ + 2, :], in_=cst_d[st_ * 128:(st_ + 1) * 128, tc, :, :]), w=[tk], dma=("tab", sl_))
                    accs = accE if st_ < 16 else accO

                    def mmf(e, tcs=tcs, tss=tss, st_=st_, accs=accs):
                        for g in range(4):
                            pi, b = accs[g]
                            e.matmul(psap(pi, b), lhsT=AB[:, st_, g * 256:g * 256 + 128], rhs=tcs,
                                     start=(st_ % 16 == 0), stop=False)
                            i = e.matmul(psap(pi, b), lhsT=AB[:, st_, g * 256 + 128:g * 256 + 256], rhs=tss,
                                         start=False, stop=(st_ % 16 == 15))
                        return i
                    A("pe", mmf, r=[tk, ("RA", st_ // 4)], w=[("ps",) + a_ for a_ in accs])
                for g in range(4):
                    pb_ = ptb[g % 3]
                    pkey = ("ptb", g % 3)
                    for par, acc in ((0, accE[g]), (1, accO[g])):
                        pi, b = acc
                        evac(pb_[:, par:1024:2], psap(pi, b), [("ps", pi, b)], [pkey], bias=bfc[:, l * 4 + g:l * 4 + g + 1])
                    r0 = 512 + g * 128
                    A("sp", lambda e, pb_=pb_, r0=r0, tc=tc: e.dma_start(out=catT[r0:r0 + 128, tc * 1024:(tc + 1) * 1024],
                                                                       in_=pb_[:, :]),
                      r=[pkey], w=[("catT", r0 // 128, 2 * tc), ("catT", r0 // 128, 2 * tc + 1)], dma=("ptb_w", g % 3))
            if stop_after == ("B2", l):
                return True

            deferred = []

            def run_deferred(n=1):
                for _ in range(n):
                    if deferred:
                        deferred.pop(0)()

            def _dmin(qc, j):
                if j * 128 + 127 < qc * 512:
                    return qc * 512 - (j * 128 + 127)
                if j * 128 > qc * 512 + 511:
                    return j * 128 - (qc * 512 + 511)
                return 0
            blocks = []
            gidx = 0
            for h in range(NH):
                for qc in range(8):
                    js = [j for j in range(NT) if (2.0 ** -(h + 1)) * _dmin(qc, j) <= SKIP_BIAS]
                    for j in js:
                        blocks.append((h, qc, j, gidx, j == js[0], j == js[-1]))
                    gidx += 1

            def aug(hh, t):
                return RA[:, (hh % 2) * 4 + t, :]

            def load_head(h):
                s_ = h % 2
                for c in range(2):
                    A("sp", lambda e, c=c: e.dma_start(out=aug(h, 0)[c * 64:c * 64 + 32, :],
                                                       in_=qkT[h * 64 + c * 32:h * 64 + c * 32 + 32, :]),
                      r=[("qkT", (h * 64) // 128)], w=rak([s_ * 4 + 0]), dma=("aug", s_, 0))
                    A("sp", lambda e, c=c: e.dma_start(out=aug(h, 0)[c * 64 + 32:c * 64 + 64, :], in_=qx_d[h]),
                      w=rak([s_ * 4 + 0]), dma=("aug", s_, 0))
                    for t, nm in ((1, "lo"), (2, "up"), (3, "dg")):
                        A("sp", lambda e, c=c, t=t: e.dma_start(out=aug(h, t)[c * 64:c * 64 + 32, :],
                                                                in_=qkT[512 + h * 64 + c * 32:512 + h * 64 + c * 32 + 32, :]),
                          r=[("qkT", (512 + h * 64) // 128)], w=rak([s_ * 4 + t]), dma=("aug", s_, t))
                        A("sp", lambda e, c=c, t=t, nm=nm: e.dma_start(out=aug(h, t)[c * 64 + 32:c * 64 + 64, :],
                                                                      in_=kx_d[nm][h]),
                          w=rak([s_ * 4 + t]), dma=("aug", s_, t))

            def emit_S(bi):
                h, qc, j = blocks[bi][0:3]
                si = bi % 2
                s_ = h % 2
                q0 = qc * 512
                Q = aug(h, 0)
                rel = j - 4 * qc

                def mms(e):
                    i = None
                    for c in range(2):
                        pr = slice(c * 64, c * 64 + 64)
                        ob = PS[si][:, c * 512:(c + 1) * 512]
                        kc_ = slice(j * 128, (j + 1) * 128)
                        if rel < 0 or rel > 3:
                            K = aug(h, 1 if rel < 0 else 2)
                            i = e.matmul(ob, lhsT=K[pr, kc_], rhs=Q[pr, q0:q0 + 512], start=True, stop=True)
                        else:
                            lo_c = 128 * rel
                            if lo_c > 0:
                                e.matmul(ob[:, 0:lo_c], lhsT=aug(h, 2)[pr, kc_], rhs=Q[pr, q0:q0 + lo_c],
                                         start=True, stop=True, skip_group_check=True)
                            e.matmul(ob[:, lo_c:lo_c + 128], lhsT=aug(h, 3)[pr, kc_], rhs=Q[pr, q0 + lo_c:q0 + lo_c + 128],
                                     start=True, stop=False, skip_group_check=True)
                            i = e.matmul(ob[:, lo_c:lo_c + 128], lhsT=idb[:, :], rhs=dgt[:, h, :], start=False, stop=True,
                                         skip_group_check=True)
                            if lo_c + 128 < 512:
                                i = e.matmul(ob[:, lo_c + 128:512], lhsT=aug(h, 1)[pr, kc_],
                                             rhs=Q[pr, q0 + lo_c + 128:q0 + 512], start=True, stop=True, skip_group_check=True)
                    return i
                A("pe", mms, r=rak([s_ * 4 + t for t in range(4)]) + ["idb", "dgt"], w=[("ps", si, 0), ("ps", si, 1)])

            def emit_EXP(bi):
                si = bi % 2
                pt = ptb[bi % 3]
                A("act", lambda e: e.activation(out=pt[:, :], in_=PS[si][:, :], func=AF.Exp),
                  r=[("ps", si, 0), ("ps", si, 1)], w=[("ptb", bi % 3)])

            def emit_PV(bi):
                h, qc, j, g_, first, last_ = blocks[bi]
                gi = g_ % 2
                pt = ptb[bi % 3]

                def mmpv(e):
                    for c in range(2):
                        i = e.matmul(PS[2 + gi][0:65, c * 512:(c + 1) * 512], lhsT=VA[:, j, h, :],
                                     rhs=pt[:, c * 512:(c + 1) * 512], start=first, stop=last_)
                    return i
                A("pe", mmpv, r=[("ptb", bi % 3), "RB"], w=[("ps", 2 + gi, 0), ("ps", 2 + gi, 1)])

            def post(h, qc, gi):
                pv = PS[2 + gi]
                pk = [("ps", 2 + gi, 0), ("ps", 2 + gi, 1)]

                def s1():
                    A("dve", lambda e: e.tensor_copy(out=ocs[0:65, :], in_=pv[0:65, :]), r=pk, w=[K_OCS])

                def s1b():
                    def trs(e):
                        for c8 in range(8):
                            i = e.transpose(out=pv[:, c8:c8 + 1], in_=ocs[0:1, c8 * 128:(c8 + 1) * 128], identity=idf[0:1, 0:1])
                        return i
                    A("pe", trs, r=[K_OCS, "idf"], w=[pk[0]])
                    A("dve", lambda e: e.reciprocal(out=Rt, in_=pv[:, 0:8]), r=[pk[0]], w=[K_RCS])
                    A("dve", lambda e: e.tensor_scalar(out=Rt[:, 4:8], in0=Rt[:, 4:8], scalar1=neglam[:, l:l + 1],
                                                       scalar2=None, op0=ALU.mult), r=[K_RCS, "neglam"], w=[K_RCS])
                    A("dve", lambda e: e.tensor_copy(out=Rb65, in_=Rt_b), r=[K_RCS], w=[K_RCS])

                def s2():
                    def mmb(e):
                        for c8 in range(8):
                            i = e.matmul(pv[0:65, c8 * 128:(c8 + 1) * 128], lhsT=Rb65[:, c8, :], rhs=idf[:, :],
                                         start=True, stop=True, skip_group_check=True)
                        return i
                    A("pe", mmb, r=[K_RCS, "idf"], w=pk)
                    A("dve", lambda e: e.tensor_tensor(out=t0s[0:65, :], in0=ocs[0:65, 0:512], in1=pv[0:65, 0:512], op=ALU.mult),
                      r=[K_OCS, pk[0]], w=[K_T])
                    A("dve", lambda e: e.tensor_tensor(out=t1s[0:65, :], in0=ocs[0:65, 512:1024], in1=pv[0:65, 512:1024],
                                                       op=ALU.mult), r=[K_OCS, pk[1]], w=[K_T])
                    A("dve", lambda e: e.tensor_tensor(out=t0s[0:65, :], in0=t0s[0:65, :], in1=t1s[0:65, :], op=ALU.add),
                      r=[K_T], w=[K_T])
                    A("dve", lambda e: e.tensor_tensor(out=sqsB[0:65, :], in0=t0s[0:65, :], in1=t0s[0:65, :], op=ALU.mult),
                      r=[K_T], w=[K_SQ])

                def s3():
                    A("pe", lambda e: e.matmul(pv[0:65, 0:512], lhsT=onesmB[0:65, 0:65], rhs=sqsB[0:65, :], start=True, stop=True),
                      r=[K_SQ, "onesmB"], w=[pk[0]])
                    A("act", lambda e: e.activation(out=rss[0:65, :], in_=pv[0:65, 0:512], func=AF.Ln, bias=SUBLN_EPS, scale=1.0),
                      r=[pk[0]], w=[K_SQ])
                    A("act", lambda e: e.activation(out=rss[0:65, :], in_=rss[0:65, :], func=AF.Exp, scale=-0.5),
                      r=[K_SQ], w=[K_SQ])
                    si = cnt["stg"] % 3
                    cnt["stg"] += 1
                    A("dve", lambda e: e.scalar_tensor_tensor(out=stg[si][0:65, :], in0=t0s[0:65, :],
                                                              scalar=gcol[0:65, l:l + 1], in1=rss[0:65, :],
                                                              op0=ALU.mult, op1=ALU.mult),
                      r=[K_T, K_SQ, "gcol"], w=[("stg", si)])
                    A("sp", lambda e: e.dma_start(out=catT[h * 64:(h + 1) * 64, qc * 512:(qc + 1) * 512], in_=stg[si][1:65, :]),
                      r=[("stg", si)], w=[("catT", "a", h, qc)], dma=("stg_w", si))
                return s1, [(2, s1b), (5, s2), (10, s3)]

            nb = len(blocks)
            load_head(0)
            emit_S(0)
            emit_S(1)
            pos = 0
            for bi in range(nb):
                h, qc, j, g_, first, last_ = blocks[bi]
                if first:
                    pos = 0
                if first and qc == 0 and h + 1 < NH:
                    load_head(h + 1)
                emit_EXP(bi)
                if bi + 2 < nb:
                    emit_S(bi + 2)
                emit_PV(bi)
                while deferred and deferred[0][0] <= pos:
                    deferred.pop(0)[1]()
                pos += 1
                if last_:
                    while deferred:
                        deferred.pop(0)[1]()
                    s1_, rest = post(h, qc, g_ % 2)
                    s1_()
                    deferred.extend(rest)
            while deferred:
                deferred.pop(0)[1]()
            if stop_after == ("B", l):
                return True

            w_o_v = w_o[l].rearrange("(kc p) n -> p kc n", p=128)
            for i in range(2):
                A("pool", lambda e, i=i: e.dma_start(out=wbuf[i][:, :, :], in_=w_o_v[:, :, i * 512:(i + 1) * 512]),
                  w=[("wbuf", i, 0), ("wbuf", i, 1)], dma=("wbuf", i))
            load_gb(ln1_g, ln1_b, l * D)
            cat_keys = [("catT", rr, tc_) for rr in range(4, 8) for tc_ in range(8)] + \
                       [("catT", "a", hh, qq) for hh in range(NH) for qq in range(8)]
            def load_cat(k):
                if k < 8:
                    A("sp", lambda e: e.dma_start(
                        out=catb[k % 2][:, :, :], in_=catT[:, k * 512:(k + 1) * 512].rearrange("(kc p) t -> p kc t", p=128)),
                      r=cat_keys, w=[("catb", k % 2)], dma=("catb", k % 2))

            load_cat(0)
            for tt in range(NT):
                if tt % 4 == 0:
                    cb = catb[(tt // 4) % 2]
                    ck = ("catb", (tt // 4) % 2)
                    load_cat(tt // 4 + 1)
                xi = lnx[tt % 2]
                if tt == 0:
                    load_h(0)
                    load_h(1)
                lnp.pre()
                pi = cnt["ps"] % 2
                cnt["ps"] += 1

                def mmo(e, cb=cb, tt=tt, pi=pi):
                    for cc in range(2):
                        for kc in range(8):
                            i = e.matmul(PS[pi][:, cc * 512:(cc + 1) * 512], lhsT=cb[:, kc, (tt % 4) * 128:(tt % 4 + 1) * 128],
                                         rhs=wbuf[cc][:, kc, :], start=(kc == 0), stop=(kc == 7))
                    return i
                A("pe", mmo, r=[ck] + [("wbuf", i, j_) for i in range(2) for j_ in range(2)], w=[("ps", pi, 0), ("ps", pi, 1)])
                A("dve", lambda e, xi=xi, pi=pi: e.scalar_tensor_tensor(out=xi[:], in0=xi[:], scalar=ALPHA, in1=PS[pi][:, :],
                                                                       op0=ALU.mult, op1=ALU.add),
                  r=[("lnx", tt % 2), ("ps", pi, 0), ("ps", pi, 1)], w=[("lnx", tt % 2)])
                lnp.push(xi, ("lnx", tt % 2), tt % 2, tt, True, False, True, hook=(lambda tt=tt: load_h(tt + 2)))
            lnp.flush()
            if stop_after == ("C", l):
                return True

            A("pool", lambda e: e.dma_start(out=RB[:, :, :], in_=w_dn[l].rearrange("(f p) n -> p f n", p=128)),
              w=["RB"], dma="RB")
            load_gb(ln2_g, ln2_b, l * D)
            cnt["trbase"] = 0
            w_gu_v = w_gu[l].rearrange("(kc p) n -> p kc n", p=128)
            last = (l == DEPTH - 1)
            gi_ = 0
            for tb in range(4):
                load_hTc(0, 2 * tb)
                load_hTc(1, 2 * tb + 1)
                for f in range(NF):
                    ws = gi_ % 4
                    gi_ += 1
                    wt_ = wbuf[ws // 2]
                    o_ = (ws % 2) * 256
                    wk_ = ("wbuf", ws // 2, ws % 2)
                    A("pool", lambda e, wt_=wt_, o_=o_, f=f: e.dma_start(out=wt_[:, :, o_:o_ + 128],
                                                                        in_=w_gu_v[:, :, f * 128:(f + 1) * 128]),
                      w=[wk_], dma=wk_)
                    A("pool", lambda e, wt_=wt_, o_=o_, f=f: e.dma_start(out=wt_[:, :, o_ + 128:o_ + 256],
                                                                        in_=w_gu_v[:, :, DFF + f * 128:DFF + (f + 1) * 128]),
                      w=[wk_], dma=wk_)
                    for tc in range(2):
                        pi = cnt["ps"] % 2
                        cnt["ps"] += 1

                        def mmgu(e, wt_=wt_, o_=o_, tc=tc, pi=pi):
                            for u_ in range(2):
                                for kc in range(8):
                                    i = e.matmul(PS[pi][:, u_ * 512:(u_ + 1) * 512],
                                                 lhsT=wt_[:, kc, o_ + u_ * 128:o_ + (u_ + 1) * 128],
                                                 rhs=hTc[tc][:, kc, :], start=(kc == 0), stop=(kc == 7))
                            return i
                        A("pe", mmgu, r=[wk_] + hk(tc), w=[("ps", pi, 0), ("ps", pi, 1)])
                        sgt = sgs[pi]
                        A("act", lambda e, sgt=sgt, pi=pi: e.activation(out=sgt[:, :], in_=PS[pi][:, 0:512], func=AF.Silu),
                          r=[("ps", pi, 0)], w=[("sgs", pi)])
                        A("dve", lambda e, sgt=sgt, pi=pi, f=f, tc=tc: e.tensor_tensor(
                            out=ACTT[:, f, tc * 512:(tc + 1) * 512], in0=sgt[:, :], in1=PS[pi][:, 512:1024], op=ALU.mult),
                          r=[("sgs", pi), ("ps", pi, 1)], w=[("RA", f // 4)])
                for t8 in range(8):
                    tt = tb * 8 + t8
                    xi = lnx[tt % 2]
                    if t8 == 0:
                        load_h(tt)
                        load_h(tt + 1)
                    lnp.pre()
                    pi = 2 + (cnt["ps"] % 2)
                    cnt["ps"] += 1

                    def mmd(e, t8=t8, pi=pi):
                        for cc in range(2):
                            for f in range(NF):
                                i = e.matmul(PS[pi][:, cc * 512:(cc + 1) * 512], lhsT=ACTT[:, f, t8 * 128:(t8 + 1) * 128],
                                             rhs=RB[:, f, cc * 512:(cc + 1) * 512], start=(f == 0), stop=(f == NF - 1))
                        return i
                    A("pe", mmd, r=rak(range(6)) + ["RB"], w=[("ps", pi, 0), ("ps", pi, 1)])
                    A("dve", lambda e, xi=xi, pi=pi: e.scalar_tensor_tensor(out=xi[:], in0=xi[:], scalar=ALPHA, in1=PS[pi][:, :],
                                                                           op0=ALU.mult, op1=ALU.add),
                      r=[("lnx", tt % 2), ("ps", pi, 0), ("ps", pi, 1)], w=[("lnx", tt % 2)])
                    lnp.push(xi, ("lnx", tt % 2), tt % 2, tt, not last, last, not last,
                             hook=(lambda tt=tt, tb=tb: load_h(tt + 2, (tb + 1) * 8)))
                lnp.flush()
            cnt["trbase"] = 2
            if not last:
                A("dve", lambda e: e.memset(RB[:, :, :], 1.0), w=["RB"])
            if stop_after == ("D", l):
                return True

            return False

        for l_ in range(DEPTH):
            if do_layer(l_):
                break

        fin_keys = [k for k in p.res.keys() if isinstance(k, tuple) and k[0] in ("out", "hres", "qkT", "catT", "hTd")]
        A("sp", lambda e: None, r=fin_keys)
        p.emit()
    return nc


_NC = {}


def _get_nc():
    if "nc" not in _NC:
        _NC["nc"] = build()
    return _NC["nc"]


def _in_maps(inputs):
    c = _consts()
    f32 = lambda a: np.ascontiguousarray(np.asarray(a, dtype=np.float32))
    shared = {
        "ln_in_g": f32(inputs["ln_in_g"]), "ln_in_b": f32(inputs["ln_in_b"]),
        "w_in": f32(inputs["w_in"]), "lam_params": f32(inputs["lam_params"]).reshape(-1),
        "subln_g": f32(inputs["subln_g"]), "w_f": f32(inputs["w_f"]), "b_f": f32(inputs["b_f"]),
        "w_o": f32(inputs["w_o"]), "ln1_g": f32(inputs["ln1_g"]).reshape(-1), "ln1_b": f32(inputs["ln1_b"]).reshape(-1),
        "w_gu": f32(inputs["w_gu"]), "w_down": f32(inputs["w_down"]),
        "ln2_g": f32(inputs["ln2_g"]).reshape(-1), "ln2_b": f32(inputs["ln2_b"]).reshape(-1),
    }
    shared.update(c)
    x = f32(inputs["x"])
    return [dict(shared, x=x[b]) for b in range(x.shape[0])]


def kernel(**inputs):
    nc = _get_nc()
    maps = _in_maps(inputs)
    res = run_bass_kernel_spmd(nc, maps, core_ids=list(range(8)))
    return np.stack([np.asarray(r["out"], dtype=np.float32) for r in res.results], axis=0)
```

```python
import math
import contextlib
import numpy as np
import ml_dtypes
import concourse.bass as bass
import concourse.mybir as mybir
from concourse.bass_utils import run_bass_kernel_spmd

F32 = mybir.dt.float32
BF16 = mybir.dt.bfloat16
F32R = mybir.dt.float32r
AF = mybir.ActivationFunctionType
ALU = mybir.AluOpType

S = 4096
D = 1024
NT = 32
DEPTH = 2
DFF = 2816
NF = 22
NH = 8
ALPHA = (2.0 * DEPTH) ** 0.25
SCALE = 32 ** -0.5
LN_EPS = 1e-5
SUBLN_EPS = 1e-5
SKIP_BIAS = 48.0
SAME_ENGINE_SYNC = True
STATS = {}


def lam_init_fn(l):
    return 0.8 - 0.6 * math.exp(-0.3 * l)


class Op:
    __slots__ = ("eng", "fn", "dma_key", "dma_cnt", "signal", "sig_idx", "waits")

    def __init__(self, eng, fn, dma_key):
        self.eng = eng
        self.fn = fn
        self.dma_key = dma_key
        self.dma_cnt = 0
        self.signal = False
        self.sig_idx = 0
        self.waits = []


class Prog:
    ENGS = ("pe", "act", "dve", "pool", "sp")

    def __init__(self, nc):
        self.nc = nc
        self.streams = {e: [] for e in self.ENGS}
        self.res = {}
        self.dma_counts = {}

    def add(self, eng, fn, r=(), w=(), dma=None):
        op = Op(eng, fn, dma)
        if dma is not None:
            c = self.dma_counts.get(dma, 0) + 1
            self.dma_counts[dma] = c
            op.dma_cnt = c
        deps = []
        raw = set()
        for k in r:
            st = self.res.get(k)
            if st is None:
                st = [None, []]
                self.res[k] = st
            if st[0] is not None:
                deps.append(st[0])
                raw.add(id(st[0]))
            st[1].append(op)
        for k in w:
            st = self.res.get(k)
            if st is not None:
                if st[0] is not None:
                    deps.append(st[0])
                for rd in st[1]:
                    if rd is not op:
                        deps.append(rd)
            self.res[k] = [op, []]
        seen = set()
        for d in deps:
            if id(d) in seen or d is op:
                continue
            seen.add(id(d))
            if d.dma_key is None:
                if d.eng == eng and (eng == "pe" or not SAME_ENGINE_SYNC):
                    continue
                if d.eng == eng and id(d) not in raw:
                    STATS["skipped"] = STATS.get("skipped", 0) + 1
                    continue
                if d.eng == eng:
                    STATS["self_raw"] = STATS.get("self_raw", 0) + 1
                d.signal = True
            op.waits.append(d)
        self.streams[eng].append(op)
        return op

    def emit(self):
        nc = self.nc
        with contextlib.ExitStack() as es:
            esem = {e: es.enter_context(nc.semaphore("s_" + e)) for e in ("pe", "act", "dve", "pool")}
            dsem = {k: es.enter_context(nc.semaphore("d_%d" % i)) for i, k in enumerate(self.dma_counts)}
            for e, st in self.streams.items():
                c = 0
                for op in st:
                    if op.dma_key is None and op.signal:
                        c += 1
                        op.sig_idx = c
            block = es.enter_context(nc.Block())
            handles = {"pe": block.tensor, "act": block.scalar, "dve": block.vector,
                       "pool": block.gpsimd, "sp": block.sync}

            def mk(ename):
                st = self.streams[ename]

                def body(eng):
                    waited = {}
                    for op in st:
                        for d in op.waits:
                            if d.dma_key is not None:
                                key = ("d", d.dma_key)
                                val = d.dma_cnt * 16
                                sem = dsem[d.dma_key]
                            else:
                                key = ("e", d.eng)
                                val = d.sig_idx
                                sem = esem[d.eng]
                            if waited.get(key, 0) >= val:
                                continue
                            waited[key] = val
                            eng.wait_ge(sem, val)
                        inst = op.fn(eng)
                        if inst is None:
                            continue
                        if op.dma_key is not None:
                            inst.then_inc(dsem[op.dma_key], 16)
                        elif op.signal:
                            inst.then_inc(esem[op.eng], 1)

                return body

            for ename in self.ENGS:
                if self.streams[ename]:
                    handles[ename](mk(ename))


_CONSTS = None


def _consts():
    global _CONSTS
    if _CONSTS is not None:
        return _CONSTS
    bf = ml_dtypes.bfloat16
    c = {}
    c["idf"] = np.eye(128, dtype=np.float32)
    c["idb"] = np.eye(128, dtype=np.float32).astype(bf)
    ab = np.outer(np.arange(128), np.arange(128)) % 128
    ang = 2.0 * np.pi * ab / 128.0
    c["ccf"] = (np.cos(ang) / math.sqrt(128.0)).astype(np.float32)
    c["scn"] = (-np.sin(ang) / math.sqrt(128.0)).astype(np.float32)
    k = np.arange(S, dtype=np.int64)
    tab_c = (np.cos(2.0 * np.pi * k / S) / 64.0).astype(np.float32).astype(bf)
    tab_s = (np.sin(2.0 * np.pi * k / S) / 64.0).astype(np.float32).astype(bf)
    half = k[:S // 2]
    st = np.concatenate([np.outer(half, 2 * half), np.outer(half, 2 * half + 1)], axis=0)
    st = (st % S).astype(np.int32)
    c["cst"] = np.ascontiguousarray(np.stack([tab_c[st].reshape(S, 4, 512), tab_s[st].reshape(S, 4, 512)], axis=2))
    pos = np.arange(S)
    pa = (pos // 64).astype(np.float32) * 64.0
    pb = (pos % 64).astype(np.float32)
    qx = np.zeros((NH, 32, S), np.float32)
    klo = np.zeros((NH, 32, S), np.float32)
    for h in range(NH):
        sl = 2.0 ** -(h + 1)
        qx[h, 1] = -sl * pa
        qx[h, 2] = -sl * pb
        qx[h, 3] = 1.0
        qx[h, 4] = 1.0
        klo[h, 1] = 1.0
        klo[h, 2] = 1.0
        klo[h, 3] = sl * pa
        klo[h, 4] = sl * pb
    c["qx"] = qx.astype(bf)
    c["kxlo"] = klo.astype(bf)
    c["kxup"] = (-klo).astype(bf)
    c["kxdg"] = np.zeros((NH, 32, S), np.float32).astype(bf)
    dg = np.zeros((128, NH, 128), np.float32)
    pf = np.abs(np.arange(128)[None, :] - np.arange(128)[:, None]).astype(np.float32)
    for h in range(NH):
        dg[:, h, :] = -(2.0 ** -(h + 1)) * pf
    c["dg"] = dg.astype(bf)
    _CONSTS = c
    return c


def build(dbg=None, stop_after=None):
    dbg = dbg or set()
    nc = bass.Bass("TRN2", target_bir_lowering=False)

    def din(name, shape, dtype):
        return nc.dram_tensor(name, shape, dtype, kind="ExternalInput")

    def dscr(name, shape, dtype):
        return nc.dram_tensor(name, shape, dtype, kind="ExternalOutput" if name in dbg else "Internal")

    x_d = din("x", [S, D], F32).ap()
    ln_in_g = din("ln_in_g", [D], F32)
    ln_in_b = din("ln_in_b", [D], F32)
    w_in = din("w_in", [DEPTH, D, 2048], F32).ap()
    lam_p = din("lam_params", [DEPTH * 4 * 32], F32)
    subln = din("subln_g", [DEPTH, 64], F32).ap()
    w_f = din("w_f", [DEPTH, 4, 128, 128], F32).ap()
    b_f = din("b_f", [DEPTH, 512], F32).ap()
    w_o = din("w_o", [DEPTH, D, D], F32).ap()
    ln1_g = din("ln1_g", [DEPTH * D], F32)
    ln1_b = din("ln1_b", [DEPTH * D], F32)
    w_gu = din("w_gu", [DEPTH, D, 2 * DFF], F32).ap()
    w_dn = din("w_down", [DEPTH, DFF, D], F32).ap()
    ln2_g = din("ln2_g", [DEPTH * D], F32)
    ln2_b = din("ln2_b", [DEPTH * D], F32)
    idf_d = din("idf", [128, 128], F32).ap()
    idb_d = din("idb", [128, 128], BF16).ap()
    ccf_d = din("ccf", [128, 128], F32).ap()
    scn_d = din("scn", [128, 128], F32).ap()
    cst_d = din("cst", [S, 4, 2, 512], BF16).ap()
    qx_d = din("qx", [NH, 32, S], BF16).ap()
    kx_d = {"lo": din("kxlo", [NH, 32, S], BF16).ap(), "up": din("kxup", [NH, 32, S], BF16).ap(),
            "dg": din("kxdg", [NH, 32, S], BF16).ap()}
    dg_d = din("dg", [128, NH, 128], BF16).ap()
    out_d = nc.dram_tensor("out", [S, D], F32, kind="ExternalOutput").ap()

    hres = dscr("hres", [S, D], F32).ap()
    hTd = dscr("hTd", [1024, S], BF16).ap()
    qkT = dscr("qkT", [1024, S], BF16).ap()
    catT = dscr("catT", [1024, S], BF16).ap()

    with contextlib.ExitStack() as es:
        E = es.enter_context

        def sb(name, shape, dtype):
            return E(nc.sbuf_tensor("sb_" + name, shape, dtype))

        RA = sb("RA", [128, 8, S], BF16)
        RB = sb("RB", [128, NF, 1024], BF16)
        PS = [E(nc.psum_tensor("ps%d" % i, [128, 1024], F32)) for i in range(4)]
        idf = sb("idf", [128, 128], F32)
        idb = sb("idb", [128, 128], BF16)
        ccf = sb("ccf", [128, 128], F32)
        scn = sb("scn", [128, 128], F32)
        dgt = sb("dgt", [128, NH, 128], BF16)
        onesf = sb("onesf", [128, 72], F32)
        onesm = sb("onesm", [128, 72], F32)
        onesmB = sb("onesmB", [128, 72], BF16)
        gbt = sb("gbt", [128, 2, 1024], F32)
        lamt = sb("lamt", [128, 256], F32)
        lamw = sb("lamw", [128, 16], F32)
        neglam = sb("neglam", [128, DEPTH], F32)
        gcol = sb("gcol", [128, DEPTH], F32)
        bfc = sb("bfc", [128, DEPTH * 4], F32)
        wbuf = [sb("wbuf%d" % i, [128, 8, 512], BF16) for i in range(2)]
        catb = [sb("catb%d" % i, [128, 8, 512], BF16) for i in range(2)]
        hTc = [sb("hTc%d" % i, [128, 8, 512], BF16) for i in range(2)]
        hTs = sb("hTs", [128, 8, 512], BF16)
        stg = [sb("stg%d" % i, [128, 512], BF16) for i in range(3)]
        lnx = [sb("lnx%d" % i, [128, 1024], F32) for i in range(2)]
        lnr = [sb("lnr%d" % i, [128, 1024], F32) for i in range(2)]
        lnst = [sb("lnst%d" % i, [128, 12], F32) for i in range(2)]
        lnmv = [sb("lnmv%d" % i, [128, 8], F32) for i in range(2)]
        ptb = [sb("ptb%d" % i, [128, 1024], BF16) for i in range(3)]
        sgs = [sb("sgs%d" % i, [128, 512], F32) for i in range(2)]

        ocs, K_OCS = lnr[0], ("lnr", 0)
        rcs, K_RCS = lnr[1], ("lnr", 1)
        Rt = lnr[1][:, 1016:1024]
        Rt_b = bass.AP(lnr[1], 1016, [[1024, 128], [1, 8], [0, 65]])
        Rb65 = lnr[1][:, 0:520].rearrange("p (c m) -> p c m", c=8)
        t0s, t1s, K_T = lnx[0][:, 0:512], lnx[0][:, 512:1024], ("lnx", 0)
        sqs, rss, K_SQ = lnx[1][:, 0:512], lnx[1][:, 512:1024], ("lnx", 1)
        sqsB = lnx[1][:, 0:256].bitcast(BF16)
        wu32, K_WU = lnr[0][:, :].rearrange("p (k c) -> p k c", k=8), ("lnr", 0)
        wuT, K_WT = lnr[1], ("lnr", 1)
        m12, wf32, K_M = lnx[0][:, 0:256], lnx[0][:, 256:384], ("lnx", 0)

        RAf = RA[:, :, :].rearrange("p a b -> p (a b)")
        AB = RAf.rearrange("p (t c) -> p t c", t=NT)
        ACTT = RAf[:, 0:NF * 1024].rearrange("p (f t) -> p f t", f=NF)
        VAf = RB[:, :, :].rearrange("p a b -> p (a b)")
        VA = RB[:, :, :].rearrange("p a b -> p (a b)")[:, 0:NT * NH * 65].rearrange(
            "p (t h c) -> p t h c", t=NT, h=NH)

        p = Prog(nc)
        A = p.add
        cnt = {"ev": 0, "ps": 0, "stg": 0, "trbase": 2}

        def rak(ts):
            return [("RA", t) for t in ts]

        ALLRA = rak(range(8))

        for t, d_, nm in ((idf, idf_d, "idf"), (idb, idb_d, "idb"), (ccf, ccf_d, "ccf"), (scn, scn_d, "scn")):
            A("sp", lambda e, t=t, d_=d_: e.dma_start(out=t[:], in_=d_), w=[nm], dma=("const", nm))
        A("sp", lambda e: e.dma_start(out=dgt[:, :, :], in_=dg_d), w=["dgt"], dma=("const", "dgt"))
        A("sp", lambda e: e.dma_start(out=lamt[:], in_=bass.AP(lam_p, 0, [[0, 128], [1, 256]])), w=["lamt"], dma=("const", "lamt"))
        A("dve", lambda e: e.memset(gcol[:], 0.0), w=["gcol"])
        for l in range(DEPTH):
            A("sp", lambda e, l=l: e.dma_start(out=gcol[1:65, l:l + 1], in_=subln[l:l + 1, :].rearrange("a d -> d a")),
              w=["gcol"], dma=("const", "gcol", l))
            for g in range(4):
                A("sp", lambda e, l=l, g=g: e.dma_start(out=bfc[:, l * 4 + g:l * 4 + g + 1],
                                                        in_=b_f[l, g * 128:(g + 1) * 128].rearrange("(p a) -> p a", a=1)),
                  w=["bfc"], dma=("const", "bfc", l, g))
        A("dve", lambda e: e.memset(onesf[:], 1.0), w=["onesf"])
        A("dve", lambda e: e.memset(onesm[:], 1.0 / 64.0), w=["onesm"])
        A("dve", lambda e: e.memset(onesm[0:1, :], 0.0), w=["onesm"])
        A("dve", lambda e: e.tensor_copy(out=onesmB[:], in_=onesm[:]), r=["onesm"], w=["onesmB"])
        A("dve", lambda e: e.memset(RB[:, :, :], 1.0), w=["RB"])
        for l in range(DEPTH):
            b0 = l * 128
            A("dve", lambda e, b0=b0: e.tensor_tensor(out=lamt[:, b0:b0 + 32], in0=lamt[:, b0:b0 + 32],
                                                     in1=lamt[:, b0 + 32:b0 + 64], op=ALU.mult), r=["lamt"], w=["lamt"])
            A("dve", lambda e, b0=b0: e.tensor_tensor(out=lamt[:, b0 + 64:b0 + 96], in0=lamt[:, b0 + 64:b0 + 96],
                                                     in1=lamt[:, b0 + 96:b0 + 128], op=ALU.mult), r=["lamt"], w=["lamt"])
            A("dve", lambda e, b0=b0, l=l: e.tensor_reduce(out=lamw[:, 4 * l:4 * l + 1], in_=lamt[:, b0:b0 + 32],
                                                          axis=mybir.AxisListType.X, op=ALU.add), r=["lamt"], w=["lamw"])
            A("dve", lambda e, b0=b0, l=l: e.tensor_reduce(out=lamw[:, 4 * l + 1:4 * l + 2], in_=lamt[:, b0 + 64:b0 + 96],
                                                          axis=mybir.AxisListType.X, op=ALU.add), r=["lamt", "lamw"], w=["lamw"])
            A("act", lambda e, l=l: e.activation(out=lamw[:, 4 * l + 2:4 * l + 4], in_=lamw[:, 4 * l:4 * l + 2], func=AF.Exp),
              r=["lamw"], w=["lamw"])
            A("dve", lambda e, l=l: e.scalar_tensor_tensor(out=neglam[:, l:l + 1], in0=lamw[:, 4 * l + 3:4 * l + 4],
                                                          scalar=-lam_init_fn(l), in1=lamw[:, 4 * l + 2:4 * l + 3],
                                                          op0=ALU.add, op1=ALU.subtract), r=["lamw"], w=["neglam"])
            A("dve", lambda e, l=l: e.tensor_scalar(out=gcol[0:65, l:l + 1], in0=gcol[0:65, l:l + 1],
                                                   scalar1=1.0 - lam_init_fn(l), scalar2=None, op0=ALU.mult),
              r=["gcol"], w=["gcol"])

        def load_gb(g_h, b_h, off):
            A("sp", lambda e: e.dma_start(out=gbt[:, 0, :], in_=bass.AP(g_h, off, [[0, 128], [1, 1024]])), w=["gbt"], dma="gbt")
            A("sp", lambda e: e.dma_start(out=gbt[:, 1, :], in_=bass.AP(b_h, off, [[0, 128], [1, 1024]])), w=["gbt"], dma="gbt")

        def evac(out_ap, in_ap, r, w, scale=None, bias=None, force=None):
            cnt["ev"] += 1
            if bias is not None:
                r = list(r) + ["bfc"]
            eng = force or ("act" if cnt["ev"] % 2 else "dve")
            if eng == "act":
                if bias is not None:
                    A("act", lambda e: e.activation(out=out_ap, in_=in_ap, func=AF.Identity, bias=bias), r=r, w=w)
                elif scale is not None:
                    A("act", lambda e: e.activation(out=out_ap, in_=in_ap, func=AF.Copy, scale=scale), r=r, w=w)
                else:
                    A("act", lambda e: e.activation(out=out_ap, in_=in_ap, func=AF.Copy), r=r, w=w)
            else:
                if bias is not None:
                    A("dve", lambda e: e.tensor_scalar(out=out_ap, in0=in_ap, scalar1=bias, scalar2=None, op0=ALU.add), r=r, w=w)
                elif scale is not None:
                    A("dve", lambda e: e.tensor_scalar(out=out_ap, in0=in_ap, scalar1=scale, scalar2=None, op0=ALU.mult), r=r, w=w)
                else:
                    A("dve", lambda e: e.tensor_copy(out=out_ap, in_=in_ap), r=r, w=w)

        def psnext():
            i = cnt["ps"] % 8
            cnt["ps"] += 1
            return i // 2, i % 2

        def psap(pi, b, rows=128):
            return PS[pi][0:rows, b * 512:(b + 1) * 512]

        def ln_stats(src, src_key, wk):
            st_, mv_ = lnst[wk], lnmv[wk]
            A("dve", lambda e: (e.bn_stats(out=st_[:, 0:6], in_=src[:, 0:512]),
                                e.bn_stats(out=st_[:, 6:12], in_=src[:, 512:1024]))[-1], r=[src_key], w=[("lnst", wk)])
            A("dve", lambda e: e.bn_aggr(out=mv_[:, 0:2], in_=st_[:, 0:12]), r=[("lnst", wk)], w=[("lnmv", wk)])
            A("act", lambda e: e.activation(out=mv_[:, 2:3], in_=mv_[:, 1:2], func=AF.Ln, bias=LN_EPS, scale=1.0),
              r=[("lnmv", wk)], w=[("lnmv2", wk)])
            A("act", lambda e: e.activation(out=mv_[:, 3:4], in_=mv_[:, 2:3], func=AF.Exp, scale=-0.5),
              r=[("lnmv2", wk)], w=[("lnmv3", wk)])
            A("act", lambda e: e.mul(out=mv_[:, 4:5], in_=mv_[:, 0:1], mul=-1.0), r=[("lnmv", wk)], w=[("lnmv4", wk)])
            A("act", lambda e: e.activation(out=mv_[:, 5:6], in_=mv_[:, 4:5], func=AF.Copy, scale=mv_[:, 3:4]),
              r=[("lnmv3", wk), ("lnmv4", wk)], w=[("lnmv5", wk)])

        def ln_apply_a(src, src_key, wk, tt, to_hres, to_out, hook=None):
            rt = lnr[wk]
            rk = ("lnr", wk)
            mv_ = lnmv[wk]
            A("act", lambda e: e.activation(out=rt[:], in_=src[:], func=AF.Identity, bias=mv_[:, 5:6], scale=mv_[:, 3:4]),
              r=[src_key, ("lnmv3", wk), ("lnmv5", wk)], w=[rk])
            lnp.emit_stores()
            if hook is not None:
                hook()
            A("pool", lambda e: e.tensor_tensor(out=rt[:], in0=rt[:], in1=gbt[:, 0, :], op=ALU.mult), r=[rk, "gbt"], w=[rk])
            A("pool", lambda e: e.tensor_tensor(out=rt[:], in0=rt[:], in1=gbt[:, 1, :], op=ALU.add), r=[rk, "gbt"], w=[rk])
            rows = slice(tt * 128, (tt + 1) * 128)
            if to_hres:
                lnp.stores.append(lambda: A("act", lambda e: e.dma_start(out=hres[rows, :], in_=rt[:]), r=[rk],
                                            w=[("hres", tt)], dma=("hres_w", wk)))
            if to_out:
                lnp.stores.append(lambda: A("act", lambda e: e.dma_start(out=out_d[rows, :], in_=rt[:]), r=[rk],
                                            w=[("out", tt)], dma=("out_w", wk)))

        def ln_apply_b(wk, tt):
            rt = lnr[wk]
            rk = ("lnr", wk)
            pi = cnt["trbase"] + (cnt["ps"] % 2)
            cnt["ps"] += 1
            pst = PS[pi]

            def tr(e):
                for k in range(8):
                    i = e.transpose(out=pst[:, k * 128:(k + 1) * 128], in_=rt[:, k * 128:(k + 1) * 128], identity=idf[:])
                return i
            A("pe", tr, r=[rk, "idf"], w=[("ps", pi, 0), ("ps", pi, 1)])
            t4 = tt % 4
            evac(hTs[:, :, t4 * 128:(t4 + 1) * 128], pst[:, :].rearrange("p (a b) -> p a b", a=8),
                 [("ps", pi, 0), ("ps", pi, 1)], ["hTs"], force="dve")
            if t4 == 3:
                c = tt // 4
                lnp.stores.append(lambda: A("act", lambda e: e.dma_start(
                    out=hTd[:, c * 512:(c + 1) * 512].rearrange("(kc p) t -> p kc t", p=128), in_=hTs[:, :, :]),
                    r=["hTs"], w=[("hTd", c)], dma="hTs_w"))

        class LNPipe:
            def __init__(self):
                self.q = []
                self.stores = []

            def emit_stores(self):
                while self.stores:
                    self.stores.pop(0)()

            def pre(self):
                if len(self.q) >= 2:
                    ln_apply_b(*self.q.pop(0))

            def push(self, src, src_key, wk, tt, to_hres, to_out, to_hT, hook=None):
                ln_stats(src, src_key, wk % 2)
                ln_apply_a(src, src_key, wk, tt, to_hres, to_out, hook=hook)
                if to_hT:
                    self.q.append((wk, tt))

            def flush(self):
                while self.q:
                    ln_apply_b(*self.q.pop(0))
                self.emit_stores()

        lnp = LNPipe()

        def hk(i):
            return [("hTc", i, a) for a in range(4)]

        def load_hTc(i, c):
            A("sp", lambda e: e.dma_start(out=hTc[i][:, :, :],
                                          in_=hTd[:, c * 512:(c + 1) * 512].rearrange("(kc p) t -> p kc t", p=128)),
              r=[("hTd", c)], w=hk(i), dma=("hTc", i))

        load_gb(ln_in_g, ln_in_b, 0)
        def load_x(tt):
            if tt < NT:
                A("sp", lambda e: e.dma_start(out=lnx[tt % 2][:], in_=x_d[tt * 128:(tt + 1) * 128, :]),
                  w=[("lnx", tt % 2)], dma=("lnx", tt % 2))

        def load_h(tt, lim=NT):
            if tt < lim:
                A("sp", lambda e: e.dma_start(out=lnx[tt % 2][:], in_=hres[tt * 128:(tt + 1) * 128, :]),
                  r=[("hres", tt)], w=[("lnx", tt % 2)], dma=("lnx", tt % 2))

        load_x(0)
        load_x(1)
        for tt in range(NT):
            xi = lnx[tt % 2]
            lnp.pre()
            lnp.push(xi, ("lnx", tt % 2), tt % 2, tt, True, False, True, hook=(lambda tt=tt: load_x(tt + 2)))
        lnp.flush()

        def do_layer(l):
            if stop_after == ("ln0",):
                return True
            w_in_v = w_in[l].rearrange("(kc p) n -> p kc n", p=128)
            WAB = catb
            for g in range(4):
                A("sp", lambda e, g=g: e.dma_start(out=wu32, in_=w_in_v[:, :, 1536 + g * 128:1536 + (g + 1) * 128]),
                  w=[K_WU], dma="wu32")
                A("sp", lambda e, g=g: e.dma_start(out=wf32, in_=w_f[l, g]), w=[K_M], dma="wf32")
                pi, b = psnext()
                pi2, b2 = psnext()

                def trw(e, pi=pi, b=b, pi2=pi2, b2=b2):
                    for k in range(8):
                        tgt = psap(pi, b) if k < 4 else psap(pi2, b2)
                        i = e.transpose(out=tgt[:, (k % 4) * 128:(k % 4 + 1) * 128], in_=wu32[:, k, :], identity=idf[:])
                    return i
                A("pe", trw, r=[K_WU, "idf"], w=[("ps", pi, b), ("ps", pi2, b2)])
                evac(wuT[:, 0:512], psap(pi, b), [("ps", pi, b)], [K_WT])
                evac(wuT[:, 512:1024], psap(pi2, b2), [("ps", pi2, b2)], [K_WT])
                pi, b = psnext()

                def mm12(e, pi=pi, b=b):
                    e.matmul(psap(pi, b)[:, 0:128], lhsT=ccf[:], rhs=wf32, start=True, stop=True)
                    return e.matmul(psap(pi, b)[:, 128:256], lhsT=scn[:], rhs=wf32, start=True, stop=True)
                A("pe", mm12, r=["ccf", "scn", K_M], w=[("ps", pi, b)])
                evac(m12, psap(pi, b)[:, 0:256], [("ps", pi, b)], [K_M])
                for q4 in range(4):
                    pi, b = psnext()

                    def mmab3(e, pi=pi, b=b, q4=q4):
                        i = None
                        for k2 in range(2):
                            kc = q4 * 2 + k2
                            i = e.matmul(psap(pi, b)[:, k2 * 256:(k2 + 1) * 256], lhsT=wuT[:, kc * 128:(kc + 1) * 128],
                                         rhs=m12, start=True, stop=True)
                        return i
                    A("pe", mmab3, r=[K_WT, K_M], w=[("ps", pi, b)])
                    hf = g // 2
                    c0 = (g % 2) * 256
                    evac(WAB[hf][:, q4 * 2:q4 * 2 + 2, c0:c0 + 256],
                         psap(pi, b).rearrange("p (k c) -> p k c", k=2), [("ps", pi, b)], [("catb", hf)])

            for blk in range(2):
                A("pool", lambda e, blk=blk: e.dma_start(out=wbuf[blk][:, :, :], in_=w_in_v[:, :, blk * 512:(blk + 1) * 512]),
                  w=[("wbuf", blk, 0), ("wbuf", blk, 1)], dma=("wbuf", blk))
            for tc in range(8):
                load_hTc(tc % 2, tc)
                hc = hTc[tc % 2]
                for blk in range(2):
                    for m in range(4):
                        pi, b = psnext()

                        def mmqk(e, hc=hc, blk=blk, m=m, pi=pi, b=b):
                            for kc in range(8):
                                i = e.matmul(psap(pi, b), lhsT=wbuf[blk][:, kc, m * 128:(m + 1) * 128],
                                             rhs=hc[:, kc, :], start=(kc == 0), stop=(kc == 7))
                            return i
                        A("pe", mmqk, r=[("wbuf", blk, 0), ("wbuf", blk, 1)] + hk(tc % 2), w=[("ps", pi, b)])
                        si = cnt["stg"] % 3
                        cnt["stg"] += 1
                        evac(stg[si][:, :], psap(pi, b), [("ps", pi, b)], [("stg", si)],
                             scale=(SCALE if blk == 0 else None))
                        r0 = blk * 512 + m * 128
                        A("sp", lambda e, si=si, r0=r0, tc=tc: e.dma_start(out=qkT[r0:r0 + 128, tc * 512:(tc + 1) * 512],
                                                                         in_=stg[si][:, :]),
                          r=[("stg", si)], w=[("qkT", r0 // 128)], dma=("stg_w", si))
            A("pool", lambda e: e.dma_start(out=wbuf[0][:, :, :], in_=w_in_v[:, :, 1024:1536]),
              w=[("wbuf", 0, 0), ("wbuf", 0, 1)], dma=("wbuf", 0))
            for tc in range(8):
                load_hTc(tc % 2, tc)
                hc = hTc[tc % 2]
                for t4 in range(4):
                    tt = tc * 4 + t4
                    pi, b = psnext()

                    def mmv(e, hc=hc, t4=t4, pi=pi, b=b):
                        for kc in range(8):
                            i = e.matmul(psap(pi, b), lhsT=hc[:, kc, t4 * 128:(t4 + 1) * 128], rhs=wbuf[0][:, kc, :],
                                         start=(kc == 0), stop=(kc == 7))
                        return i
                    A("pe", mmv, r=[("wbuf", 0, 0), ("wbuf", 0, 1)] + hk(tc % 2), w=[("ps", pi, b)])
                    evac(VA[:, tt, :, 1:65], psap(pi, b).rearrange("p (h c) -> p h c", h=NH), [("ps", pi, b)], ["RB"])
                    for hf in range(2):
                        pi, b = psnext()

                        def mmab4(e, hc=hc, t4=t4, hf=hf, pi=pi, b=b):
                            for kc in range(8):
                                i = e.matmul(psap(pi, b), lhsT=hc[:, kc, t4 * 128:(t4 + 1) * 128], rhs=WAB[hf][:, kc, :],
                                             start=(kc == 0), stop=(kc == 7))
                            return i
                        A("pe", mmab4, r=[("catb", hf)] + hk(tc % 2), w=[("ps", pi, b)])
                        evac(AB[:, tt, hf * 512:(hf + 1) * 512], psap(pi, b), [("ps", pi, b)], [("RA", tt // 4)])
            for i_ in range(16):
                lo_, hi_ = AB[:, i_, :], AB[:, i_ + 16, :]
                A("dve", lambda e, lo_=lo_, hi_=hi_: e.tensor_tensor(out=hi_, in0=lo_, in1=hi_, op=ALU.subtract),
                  r=[("RA", i_ // 4), ("RA", (i_ + 16) // 4)], w=[("RA", (i_ + 16) // 4)])
                A("dve", lambda e, lo_=lo_, hi_=hi_: e.scalar_tensor_tensor(out=lo_, in0=lo_, scalar=2.0, in1=hi_,
                                                                           op0=ALU.mult, op1=ALU.subtract),
                  r=[("RA", i_ // 4), ("RA", (i_ + 16) // 4)], w=[("RA", i_ // 4)])
            if stop_after == ("A", l):
                return True

            ti = 0
            for tc in range(4):
                accE = [psnext() for _ in range(4)]
                accO = [psnext() for _ in range(4)]
                for st_ in range(NT):
                    sl_ = ti % 8
                    ti += 1
                    hb_, a_ = hTc[sl_ // 4], sl_ % 4
                    tk = ("hTc", sl_ // 4, a_)
                    tcs, tss = hb_[:, 2 * a_, :], hb_[:, 2 * a_ + 1, :]
                    A("sp", lambda e, hb_=hb_, a_=a_, st_=st_, tc=tc: e.dma_start(
                        out=hb_[:, 2 * a_:2 * a_ + 2, :], in_=cst_d[st_ * 128:(st_ + 1) * 128, tc, :, :]), w=[tk], dma=("tab", sl_))
                    accs = accE if st_ < 16 else accO

                    def mmf(e, tcs=tcs, tss=tss, st_=st_, accs=accs):
                        for g in range(4):
                            pi, b = accs[g]
                            e.matmul(psap(pi, b), lhsT=AB[:, st_, g * 256:g * 256 + 128], rhs=tcs,
                                     start=(st_ % 16 == 0), stop=False)
                            i = e.matmul(psap(pi, b), lhsT=AB[:, st_, g * 256 + 128:g * 256 + 256], rhs=tss,
                                         start=False, stop=(st_ % 16 == 15))
                        return i
                    A("pe", mmf, r=[tk, ("RA", st_ // 4)], w=[("ps",) + a_ for a_ in accs])
                for g in range(4):
                    pb_ = ptb[g % 3]
                    pkey = ("ptb", g % 3)
                    for par, acc in ((0, accE[g]), (1, accO[g])):
                        pi, b = acc
                        evac(pb_[:, par:1024:2], psap(pi, b), [("ps", pi, b)], [pkey], bias=bfc[:, l * 4 + g:l * 4 + g + 1])
                    r0 = 512 + g * 128
                    A("sp", lambda e, pb_=pb_, r0=r0, tc=tc: e.dma_start(out=catT[r0:r0 + 128, tc * 1024:(tc + 1) * 1024],
                                                                       in_=pb_[:, :]),
                      r=[pkey], w=[("catT", r0 // 128, 2 * tc), ("catT", r0 // 128, 2 * tc + 1)], dma=("ptb_w", g % 3))
            if stop_after == ("B2", l):
                return True

            deferred = []

            def run_deferred(n=1):
                for _ in range(n):
                    if deferred:
                        deferred.pop(0)()

            def _dmin(qc, j):
                if j * 128 + 127 < qc * 512:
                    return qc * 512 - (j * 128 + 127)
                if j * 128 > qc * 512 + 511:
                    return j * 128 - (qc * 512 + 511)
                return 0
            blocks = []
            gidx = 0
            for h in range(NH):
                for qc in range(8):
                    js = [j for j in range(NT) if (2.0 ** -(h + 1)) * _dmin(qc, j) <= SKIP_BIAS]
                    for j in js:
                        blocks.append((h, qc, j, gidx, j == js[0], j == js[-1]))
                    gidx += 1

            def aug(hh, t):
                return RA[:, (hh % 2) * 4 + t, :]

            def load_head(h):
                s_ = h % 2
                for c in range(2):
                    A("sp", lambda e, c=c: e.dma_start(out=aug(h, 0)[c * 64:c * 64 + 32, :],
                                                       in_=qkT[h * 64 + c * 32:h * 64 + c * 32 + 32, :]),
                      r=[("qkT", (h * 64) // 128)], w=rak([s_ * 4 + 0]), dma=("aug", s_, 0))
                    A("sp", lambda e, c=c: e.dma_start(out=aug(h, 0)[c * 64 + 32:c * 64 + 64, :], in_=qx_d[h]),
                      w=rak([s_ * 4 + 0]), dma=("aug", s_, 0))
                    for t, nm in ((1, "lo"), (2, "up"), (3, "dg")):
                        A("sp", lambda e, c=c, t=t: e.dma_start(out=aug(h, t)[c * 64:c * 64 + 32, :],
                                                                in_=qkT[512 + h * 64 + c * 32:512 + h * 64 + c * 32 + 32, :]),
                          r=[("qkT", (512 + h * 64) // 128)], w=rak([s_ * 4 + t]), dma=("aug", s_, t))
                        A("sp", lambda e, c=c, t=t, nm=nm: e.dma_start(out=aug(h, t)[c * 64 + 32:c * 64 + 64, :],
                                                                      in_=kx_d[nm][h]),
                          w=rak([s_ * 4 + t]), dma=("aug", s_, t))

            def emit_S(bi):
                h, qc, j = blocks[bi][0:3]
                si = bi % 2
                s_ = h % 2
                q0 = qc * 512
                Q = aug(h, 0)
                rel = j - 4 * qc

                def mms(e):
                    i = None
                    for c in range(2):
                        pr = slice(c * 64, c * 64 + 64)
                        ob = PS[si][:, c * 512:(c + 1) * 512]
                        kc_ = slice(j * 128, (j + 1) * 128)
                        if rel < 0 or rel > 3:
                            K = aug(h, 1 if rel < 0 else 2)
                            i = e.matmul(ob, lhsT=K[pr, kc_], rhs=Q[pr, q0:q0 + 512], start=True, stop=True)
                        else:
                            lo_c = 128 * rel
                            if lo_c > 0:
                                e.matmul(ob[:, 0:lo_c], lhsT=aug(h, 2)[pr, kc_], rhs=Q[pr, q0:q0 + lo_c],
                                         start=True, stop=True, skip_group_check=True)
                            e.matmul(ob[:, lo_c:lo_c + 128], lhsT=aug(h, 3)[pr, kc_], rhs=Q[pr, q0 + lo_c:q0 + lo_c + 128],
                                     start=True, stop=False, skip_group_check=True)
                            i = e.matmul(ob[:, lo_c:lo_c + 128], lhsT=idb[:, :], rhs=dgt[:, h, :], start=False, stop=True,
                                         skip_group_check=True)
                            if lo_c + 128 < 512:
                                i = e.matmul(ob[:, lo_c + 128:512], lhsT=aug(h, 1)[pr, kc_],
                                             rhs=Q[pr, q0 + lo_c + 128:q0 + 512], start=True, stop=True, skip_group_check=True)
                    return i
                A("pe", mms, r=rak([s_ * 4 + t for t in range(4)]) + ["idb", "dgt"], w=[("ps", si, 0), ("ps", si, 1)])

            def emit_EXP(bi):
                si = bi % 2
                pt = ptb[bi % 3]
                A("act", lambda e: e.activation(out=pt[:, :], in_=PS[si][:, :], func=AF.Exp),
                  r=[("ps", si, 0), ("ps", si, 1)], w=[("ptb", bi % 3)])

            def emit_PV(bi):
                h, qc, j, g_, first, last_ = blocks[bi]
                gi = g_ % 2
                pt = ptb[bi % 3]

                def mmpv(e):
                    for c in range(2):
                        i = e.matmul(PS[2 + gi][0:65, c * 512:(c + 1) * 512], lhsT=VA[:, j, h, :],
                                     rhs=pt[:, c * 512:(c + 1) * 512], start=first, stop=last_)
                    return i
                A("pe", mmpv, r=[("ptb", bi % 3), "RB"], w=[("ps", 2 + gi, 0), ("ps", 2 + gi, 1)])

            def post(h, qc, gi):
                pv = PS[2 + gi]
                pk = [("ps", 2 + gi, 0), ("ps", 2 + gi, 1)]

                def s1():
                    A("dve", lambda e: e.tensor_copy(out=ocs[0:65, :], in_=pv[0:65, :]), r=pk, w=[K_OCS])

                def s1b():
                    def trs(e):
                        for c8 in range(8):
                            i = e.transpose(out=pv[:, c8:c8 + 1], in_=ocs[0:1, c8 * 128:(c8 + 1) * 128], identity=idf[0:1, 0:1])
                        return i
                    A("pe", trs, r=[K_OCS, "idf"], w=[pk[0]])
                    A("dve", lambda e: e.reciprocal(out=Rt, in_=pv[:, 0:8]), r=[pk[0]], w=[K_RCS])
                    A("dve", lambda e: e.tensor_scalar(out=Rt[:, 4:8], in0=Rt[:, 4:8], scalar1=neglam[:, l:l + 1],
                                                       scalar2=None, op0=ALU.mult), r=[K_RCS, "neglam"], w=[K_RCS])
                    A("dve", lambda e: e.tensor_copy(out=Rb65, in_=Rt_b), r=[K_RCS], w=[K_RCS])

                def s2():
                    def mmb(e):
                        for c8 in range(8):
                            i = e.matmul(pv[0:65, c8 * 128:(c8 + 1) * 128], lhsT=Rb65[:, c8, :], rhs=idf[:, :],
                                         start=True, stop=True, skip_group_check=True)
                        return i
                    A("pe", mmb, r=[K_RCS, "idf"], w=pk)
                    A("dve", lambda e: e.tensor_tensor(out=t0s[0:65, :], in0=ocs[0:65, 0:512], in1=pv[0:65, 0:512], op=ALU.mult),
                      r=[K_OCS, pk[0]], w=[K_T])
                    A("dve", lambda e: e.tensor_tensor(out=t1s[0:65, :], in0=ocs[0:65, 512:1024], in1=pv[0:65, 512:1024],
                                                       op=ALU.mult), r=[K_OCS, pk[1]], w=[K_T])
                    A("dve", lambda e: e.tensor_tensor(out=t0s[0:65, :], in0=t0s[0:65, :], in1=t1s[0:65, :], op=ALU.add),
                      r=[K_T], w=[K_T])
                    A("dve", lambda e: e.tensor_tensor(out=sqsB[0:65, :], in0=t0s[0:65, :], in1=t0s[0:65, :], op=ALU.mult),
                      r=[K_T], w=[K_SQ])

                def s3():
                    A("pe", lambda e: e.matmul(pv[0:65, 0:512], lhsT=onesmB[0:65, 0:65], rhs=sqsB[0:65, :], start=True, stop=True),
                      r=[K_SQ, "onesmB"], w=[pk[0]])
                    A("act", lambda e: e.activation(out=rss[0:65, :], in_=pv[0:65, 0:512], func=AF.Ln, bias=SUBLN_EPS, scale=1.0),
                      r=[pk[0]], w=[K_SQ])
                    A("act", lambda e: e.activation(out=rss[0:65, :], in_=rss[0:65, :], func=AF.Exp, scale=-0.5),
                      r=[K_SQ], w=[K_SQ])
                    si = cnt["stg"] % 3
                    cnt["stg"] += 1
                    A("dve", lambda e: e.scalar_tensor_tensor(out=stg[si][0:65, :], in0=t0s[0:65, :],
                                                              scalar=gcol[0:65, l:l + 1], in1=rss[0:65, :],
                                                              op0=ALU.mult, op1=ALU.mult),
                      r=[K_T, K_SQ, "gcol"], w=[("stg", si)])
                    A("sp", lambda e: e.dma_start(out=catT[h * 64:(h + 1) * 64, qc * 512:(qc + 1) * 512], in_=stg[si][1:65, :]),
                      r=[("stg", si)], w=[("catT", "a", h, qc)], dma=("stg_w", si))
                return s1, [(2, s1b), (5, s2), (10, s3)]

            nb = len(blocks)
            load_head(0)
            emit_S(0)
            emit_S(1)
            pos = 0
            for bi in range(nb):
                h, qc, j, g_, first, last_ = blocks[bi]
                if first:
                    pos = 0
                if first and qc == 0 and h + 1 < NH:
                    load_head(h + 1)
                emit_EXP(bi)
                if bi + 2 < nb:
                    emit_S(bi + 2)
                emit_PV(bi)
                while deferred and deferred[0][0] <= pos:
                    deferred.pop(0)[1]()
                pos += 1
                if last_:
                    while deferred:
                        deferred.pop(0)[1]()
                    s1_, rest = post(h, qc, g_ % 2)
                    s1_()
                    deferred.extend(rest)
            while deferred:
                deferred.pop(0)[1]()
            if stop_after == ("B", l):
                return True

            w_o_v = w_o[l].rearrange("(kc p) n -> p kc n", p=128)
            for i in range(2):
                A("pool", lambda e, i=i: e.dma_start(out=wbuf[i][:, :, :], in_=w_o_v[:, :, i * 512:(i + 1) * 512]),
                  w=[("wbuf", i, 0), ("wbuf", i, 1)], dma=("wbuf", i))
            load_gb(ln1_g, ln1_b, l * D)
            cat_keys = [("catT", rr, tc_) for rr in range(4, 8) for tc_ in range(8)] + \
                       [("catT", "a", hh, qq) for hh in range(NH) for qq in range(8)]
            def load_cat(k):
                if k < 8:
                    A("sp", lambda e: e.dma_start(
                        out=catb[k % 2][:, :, :], in_=catT[:, k * 512:(k + 1) * 512].rearrange("(kc p) t -> p kc t", p=128)),
                      r=cat_keys, w=[("catb", k % 2)], dma=("catb", k % 2))

            load_cat(0)
            for tt in range(NT):
                if tt % 4 == 0:
                    cb = catb[(tt // 4) % 2]
                    ck = ("catb", (tt // 4) % 2)
                    load_cat(tt // 4 + 1)
                xi = lnx[tt % 2]
                if tt == 0:
                    load_h(0)
                    load_h(1)
                lnp.pre()
                pi = cnt["ps"] % 2
                cnt["ps"] += 1

                def mmo(e, cb=cb, tt=tt, pi=pi):
                    for cc in range(2):
                        for kc in range(8):
                            i = e.matmul(PS[pi][:, cc * 512:(cc + 1) * 512], lhsT=cb[:, kc, (tt % 4) * 128:(tt % 4 + 1) * 128],
                                         rhs=wbuf[cc][:, kc, :], start=(kc == 0), stop=(kc == 7))
                    return i
                A("pe", mmo, r=[ck] + [("wbuf", i, j_) for i in range(2) for j_ in range(2)], w=[("ps", pi, 0), ("ps", pi, 1)])
                A("dve", lambda e, xi=xi, pi=pi: e.scalar_tensor_tensor(out=xi[:], in0=xi[:], scalar=ALPHA, in1=PS[pi][:, :],
                                                                       op0=ALU.mult, op1=ALU.add),
                  r=[("lnx", tt % 2), ("ps", pi, 0), ("ps", pi, 1)], w=[("lnx", tt % 2)])
                lnp.push(xi, ("lnx", tt % 2), tt % 2, tt, True, False, True, hook=(lambda tt=tt: load_h(tt + 2)))
            lnp.flush()
            if stop_after == ("C", l):
                return True

            A("pool", lambda e: e.dma_start(out=RB[:, :, :], in_=w_dn[l].rearrange("(f p) n -> p f n", p=128)),
              w=["RB"], dma="RB")
            load_gb(ln2_g, ln2_b, l * D)
            cnt["trbase"] = 0
            w_gu_v = w_gu[l].rearrange("(kc p) n -> p kc n", p=128)
            last = (l == DEPTH - 1)
            gi_ = 0
            for tb in range(4):
                load_hTc(0, 2 * tb)
                load_hTc(1, 2 * tb + 1)
                for f in range(NF):
                    ws = gi_ % 4
                    gi_ += 1
                    wt_ = wbuf[ws // 2]
                    o_ = (ws % 2) * 256
                    wk_ = ("wbuf", ws // 2, ws % 2)
                    A("pool", lambda e, wt_=wt_, o_=o_, f=f: e.dma_start(out=wt_[:, :, o_:o_ + 128],
                                                                        in_=w_gu_v[:, :, f * 128:(f + 1) * 128]),
                      w=[wk_], dma=wk_)
                    A("pool", lambda e, wt_=wt_, o_=o_, f=f: e.dma_start(out=wt_[:, :, o_ + 128:o_ + 256],
                                                                        in_=w_gu_v[:, :, DFF + f * 128:DFF + (f + 1) * 128]),
                      w=[wk_], dma=wk_)
                    for tc in range(2):
                        pi = cnt["ps"] % 2
                        cnt["ps"] += 1

                        def mmgu(e, wt_=wt_, o_=o_, tc=tc, pi=pi):
                            for u_ in range(2):
                                for kc in range(8):
                                    i = e.matmul(PS[pi][:, u_ * 512:(u_ + 1) * 512],
                                                 lhsT=wt_[:, kc, o_ + u_ * 128:o_ + (u_ + 1) * 128],
                                                 rhs=hTc[tc][:, kc, :], start=(kc == 0), stop=(kc == 7))
                            return i
                        A("pe", mmgu, r=[wk_] + hk(tc), w=[("ps", pi, 0), ("ps", pi, 1)])
                        sgt = sgs[pi]
                        A("act", lambda e, sgt=sgt, pi=pi: e.activation(out=sgt[:, :], in_=PS[pi][:, 0:512], func=AF.Silu),
                          r=[("ps", pi, 0)], w=[("sgs", pi)])
                        A("dve", lambda e, sgt=sgt, pi=pi, f=f, tc=tc: e.tensor_tensor(
                            out=ACTT[:, f, tc * 512:(tc + 1) * 512], in0=sgt[:, :], in1=PS[pi][:, 512:1024], op=ALU.mult),
                          r=[("sgs", pi), ("ps", pi, 1)], w=[("RA", f // 4)])
                for t8 in range(8):
                    tt = tb * 8 + t8
                    xi = lnx[tt % 2]
                    if t8 == 0:
                        load_h(tt)
                        load_h(tt + 1)
                    lnp.pre()
                    pi = 2 + (cnt["ps"] % 2)
                    cnt["ps"] += 1

                    def mmd(e, t8=t8, pi=pi):
                        for cc in range(2):
                            for f in range(NF):
                                i = e.matmul(PS[pi][:, cc * 512:(cc + 1) * 512], lhsT=ACTT[:, f, t8 * 128:(t8 + 1) * 128],
                                             rhs=RB[:, f, cc * 512:(cc + 1) * 512], start=(f == 0), stop=(f == NF - 1))
                        return i
                    A("pe", mmd, r=rak(range(6)) + ["RB"], w=[("ps", pi, 0), ("ps", pi, 1)])
                    A("dve", lambda e, xi=xi, pi=pi: e.scalar_tensor_tensor(out=xi[:], in0=xi[:], scalar=ALPHA, in1=PS[pi][:, :],
                                                                           op0=ALU.mult, op1=ALU.add),
                      r=[("lnx", tt % 2), ("ps", pi, 0), ("ps", pi, 1)], w=[("lnx", tt % 2)])
                    lnp.push(xi, ("lnx", tt % 2), tt % 2, tt, not last, last, not last,
                             hook=(lambda tt=tt, tb=tb: load_h(tt + 2, (tb + 1) * 8)))
                lnp.flush()
            cnt["trbase"] = 2
            if not last:
                A("dve", lambda e: e.memset(RB[:, :, :], 1.0), w=["RB"])
            if stop_after == ("D", l):
                return True

            return False

        for l_ in range(DEPTH):
            if do_layer(l_):
                break

        fin_keys = [k for k in p.res.keys() if isinstance(k, tuple) and k[0] in ("out", "hres", "qkT", "catT", "hTd")]
        A("sp", lambda e: None, r=fin_keys)
        p.emit()
    return nc


_NC = {}


def _get_nc():
    if "nc" not in _NC:
        _NC["nc"] = build()
    return _NC["nc"]


def _in_maps(inputs):
    c = _consts()
    f32 = lambda a: np.ascontiguousarray(np.asarray(a, dtype=np.float32))
    shared = {
        "ln_in_g": f32(inputs["ln_in_g"]), "ln_in_b": f32(inputs["ln_in_b"]),
        "w_in": f32(inputs["w_in"]), "lam_params": f32(inputs["lam_params"]).reshape(-1),
        "subln_g": f32(inputs["subln_g"]), "w_f": f32(inputs["w_f"]), "b_f": f32(inputs["b_f"]),
        "w_o": f32(inputs["w_o"]), "ln1_g": f32(inputs["ln1_g"]).reshape(-1), "ln1_b": f32(inputs["ln1_b"]).reshape(-1),
        "w_gu": f32(inputs["w_gu"]), "w_down": f32(inputs["w_down"]),
        "ln2_g": f32(inputs["ln2_g"]).reshape(-1), "ln2_b": f32(inputs["ln2_b"]).reshape(-1),
    }
    shared.update(c)
    x = f32(inputs["x"])
    return [dict(shared, x=x[b]) for b in range(x.shape[0])]


def kernel(**inputs):
    nc = _get_nc()
    maps = _in_maps(inputs)
    res = run_bass_kernel_spmd(nc, maps, core_ids=list(range(8)))
    return np.stack([np.asarray(r["out"], dtype=np.float32) for r in res.results], axis=0)
```

```python
import math
import contextlib
import numpy as np
import ml_dtypes
import concourse.bass as bass
import concourse.mybir as mybir
from concourse.bass_utils import run_bass_kernel_spmd

F32 = mybir.dt.float32
BF16 = mybir.dt.bfloat16
F32R = mybir.dt.float32r
AF = mybir.ActivationFunctionType
ALU = mybir.AluOpType

S = 4096
D = 1024
NT = 32
DEPTH = 2
DFF = 2816
NF = 22
NH = 8
ALPHA = (2.0 * DEPTH) ** 0.25
SCALE = 32 ** -0.5
LN_EPS = 1e-5
SUBLN_EPS = 1e-5
SKIP_BIAS = 48.0
SAME_ENGINE_SYNC = True
STATS = {}


def lam_init_fn(l):
    return 0.8 - 0.6 * math.exp(-0.3 * l)


class Op:
    __slots__ = ("eng", "fn", "dma_key", "dma_cnt", "signal", "sig_idx", "waits")

    def __init__(self, eng, fn, dma_key):
        self.eng = eng
        self.fn = fn
        self.dma_key = dma_key
        self.dma_cnt = 0
        self.signal = False
        self.sig_idx = 0
        self.waits = []


class Prog:
    ENGS = ("pe", "act", "dve", "pool", "sp")

    def __init__(self, nc):
        self.nc = nc
        self.streams = {e: [] for e in self.ENGS}
        self.res = {}
        self.dma_counts = {}

    def add(self, eng, fn, r=(), w=(), dma=None):
        op = Op(eng, fn, dma)
        if dma is not None:
            c = self.dma_counts.get(dma, 0) + 1
            self.dma_counts[dma] = c
            op.dma_cnt = c
        deps = []
        raw = set()
        for k in r:
            st = self.res.get(k)
            if st is None:
                st = [None, []]
                self.res[k] = st
            if st[0] is not None:
                deps.append(st[0])
                raw.add(id(st[0]))
            st[1].append(op)
        for k in w:
            st = self.res.get(k)
            if st is not None:
                if st[0] is not None:
                    deps.append(st[0])
                for rd in st[1]:
                    if rd is not op:
                        deps.append(rd)
            self.res[k] = [op, []]
        seen = set()
        for d in deps:
            if id(d) in seen or d is op:
                continue
            seen.add(id(d))
            if d.dma_key is None:
                if d.eng == eng and (eng == "pe" or not SAME_ENGINE_SYNC):
                    continue
                if d.eng == eng and id(d) not in raw:
                    STATS["skipped"] = STATS.get("skipped", 0) + 1
                    continue
                if d.eng == eng:
                    STATS["self_raw"] = STATS.get("self_raw", 0) + 1
                d.signal = True
            op.waits.append(d)
        self.streams[eng].append(op)
        return op

    def emit(self):
        nc = self.nc
        with contextlib.ExitStack() as es:
            esem = {e: es.enter_context(nc.semaphore("s_" + e)) for e in ("pe", "act", "dve", "pool")}
            dsem = {k: es.enter_context(nc.semaphore("d_%d" % i)) for i, k in enumerate(self.dma_counts)}
            for e, st in self.streams.items():
                c = 0
                for op in st:
                    if op.dma_key is None and op.signal:
                        c += 1
                        op.sig_idx = c
            block = es.enter_context(nc.Block())
            handles = {"pe": block.tensor, "act": block.scalar, "dve": block.vector,
                       "pool": block.gpsimd, "sp": block.sync}

            def mk(ename):
                st = self.streams[ename]

                def body(eng):
                    waited = {}
                    for op in st:
                        for d in op.waits:
                            if d.dma_key is not None:
                                key = ("d", d.dma_key)
                                val = d.dma_cnt * 16
                                sem = dsem[d.dma_key]
                            else:
                                key = ("e", d.eng)
                                val = d.sig_idx
                                sem = esem[d.eng]
                            if waited.get(key, 0) >= val:
                                continue
                            waited[key] = val
                            eng.wait_ge(sem, val)
                        inst = op.fn(eng)
                        if inst is None:
                            continue
                        if op.dma_key is not None:
                            inst.then_inc(dsem[op.dma_key], 16)
                        elif op.signal:
                            inst.then_inc(esem[op.eng], 1)

                return body

            for ename in self.ENGS:
                if self.streams[ename]:
                    handles[ename](mk(ename))


_CONSTS = None


def _consts():
    global _CONSTS
    if _CONSTS is not None:
        return _CONSTS
    bf = ml_dtypes.bfloat16
    c = {}
    c["idf"] = np.eye(128, dtype=np.float32)
    c["idb"] = np.eye(128, dtype=np.float32).astype(bf)
    ab = np.outer(np.arange(128), np.arange(128)) % 128
    ang = 2.0 * np.pi * ab / 128.0
    c["ccf"] = (np.cos(ang) / math.sqrt(128.0)).astype(np.float32)
    c["scn"] = (-np.sin(ang) / math.sqrt(128.0)).astype(np.float32)
    k = np.arange(S, dtype=np.int64)
    tab_c = (np.cos(2.0 * np.pi * k / S) / 64.0).astype(np.float32).astype(bf)
    tab_s = (np.sin(2.0 * np.pi * k / S) / 64.0).astype(np.float32).astype(bf)
    half = k[:S // 2]
    st = np.concatenate([np.outer(half, 2 * half), np.outer(half, 2 * half + 1)], axis=0)
    st = (st % S).astype(np.int32)
    c["cst"] = np.ascontiguousarray(np.stack([tab_c[st].reshape(S, 4, 512), tab_s[st].reshape(S, 4, 512)], axis=2))
    pos = np.arange(S)
    pa = (pos // 64).astype(np.float32) * 64.0
    pb = (pos % 64).astype(np.float32)
    qx = np.zeros((NH, 32, S), np.float32)
    klo = np.zeros((NH, 32, S), np.float32)
    for h in range(NH):
        sl = 2.0 ** -(h + 1)
        qx[h, 1] = -sl * pa
        qx[h, 2] = -sl * pb
        qx[h, 3] = 1.0
        qx[h, 4] = 1.0
        klo[h, 1] = 1.0
        klo[h, 2] = 1.0
        klo[h, 3] = sl * pa
        klo[h, 4] = sl * pb
    c["qx"] = qx.astype(bf)
    c["kxlo"] = klo.astype(bf)
    c["kxup"] = (-klo).astype(bf)
    c["kxdg"] = np.zeros((NH, 32, S), np.float32).astype(bf)
    dg = np.zeros((128, NH, 128), np.float32)
    pf = np.abs(np.arange(128)[None, :] - np.arange(128)[:, None]).astype(np.float32)
    for h in range(NH):
        dg[:, h, :] = -(2.0 ** -(h + 1)) * pf
    c["dg"] = dg.astype(bf)
    _CONSTS = c
    return c


def build(dbg=None, stop_after=None):
    dbg = dbg or set()
    nc = bass.Bass("TRN2", target_bir_lowering=False)

    def din(name, shape, dtype):
        return nc.dram_tensor(name, shape, dtype, kind="ExternalInput")

    def dscr(name, shape, dtype):
        return nc.dram_tensor(name, shape, dtype, kind="ExternalOutput" if name in dbg else "Internal")

    x_d = din("x", [S, D], F32).ap()
    ln_in_g = din("ln_in_g", [D], F32)
    ln_in_b = din("ln_in_b", [D], F32)
    w_in = din("w_in", [DEPTH, D, 2048], F32).ap()
    lam_p = din("lam_params", [DEPTH * 4 * 32], F32)
    subln = din("subln_g", [DEPTH, 64], F32).ap()
    w_f = din("w_f", [DEPTH, 4, 128, 128], F32).ap()
    b_f = din("b_f", [DEPTH, 512], F32).ap()
    w_o = din("w_o", [DEPTH, D, D], F32).ap()
    ln1_g = din("ln1_g", [DEPTH * D], F32)
    ln1_b = din("ln1_b", [DEPTH * D], F32)
    w_gu = din("w_gu", [DEPTH, D, 2 * DFF], F32).ap()
    w_dn = din("w_down", [DEPTH, DFF, D], F32).ap()
    ln2_g = din("ln2_g", [DEPTH * D], F32)
    ln2_b = din("ln2_b", [DEPTH * D], F32)
    idf_d = din("idf", [128, 128], F32).ap()
    idb_d = din("idb", [128, 128], BF16).ap()
    ccf_d = din("ccf", [128, 128], F32).ap()
    scn_d = din("scn", [128, 128], F32).ap()
    cst_d = din("cst", [S, 4, 2, 512], BF16).ap()
    qx_d = din("qx", [NH, 32, S], BF16).ap()
    kx_d = {"lo": din("kxlo", [NH, 32, S], BF16).ap(), "up": din("kxup", [NH, 32, S], BF16).ap(),
            "dg": din("kxdg", [NH, 32, S], BF16).ap()}
    dg_d = din("dg", [128, NH, 128], BF16).ap()
    out_d = nc.dram_tensor("out", [S, D], F32, kind="ExternalOutput").ap()

    hres = dscr("hres", [S, D], F32).ap()
    hTd = dscr("hTd", [1024, S], BF16).ap()
    qkT = dscr("qkT", [1024, S], BF16).ap()
    catT = dscr("catT", [1024, S], BF16).ap()

    with contextlib.ExitStack() as es:
        E = es.enter_context

        def sb(name, shape, dtype):
            return E(nc.sbuf_tensor("sb_" + name, shape, dtype))

        RA = sb("RA", [128, 8, S], BF16)
        RB = sb("RB", [128, NF, 1024], BF16)
        PS = [E(nc.psum_tensor("ps%d" % i, [128, 1024], F32)) for i in range(4)]
        idf = sb("idf", [128, 128], F32)
        idb = sb("idb", [128, 128], BF16)
        ccf = sb("ccf", [128, 128], F32)
        scn = sb("scn", [128, 128], F32)
        dgt = sb("dgt", [128, NH, 128], BF16)
        onesf = sb("onesf", [128, 72], F32)
        onesm = sb("onesm", [128, 72], F32)
        onesmB = sb("onesmB", [128, 72], BF16)
        gbt = sb("gbt", [128, 2, 1024], F32)
        lamt = sb("lamt", [128, 256], F32)
        lamw = sb("lamw", [128, 16], F32)
        neglam = sb("neglam", [128, DEPTH], F32)
        gcol = sb("gcol", [128, DEPTH], F32)
        bfc = sb("bfc", [128, DEPTH * 4], F32)
        wbuf = [sb("wbuf%d" % i, [128, 8, 512], BF16) for i in range(2)]
        catb = [sb("catb%d" % i, [128, 8, 512], BF16) for i in range(2)]
        hTc = [sb("hTc%d" % i, [128, 8, 512], BF16) for i in range(2)]
        hTs = sb("hTs", [128, 8, 512], BF16)
        stg = [sb("stg%d" % i, [128, 512], BF16) for i in range(3)]
        lnx = [sb("lnx%d" % i, [128, 1024], F32) for i in range(2)]
        lnr = [sb("lnr%d" % i, [128, 1024], F32) for i in range(2)]
        lnst = [sb("lnst%d" % i, [128, 12], F32) for i in range(2)]
        lnmv = [sb("lnmv%d" % i, [128, 8], F32) for i in range(2)]
        ptb = [sb("ptb%d" % i, [128, 1024], BF16) for i in range(3)]
        sgs = [sb("sgs%d" % i, [128, 512], F32) for i in range(2)]

        ocs, K_OCS = lnr[0], ("lnr", 0)
        rcs, K_RCS = lnr[1], ("lnr", 1)
        Rt = lnr[1][:, 1016:1024]
        Rt_b = bass.AP(lnr[1], 1016, [[1024, 128], [1, 8], [0, 65]])
        Rb65 = lnr[1][:, 0:520].rearrange("p (c m) -> p c m", c=8)
        t0s, t1s, K_T = lnx[0][:, 0:512], lnx[0][:, 512:1024], ("lnx", 0)
        sqs, rss, K_SQ = lnx[1][:, 0:512], lnx[1][:, 512:1024], ("lnx", 1)
        sqsB = lnx[1][:, 0:256].bitcast(BF16)
        wu32, K_WU = lnr[0][:, :].rearrange("p (k c) -> p k c", k=8), ("lnr", 0)
        wuT, K_WT = lnr[1], ("lnr", 1)
        m12, wf32, K_M = lnx[0][:, 0:256], lnx[0][:, 256:384], ("lnx", 0)

        RAf = RA[:, :, :].rearrange("p a b -> p (a b)")
        AB = RAf.rearrange("p (t c) -> p t c", t=NT)
        ACTT = RAf[:, 0:NF * 1024].rearrange("p (f t) -> p f t", f=NF)
        VAf = RB[:, :, :].rearrange("p a b -> p (a b)")
        VA = RB[:, :, :].rearrange("p a b -> p (a b)")[:, 0:NT * NH * 65].rearrange(
            "p (t h c) -> p t h c", t=NT, h=NH)

        p = Prog(nc)
        A = p.add
        cnt = {"ev": 0, "ps": 0, "stg": 0, "trbase": 2, "ps01": 0}

        def rak(ts):
            return [("RA", t) for t in ts]

        ALLRA = rak(range(8))

        for t, d_, nm in ((idf, idf_d, "idf"), (idb, idb_d, "idb"), (ccf, ccf_d, "ccf"), (scn, scn_d, "scn")):
            A("sp", lambda e, t=t, d_=d_: e.dma_start(out=t[:], in_=d_), w=[nm], dma=("const", nm))
        A("sp", lambda e: e.dma_start(out=dgt[:, :, :], in_=dg_d), w=["dgt"], dma=("const", "dgt"))
        A("sp", lambda e: e.dma_start(out=lamt[:], in_=bass.AP(lam_p, 0, [[0, 128], [1, 256]])), w=["lamt"], dma=("const", "lamt"))
        A("dve", lambda e: e.memset(gcol[:], 0.0), w=["gcol"])
        for l in range(DEPTH):
            A("sp", lambda e, l=l: e.dma_start(out=gcol[1:65, l:l + 1], in_=subln[l:l + 1, :].rearrange("a d -> d a")),
              w=["gcol"], dma=("const", "gcol", l))
            for g in range(4):
                A("sp", lambda e, l=l, g=g: e.dma_start(out=bfc[:, l * 4 + g:l * 4 + g + 1],
                                                        in_=b_f[l, g * 128:(g + 1) * 128].rearrange("(p a) -> p a", a=1)),
                  w=["bfc"], dma=("const", "bfc", l, g))
        A("dve", lambda e: e.memset(onesf[:], 1.0), w=["onesf"])
        A("dve", lambda e: e.memset(onesm[:], 1.0 / 64.0), w=["onesm"])
        A("dve", lambda e: e.memset(onesm[0:1, :], 0.0), w=["onesm"])
        A("dve", lambda e: e.tensor_copy(out=onesmB[:], in_=onesm[:]), r=["onesm"], w=["onesmB"])
        A("dve", lambda e: e.memset(RB[:, :, :], 1.0), w=["RB"])
        for l in range(DEPTH):
            b0 = l * 128
            A("dve", lambda e, b0=b0: e.tensor_tensor(out=lamt[:, b0:b0 + 32], in0=lamt[:, b0:b0 + 32],
                                                     in1=lamt[:, b0 + 32:b0 + 64], op=ALU.mult), r=["lamt"], w=["lamt"])
            A("dve", lambda e, b0=b0: e.tensor_tensor(out=lamt[:, b0 + 64:b0 + 96], in0=lamt[:, b0 + 64:b0 + 96],
                                                     in1=lamt[:, b0 + 96:b0 + 128], op=ALU.mult), r=["lamt"], w=["lamt"])
            A("dve", lambda e, b0=b0, l=l: e.tensor_reduce(out=lamw[:, 4 * l:4 * l + 1], in_=lamt[:, b0:b0 + 32],
                                                          axis=mybir.AxisListType.X, op=ALU.add), r=["lamt"], w=["lamw"])
            A("dve", lambda e, b0=b0, l=l: e.tensor_reduce(out=lamw[:, 4 * l + 1:4 * l + 2], in_=lamt[:, b0 + 64:b0 + 96],
                                                          axis=mybir.AxisListType.X, op=ALU.add), r=["lamt", "lamw"], w=["lamw"])
            A("act", lambda e, l=l: e.activation(out=lamw[:, 4 * l + 2:4 * l + 4], in_=lamw[:, 4 * l:4 * l + 2], func=AF.Exp),
              r=["lamw"], w=["lamw"])
            A("dve", lambda e, l=l: e.scalar_tensor_tensor(out=neglam[:, l:l + 1], in0=lamw[:, 4 * l + 3:4 * l + 4],
                                                          scalar=-lam_init_fn(l), in1=lamw[:, 4 * l + 2:4 * l + 3],
                                                          op0=ALU.add, op1=ALU.subtract), r=["lamw"], w=["neglam"])
            A("dve", lambda e, l=l: e.tensor_scalar(out=gcol[0:65, l:l + 1], in0=gcol[0:65, l:l + 1],
                                                   scalar1=1.0 - lam_init_fn(l), scalar2=None, op0=ALU.mult),
              r=["gcol"], w=["gcol"])

        def load_gb(g_h, b_h, off):
            A("sp", lambda e: e.dma_start(out=gbt[:, 0, :], in_=bass.AP(g_h, off, [[0, 128], [1, 1024]])), w=["gbt"], dma="gbt")
            A("sp", lambda e: e.dma_start(out=gbt[:, 1, :], in_=bass.AP(b_h, off, [[0, 128], [1, 1024]])), w=["gbt"], dma="gbt")

        def evac(out_ap, in_ap, r, w, scale=None, bias=None, force=None):
            cnt["ev"] += 1
            if bias is not None:
                r = list(r) + ["bfc"]
            eng = force or ("act" if cnt["ev"] % 2 else "dve")
            if eng == "act":
                if bias is not None:
                    A("act", lambda e: e.activation(out=out_ap, in_=in_ap, func=AF.Identity, bias=bias), r=r, w=w)
                elif scale is not None:
                    A("act", lambda e: e.activation(out=out_ap, in_=in_ap, func=AF.Copy, scale=scale), r=r, w=w)
                else:
                    A("act", lambda e: e.activation(out=out_ap, in_=in_ap, func=AF.Copy), r=r, w=w)
            else:
                if bias is not None:
                    A("dve", lambda e: e.tensor_scalar(out=out_ap, in0=in_ap, scalar1=bias, scalar2=None, op0=ALU.add), r=r, w=w)
                elif scale is not None:
                    A("dve", lambda e: e.tensor_scalar(out=out_ap, in0=in_ap, scalar1=scale, scalar2=None, op0=ALU.mult), r=r, w=w)
                else:
                    A("dve", lambda e: e.tensor_copy(out=out_ap, in_=in_ap), r=r, w=w)

        def psnext():
            i = cnt["ps"] % 8
            cnt["ps"] += 1
            return i // 2, i % 2

        def psap(pi, b, rows=128):
            return PS[pi][0:rows, b * 512:(b + 1) * 512]

        def ln_stats(src, src_key, wk):
            st_, mv_ = lnst[wk], lnmv[wk]
            A("dve", lambda e: (e.bn_stats(out=st_[:, 0:6], in_=src[:, 0:512]),
                                e.bn_stats(out=st_[:, 6:12], in_=src[:, 512:1024]))[-1], r=[src_key], w=[("lnst", wk)])
            A("dve", lambda e: e.bn_aggr(out=mv_[:, 0:2], in_=st_[:, 0:12]), r=[("lnst", wk)], w=[("lnmv", wk)])
            A("act", lambda e: e.activation(out=mv_[:, 2:3], in_=mv_[:, 1:2], func=AF.Ln, bias=LN_EPS, scale=1.0),
              r=[("lnmv", wk)], w=[("lnmv2", wk)])
            A("act", lambda e: e.activation(out=mv_[:, 3:4], in_=mv_[:, 2:3], func=AF.Exp, scale=-0.5),
              r=[("lnmv2", wk)], w=[("lnmv3", wk)])
            A("act", lambda e: e.mul(out=mv_[:, 4:5], in_=mv_[:, 0:1], mul=-1.0), r=[("lnmv", wk)], w=[("lnmv4", wk)])
            A("act", lambda e: e.activation(out=mv_[:, 5:6], in_=mv_[:, 4:5], func=AF.Copy, scale=mv_[:, 3:4]),
              r=[("lnmv3", wk), ("lnmv4", wk)], w=[("lnmv5", wk)])

        def ln_apply_a(src, src_key, wk, tt, to_hres, to_out, hook=None):
            rt = lnr[wk]
            rk = ("lnr", wk)
            mv_ = lnmv[wk]
            A("act", lambda e: e.activation(out=rt[:], in_=src[:], func=AF.Identity, bias=mv_[:, 5:6], scale=mv_[:, 3:4]),
              r=[src_key, ("lnmv3", wk), ("lnmv5", wk)], w=[rk])
            lnp.emit_stores()
            if hook is not None:
                hook()
            A("pool", lambda e: e.tensor_tensor(out=rt[:], in0=rt[:], in1=gbt[:, 0, :], op=ALU.mult), r=[rk, "gbt"], w=[rk])
            A("pool", lambda e: e.tensor_tensor(out=rt[:], in0=rt[:], in1=gbt[:, 1, :], op=ALU.add), r=[rk, "gbt"], w=[rk])
            rows = slice(tt * 128, (tt + 1) * 128)
            if to_hres:
                lnp.stores.append(lambda: A("act", lambda e: e.dma_start(out=hres[rows, :], in_=rt[:]), r=[rk],
                                            w=[("hres", tt)], dma=("hres_w", wk)))
            if to_out:
                lnp.stores.append(lambda: A("act", lambda e: e.dma_start(out=out_d[rows, :], in_=rt[:]), r=[rk],
                                            w=[("out", tt)], dma=("out_w", wk)))

        def ln_apply_b(wk, tt):
            rt = lnr[wk]
            rk = ("lnr", wk)
            pi = cnt["trbase"] + (cnt["ps"] % 2)
            cnt["ps"] += 1
            pst = PS[pi]

            def tr(e):
                for k in range(8):
                    i = e.transpose(out=pst[:, k * 128:(k + 1) * 128], in_=rt[:, k * 128:(k + 1) * 128], identity=idf[:])
                return i
            A("pe", tr, r=[rk, "idf"], w=[("ps", pi, 0), ("ps", pi, 1)])
            t4 = tt % 4
            evac(hTs[:, :, t4 * 128:(t4 + 1) * 128], pst[:, :].rearrange("p (a b) -> p a b", a=8),
                 [("ps", pi, 0), ("ps", pi, 1)], ["hTs"], force="dve")
            if t4 == 3:
                c = tt // 4
                lnp.stores.append(lambda: A("act", lambda e: e.dma_start(
                    out=hTd[:, c * 512:(c + 1) * 512].rearrange("(kc p) t -> p kc t", p=128), in_=hTs[:, :, :]),
                    r=["hTs"], w=[("hTd", c)], dma="hTs_w"))

        class LNPipe:
            def __init__(self):
                self.q = []
                self.stores = []

            def emit_stores(self):
                while self.stores:
                    self.stores.pop(0)()

            def pre(self):
                if len(self.q) >= 2:
                    ln_apply_b(*self.q.pop(0))

            def push(self, src, src_key, wk, tt, to_hres, to_out, to_hT, hook=None):
                ln_stats(src, src_key, wk % 2)
                ln_apply_a(src, src_key, wk, tt, to_hres, to_out, hook=hook)
                if to_hT:
                    self.q.append((wk, tt))

            def flush(self):
                while self.q:
                    ln_apply_b(*self.q.pop(0))
                self.emit_stores()

        lnp = LNPipe()

        def hk(i):
            return [("hTc", i, a) for a in range(4)]

        def load_hTc(i, c):
            A("sp", lambda e: e.dma_start(out=hTc[i][:, :, :],
                                          in_=hTd[:, c * 512:(c + 1) * 512].rearrange("(kc p) t -> p kc t", p=128)),
              r=[("hTd", c)], w=hk(i), dma=("hTc", i))

        def sweep1_loadw(l):
            wv = w_in[l].rearrange("(kc p) n -> p kc n", p=128)
            for blk in range(2):
                A("pool", lambda e, blk=blk: e.dma_start(out=wbuf[blk][:, :, :], in_=wv[:, :, blk * 512:(blk + 1) * 512]),
                  w=[("wbuf", blk, 0), ("wbuf", blk, 1)], dma=("wbuf", blk))

        def sweep1_chunk(tc, early):
            load_hTc(tc % 2, tc)
            hc = hTc[tc % 2]
            for blk in range(2):
                for m in range(4):
                    if early:
                        i4 = cnt["ps01"] % 4
                        cnt["ps01"] += 1
                        pi, b = i4 // 2, i4 % 2
                    else:
                        pi, b = psnext()

                    def mmqk(e, hc=hc, blk=blk, m=m, pi=pi, b=b):
                        for kc in range(8):
                            i = e.matmul(psap(pi, b), lhsT=wbuf[blk][:, kc, m * 128:(m + 1) * 128],
                                         rhs=hc[:, kc, :], start=(kc == 0), stop=(kc == 7))
                        return i
                    A("pe", mmqk, r=[("wbuf", blk, 0), ("wbuf", blk, 1)] + hk(tc % 2), w=[("ps", pi, b)])
                    si = cnt["stg"] % 3
                    cnt["stg"] += 1
                    evac(stg[si][:, :], psap(pi, b), [("ps", pi, b)], [("stg", si)],
                         scale=(SCALE if blk == 0 else None))
                    r0 = blk * 512 + m * 128
                    A("act" if early else "sp",
                      lambda e, si=si, r0=r0, tc=tc: e.dma_start(out=qkT[r0:r0 + 128, tc * 512:(tc + 1) * 512],
                                                                 in_=stg[si][:, :]),
                      r=[("stg", si)], w=[("qkT", r0 // 128)], dma=("stg_w", si))

        sweep1_loadw(0)
        load_gb(ln_in_g, ln_in_b, 0)
        def load_x(tt):
            if tt < NT:
                A("sp", lambda e: e.dma_start(out=lnx[tt % 2][:], in_=x_d[tt * 128:(tt + 1) * 128, :]),
                  w=[("lnx", tt % 2)], dma=("lnx", tt % 2))

        def load_h(tt, lim=NT):
            if tt < lim:
                A("sp", lambda e: e.dma_start(out=lnx[tt % 2][:], in_=hres[tt * 128:(tt + 1) * 128, :]),
                  r=[("hres", tt)], w=[("lnx", tt % 2)], dma=("lnx", tt % 2))

        load_x(0)
        load_x(1)
        for tt in range(NT):
            xi = lnx[tt % 2]
            lnp.pre()
            lnp.push(xi, ("lnx", tt % 2), tt % 2, tt, True, False, True, hook=(lambda tt=tt: load_x(tt + 2)))
            if tt >= 5 and (tt - 5) % 4 == 0:
                sweep1_chunk((tt - 5) // 4, True)
        lnp.flush()
        sweep1_chunk(7, True)

        def do_layer(l):
            if stop_after == ("ln0",):
                return True
            w_in_v = w_in[l].rearrange("(kc p) n -> p kc n", p=128)
            WAB = catb
            for g in range(4):
                A("sp", lambda e, g=g: e.dma_start(out=wu32, in_=w_in_v[:, :, 1536 + g * 128:1536 + (g + 1) * 128]),
                  w=[K_WU], dma="wu32")
                A("sp", lambda e, g=g: e.dma_start(out=wf32, in_=w_f[l, g]), w=[K_M], dma="wf32")
                pi, b = psnext()
                pi2, b2 = psnext()

                def trw(e, pi=pi, b=b, pi2=pi2, b2=b2):
                    for k in range(8):
                        tgt = psap(pi, b) if k < 4 else psap(pi2, b2)
                        i = e.transpose(out=tgt[:, (k % 4) * 128:(k % 4 + 1) * 128], in_=wu32[:, k, :], identity=idf[:])
                    return i
                A("pe", trw, r=[K_WU, "idf"], w=[("ps", pi, b), ("ps", pi2, b2)])
                evac(wuT[:, 0:512], psap(pi, b), [("ps", pi, b)], [K_WT])
                evac(wuT[:, 512:1024], psap(pi2, b2), [("ps", pi2, b2)], [K_WT])
                pi, b = psnext()

                def mm12(e, pi=pi, b=b):
                    e.matmul(psap(pi, b)[:, 0:128], lhsT=ccf[:], rhs=wf32, start=True, stop=True)
                    return e.matmul(psap(pi, b)[:, 128:256], lhsT=scn[:], rhs=wf32, start=True, stop=True)
                A("pe", mm12, r=["ccf", "scn", K_M], w=[("ps", pi, b)])
                evac(m12, psap(pi, b)[:, 0:256], [("ps", pi, b)], [K_M])
                for q4 in range(4):
                    pi, b = psnext()

                    def mmab3(e, pi=pi, b=b, q4=q4):
                        i = None
                        for k2 in range(2):
                            kc = q4 * 2 + k2
                            i = e.matmul(psap(pi, b)[:, k2 * 256:(k2 + 1) * 256], lhsT=wuT[:, kc * 128:(kc + 1) * 128],
                                         rhs=m12, start=True, stop=True)
                        return i
                    A("pe", mmab3, r=[K_WT, K_M], w=[("ps", pi, b)])
                    hf = g // 2
                    c0 = (g % 2) * 256
                    evac(WAB[hf][:, q4 * 2:q4 * 2 + 2, c0:c0 + 256],
                         psap(pi, b).rearrange("p (k c) -> p k c", k=2), [("ps", pi, b)], [("catb", hf)])

            if l > 0:
                sweep1_loadw(l)
                for tc in range(8):
                    sweep1_chunk(tc, False)
            A("pool", lambda e: e.dma_start(out=wbuf[0][:, :, :], in_=w_in_v[:, :, 1024:1536]),
              w=[("wbuf", 0, 0), ("wbuf", 0, 1)], dma=("wbuf", 0))
            for tc in range(8):
                load_hTc(tc % 2, tc)
                hc = hTc[tc % 2]
                for t4 in range(4):
                    tt = tc * 4 + t4
                    pi, b = psnext()

                    def mmv(e, hc=hc, t4=t4, pi=pi, b=b):
                        for kc in range(8):
                            i = e.matmul(psap(pi, b), lhsT=hc[:, kc, t4 * 128:(t4 + 1) * 128], rhs=wbuf[0][:, kc, :],
                                         start=(kc == 0), stop=(kc == 7))
                        return i
                    A("pe", mmv, r=[("wbuf", 0, 0), ("wbuf", 0, 1)] + hk(tc % 2), w=[("ps", pi, b)])
                    evac(VA[:, tt, :, 1:65], psap(pi, b).rearrange("p (h c) -> p h c", h=NH), [("ps", pi, b)], ["RB"])
                    for hf in range(2):
                        pi, b = psnext()

                        def mmab4(e, hc=hc, t4=t4, hf=hf, pi=pi, b=b):
                            for kc in range(8):
                                i = e.matmul(psap(pi, b), lhsT=hc[:, kc, t4 * 128:(t4 + 1) * 128], rhs=WAB[hf][:, kc, :],
                                             start=(kc == 0), stop=(kc == 7))
                            return i
                        A("pe", mmab4, r=[("catb", hf)] + hk(tc % 2), w=[("ps", pi, b)])
                        evac(AB[:, tt, hf * 512:(hf + 1) * 512], psap(pi, b), [("ps", pi, b)], [("RA", tt // 4)])
            for i_ in range(16):
                lo_, hi_ = AB[:, i_, :], AB[:, i_ + 16, :]
                A("dve", lambda e, lo_=lo_, hi_=hi_: e.tensor_tensor(out=hi_, in0=lo_, in1=hi_, op=ALU.subtract),
                  r=[("RA", i_ // 4), ("RA", (i_ + 16) // 4)], w=[("RA", (i_ + 16) // 4)])
                A("dve", lambda e, lo_=lo_, hi_=hi_: e.scalar_tensor_tensor(out=lo_, in0=lo_, scalar=2.0, in1=hi_,
                                                                           op0=ALU.mult, op1=ALU.subtract),
                  r=[("RA", i_ // 4), ("RA", (i_ + 16) // 4)], w=[("RA", i_ // 4)])
            if stop_after == ("A", l):
                return True

            ti = 0
            for tc in range(4):
                accE = [psnext() for _ in range(4)]
                accO = [psnext() for _ in range(4)]
                for st_ in range(NT):
                    sl_ = ti % 8
                    ti += 1
                    hb_, a_ = hTc[sl_ // 4], sl_ % 4
                    tk = ("hTc", sl_ // 4, a_)
                    tcs, tss = hb_[:, 2 * a_, :], hb_[:, 2 * a_ + 1, :]
                    A("sp", lambda e, hb_=hb_, a_=a_, st_=st_, tc=tc: e.dma_start(
                        out=hb_[:, 2 * a_:2 * a_ + 2, :], in_=cst_d[st_ * 128:(st_ + 1) * 128, tc, :, :]), w=[tk], dma=("tab", sl_))
                    accs = accE if st_ < 16 else accO

                    def mmf(e, tcs=tcs, tss=tss, st_=st_, accs=accs):
                        for g in range(4):
                            pi, b = accs[g]
                            e.matmul(psap(pi, b), lhsT=AB[:, st_, g * 256:g * 256 + 128], rhs=tcs,
                                     start=(st_ % 16 == 0), stop=False)
                            i = e.matmul(psap(pi, b), lhsT=AB[:, st_, g * 256 + 128:g * 256 + 256], rhs=tss,
                                         start=False, stop=(st_ % 16 == 15))
                        return i
                    A("pe", mmf, r=[tk, ("RA", st_ // 4)], w=[("ps",) + a_ for a_ in accs])
                for g in range(4):
                    pb_ = ptb[g % 3]
                    pkey = ("ptb", g % 3)
                    for par, acc in ((0, accE[g]), (1, accO[g])):
                        pi, b = acc
                        evac(pb_[:, par:1024:2], psap(pi, b), [("ps", pi, b)], [pkey], bias=bfc[:, l * 4 + g:l * 4 + g + 1])
                    r0 = 512 + g * 128
                    A("sp", lambda e, pb_=pb_, r0=r0, tc=tc: e.dma_start(out=catT[r0:r0 + 128, tc * 1024:(tc + 1) * 1024],
                                                                       in_=pb_[:, :]),
                      r=[pkey], w=[("catT", r0 // 128, 2 * tc), ("catT", r0 // 128, 2 * tc + 1)], dma=("ptb_w", g % 3))
            if stop_after == ("B2", l):
                return True

            deferred = []

            def run_deferred(n=1):
                for _ in range(n):
                    if deferred:
                        deferred.pop(0)()

            def _dmin(qc, j):
                if j * 128 + 127 < qc * 512:
                    return qc * 512 - (j * 128 + 127)
                if j * 128 > qc * 512 + 511:
                    return j * 128 - (qc * 512 + 511)
                return 0
            blocks = []
            gidx = 0
            for h in range(NH):
                for qc in range(8):
                    js = [j for j in range(NT) if (2.0 ** -(h + 1)) * _dmin(qc, j) <= SKIP_BIAS]
                    for j in js:
                        blocks.append((h, qc, j, gidx, j == js[0], j == js[-1]))
                    gidx += 1

            def aug(hh, t):
                return RA[:, (hh % 2) * 4 + t, :]

            def load_head(h):
                s_ = h % 2
                for c in range(2):
                    A("sp", lambda e, c=c: e.dma_start(out=aug(h, 0)[c * 64:c * 64 + 32, :],
                                                       in_=qkT[h * 64 + c * 32:h * 64 + c * 32 + 32, :]),
                      r=[("qkT", (h * 64) // 128)], w=rak([s_ * 4 + 0]), dma=("aug", s_, 0))
                    A("sp", lambda e, c=c: e.dma_start(out=aug(h, 0)[c * 64 + 32:c * 64 + 64, :], in_=qx_d[h]),
                      w=rak([s_ * 4 + 0]), dma=("aug", s_, 0))
                    for t, nm in ((1, "lo"), (2, "up"), (3, "dg")):
                        A("sp", lambda e, c=c, t=t: e.dma_start(out=aug(h, t)[c * 64:c * 64 + 32, :],
                                                                in_=qkT[512 + h * 64 + c * 32:512 + h * 64 + c * 32 + 32, :]),
                          r=[("qkT", (512 + h * 64) // 128)], w=rak([s_ * 4 + t]), dma=("aug", s_, t))
                        A("sp", lambda e, c=c, t=t, nm=nm: e.dma_start(out=aug(h, t)[c * 64 + 32:c * 64 + 64, :],
                                                                      in_=kx_d[nm][h]),
                          w=rak([s_ * 4 + t]), dma=("aug", s_, t))

            def emit_S(bi):
                h, qc, j = blocks[bi][0:3]
                si = bi % 2
                s_ = h % 2
                q0 = qc * 512
                Q = aug(h, 0)
                rel = j - 4 * qc

                def mms(e):
                    i = None
                    for c in range(2):
                        pr = slice(c * 64, c * 64 + 64)
                        ob = PS[si][:, c * 512:(c + 1) * 512]
                        kc_ = slice(j * 128, (j + 1) * 128)
                        if rel < 0 or rel > 3:
                            K = aug(h, 1 if rel < 0 else 2)
                            i = e.matmul(ob, lhsT=K[pr, kc_], rhs=Q[pr, q0:q0 + 512], start=True, stop=True)
                        else:
                            lo_c = 128 * rel
                            if lo_c > 0:
                                e.matmul(ob[:, 0:lo_c], lhsT=aug(h, 2)[pr, kc_], rhs=Q[pr, q0:q0 + lo_c],
                                         start=True, stop=True, skip_group_check=True)
                            e.matmul(ob[:, lo_c:lo_c + 128], lhsT=aug(h, 3)[pr, kc_], rhs=Q[pr, q0 + lo_c:q0 + lo_c + 128],
                                     start=True, stop=False, skip_group_check=True)
                            i = e.matmul(ob[:, lo_c:lo_c + 128], lhsT=idb[:, :], rhs=dgt[:, h, :], start=False, stop=True,
                                         skip_group_check=True)
                            if lo_c + 128 < 512:
                                i = e.matmul(ob[:, lo_c + 128:512], lhsT=aug(h, 1)[pr, kc_],
                                             rhs=Q[pr, q0 + lo_c + 128:q0 + 512], start=True, stop=True, skip_group_check=True)
                    return i
                A("pe", mms, r=rak([s_ * 4 + t for t in range(4)]) + ["idb", "dgt"], w=[("ps", si, 0), ("ps", si, 1)])

            def emit_EXP(bi):
                si = bi % 2
                pt = ptb[bi % 3]
                A("act", lambda e: e.activation(out=pt[:, :], in_=PS[si][:, :], func=AF.Exp),
                  r=[("ps", si, 0), ("ps", si, 1)], w=[("ptb", bi % 3)])

            def emit_PV(bi):
                h, qc, j, g_, first, last_ = blocks[bi]
                gi = g_ % 2
                pt = ptb[bi % 3]

                def mmpv(e):
                    for c in range(2):
                        i = e.matmul(PS[2 + gi][0:65, c * 512:(c + 1) * 512], lhsT=VA[:, j, h, :],
                                     rhs=pt[:, c * 512:(c + 1) * 512], start=first, stop=last_)
                    return i
                A("pe", mmpv, r=[("ptb", bi % 3), "RB"], w=[("ps", 2 + gi, 0), ("ps", 2 + gi, 1)])

            def post(h, qc, gi):
                pv = PS[2 + gi]
                pk = [("ps", 2 + gi, 0), ("ps", 2 + gi, 1)]

                def s1():
                    A("dve", lambda e: e.tensor_copy(out=ocs[0:65, :], in_=pv[0:65, :]), r=pk, w=[K_OCS])

                def s1b():
                    def trs(e):
                        for c8 in range(8):
                            i = e.transpose(out=pv[:, c8:c8 + 1], in_=ocs[0:1, c8 * 128:(c8 + 1) * 128], identity=idf[0:1, 0:1])
                        return i
                    A("pe", trs, r=[K_OCS, "idf"], w=[pk[0]])
                    A("dve", lambda e: e.reciprocal(out=Rt, in_=pv[:, 0:8]), r=[pk[0]], w=[K_RCS])
                    A("dve", lambda e: e.tensor_scalar(out=Rt[:, 4:8], in0=Rt[:, 4:8], scalar1=neglam[:, l:l + 1],
                                                       scalar2=None, op0=ALU.mult), r=[K_RCS, "neglam"], w=[K_RCS])
                    A("dve", lambda e: e.tensor_copy(out=Rb65, in_=Rt_b), r=[K_RCS], w=[K_RCS])

                def s2():
                    def mmb(e):
                        for c8 in range(8):
                            i = e.matmul(pv[0:65, c8 * 128:(c8 + 1) * 128], lhsT=Rb65[:, c8, :], rhs=idf[:, :],
                                         start=True, stop=True, skip_group_check=True)
                        return i
                    A("pe", mmb, r=[K_RCS, "idf"], w=pk)
                    A("dve", lambda e: e.tensor_tensor(out=t0s[0:65, :], in0=ocs[0:65, 0:512], in1=pv[0:65, 0:512], op=ALU.mult),
                      r=[K_OCS, pk[0]], w=[K_T])
                    A("dve", lambda e: e.tensor_tensor(out=t1s[0:65, :], in0=ocs[0:65, 512:1024], in1=pv[0:65, 512:1024],
                                                       op=ALU.mult), r=[K_OCS, pk[1]], w=[K_T])
                    A("dve", lambda e: e.tensor_tensor(out=t0s[0:65, :], in0=t0s[0:65, :], in1=t1s[0:65, :], op=ALU.add),
                      r=[K_T], w=[K_T])
                    A("dve", lambda e: e.tensor_tensor(out=sqsB[0:65, :], in0=t0s[0:65, :], in1=t0s[0:65, :], op=ALU.mult),
                      r=[K_T], w=[K_SQ])

                def s3():
                    A("pe", lambda e: e.matmul(pv[0:65, 0:512], lhsT=onesmB[0:65, 0:65], rhs=sqsB[0:65, :], start=True, stop=True),
                      r=[K_SQ, "onesmB"], w=[pk[0]])
                    A("act", lambda e: e.activation(out=rss[0:65, :], in_=pv[0:65, 0:512], func=AF.Ln, bias=SUBLN_EPS, scale=1.0),
                      r=[pk[0]], w=[K_SQ])
                    A("act", lambda e: e.activation(out=rss[0:65, :], in_=rss[0:65, :], func=AF.Exp, scale=-0.5),
                      r=[K_SQ], w=[K_SQ])
                    si = cnt["stg"] % 3
                    cnt["stg"] += 1
                    A("dve", lambda e: e.scalar_tensor_tensor(out=stg[si][0:65, :], in0=t0s[0:65, :],
                                                              scalar=gcol[0:65, l:l + 1], in1=rss[0:65, :],
                                                              op0=ALU.mult, op1=ALU.mult),
                      r=[K_T, K_SQ, "gcol"], w=[("stg", si)])
                    A("sp", lambda e: e.dma_start(out=catT[h * 64:(h + 1) * 64, qc * 512:(qc + 1) * 512], in_=stg[si][1:65, :]),
                      r=[("stg", si)], w=[("catT", "a", h, qc)], dma=("stg_w", si))
                return s1, [(2, s1b), (5, s2), (10, s3)]

            nb = len(blocks)
            load_head(0)
            emit_S(0)
            emit_S(1)
            pos = 0
            for bi in range(nb):
                h, qc, j, g_, first, last_ = blocks[bi]
                if first:
                    pos = 0
                if first and qc == 0 and h + 1 < NH:
                    load_head(h + 1)
                emit_EXP(bi)
                if bi + 2 < nb:
                    emit_S(bi + 2)
                emit_PV(bi)
                while deferred and deferred[0][0] <= pos:
                    deferred.pop(0)[1]()
                pos += 1
                if last_:
                    while deferred:
                        deferred.pop(0)[1]()
                    s1_, rest = post(h, qc, g_ % 2)
                    s1_()
                    deferred.extend(rest)
            while deferred:
                deferred.pop(0)[1]()
            if stop_after == ("B", l):
                return True

            w_o_v = w_o[l].rearrange("(kc p) n -> p kc n", p=128)
            for i in range(2):
                A("pool", lambda e, i=i: e.dma_start(out=wbuf[i][:, :, :], in_=w_o_v[:, :, i * 512:(i + 1) * 512]),
                  w=[("wbuf", i, 0), ("wbuf", i, 1)], dma=("wbuf", i))
            load_gb(ln1_g, ln1_b, l * D)
            cat_keys = [("catT", rr, tc_) for rr in range(4, 8) for tc_ in range(8)] + \
                       [("catT", "a", hh, qq) for hh in range(NH) for qq in range(8)]
            def load_cat(k):
                if k < 8:
                    A("sp", lambda e: e.dma_start(
                        out=catb[k % 2][:, :, :], in_=catT[:, k * 512:(k + 1) * 512].rearrange("(kc p) t -> p kc t", p=128)),
                      r=cat_keys, w=[("catb", k % 2)], dma=("catb", k % 2))

            load_cat(0)
            for tt in range(NT):
                if tt % 4 == 0:
                    cb = catb[(tt // 4) % 2]
                    ck = ("catb", (tt // 4) % 2)
                    load_cat(tt // 4 + 1)
                xi = lnx[tt % 2]
                if tt == 0:
                    load_h(0)
                    load_h(1)
                lnp.pre()
                pi = cnt["ps"] % 2
                cnt["ps"] += 1

                def mmo(e, cb=cb, tt=tt, pi=pi):
                    for cc in range(2):
                        for kc in range(8):
                            i = e.matmul(PS[pi][:, cc * 512:(cc + 1) * 512], lhsT=cb[:, kc, (tt % 4) * 128:(tt % 4 + 1) * 128],
                                         rhs=wbuf[cc][:, kc, :], start=(kc == 0), stop=(kc == 7))
                    return i
                A("pe", mmo, r=[ck] + [("wbuf", i, j_) for i in range(2) for j_ in range(2)], w=[("ps", pi, 0), ("ps", pi, 1)])
                A("dve", lambda e, xi=xi, pi=pi: e.scalar_tensor_tensor(out=xi[:], in0=xi[:], scalar=ALPHA, in1=PS[pi][:, :],
                                                                       op0=ALU.mult, op1=ALU.add),
                  r=[("lnx", tt % 2), ("ps", pi, 0), ("ps", pi, 1)], w=[("lnx", tt % 2)])
                lnp.push(xi, ("lnx", tt % 2), tt % 2, tt, True, False, True, hook=(lambda tt=tt: load_h(tt + 2)))
            lnp.flush()
            if stop_after == ("C", l):
                return True

            A("pool", lambda e: e.dma_start(out=RB[:, :, :], in_=w_dn[l].rearrange("(f p) n -> p f n", p=128)),
              w=["RB"], dma="RB")
            load_gb(ln2_g, ln2_b, l * D)
            cnt["trbase"] = 0
            w_gu_v = w_gu[l].rearrange("(kc p) n -> p kc n", p=128)
            last = (l == DEPTH - 1)
            gi_ = 0
            for tb in range(4):
                load_hTc(0, 2 * tb)
                load_hTc(1, 2 * tb + 1)
                for f in range(NF):
                    ws = gi_ % 4
                    gi_ += 1
                    wt_ = wbuf[ws // 2]
                    o_ = (ws % 2) * 256
                    wk_ = ("wbuf", ws // 2, ws % 2)
                    A("pool", lambda e, wt_=wt_, o_=o_, f=f: e.dma_start(out=wt_[:, :, o_:o_ + 128],
                                                                        in_=w_gu_v[:, :, f * 128:(f + 1) * 128]),
                      w=[wk_], dma=wk_)
                    A("pool", lambda e, wt_=wt_, o_=o_, f=f: e.dma_start(out=wt_[:, :, o_ + 128:o_ + 256],
                                                                        in_=w_gu_v[:, :, DFF + f * 128:DFF + (f + 1) * 128]),
                      w=[wk_], dma=wk_)
                    for tc in range(2):
                        pi = cnt["ps"] % 2
                        cnt["ps"] += 1

                        def mmgu(e, wt_=wt_, o_=o_, tc=tc, pi=pi):
                            for u_ in range(2):
                                for kc in range(8):
                                    i = e.matmul(PS[pi][:, u_ * 512:(u_ + 1) * 512],
                                                 lhsT=wt_[:, kc, o_ + u_ * 128:o_ + (u_ + 1) * 128],
                                                 rhs=hTc[tc][:, kc, :], start=(kc == 0), stop=(kc == 7))
                            return i
                        A("pe", mmgu, r=[wk_] + hk(tc), w=[("ps", pi, 0), ("ps", pi, 1)])
                        sgt = sgs[pi]
                        A("act", lambda e, sgt=sgt, pi=pi: e.activation(out=sgt[:, :], in_=PS[pi][:, 0:512], func=AF.Silu),
                          r=[("ps", pi, 0)], w=[("sgs", pi)])
                        A("dve", lambda e, sgt=sgt, pi=pi, f=f, tc=tc: e.tensor_tensor(
                            out=ACTT[:, f, tc * 512:(tc + 1) * 512], in0=sgt[:, :], in1=PS[pi][:, 512:1024], op=ALU.mult),
                          r=[("sgs", pi), ("ps", pi, 1)], w=[("RA", f // 4)])
                for t8 in range(8):
                    tt = tb * 8 + t8
                    xi = lnx[tt % 2]
                    if t8 == 0:
                        load_h(tt)
                        load_h(tt + 1)
                    lnp.pre()
                    pi = 2 + (cnt["ps"] % 2)
                    cnt["ps"] += 1

                    def mmd(e, t8=t8, pi=pi):
                        for cc in range(2):
                            for f in range(NF):
                                i = e.matmul(PS[pi][:, cc * 512:(cc + 1) * 512], lhsT=ACTT[:, f, t8 * 128:(t8 + 1) * 128],
                                             rhs=RB[:, f, cc * 512:(cc + 1) * 512], start=(f == 0), stop=(f == NF - 1))
                        return i
                    A("pe", mmd, r=rak(range(6)) + ["RB"], w=[("ps", pi, 0), ("ps", pi, 1)])
                    A("dve", lambda e, xi=xi, pi=pi: e.scalar_tensor_tensor(out=xi[:], in0=xi[:], scalar=ALPHA, in1=PS[pi][:, :],
                                                                           op0=ALU.mult, op1=ALU.add),
                      r=[("lnx", tt % 2), ("ps", pi, 0), ("ps", pi, 1)], w=[("lnx", tt % 2)])
                    lnp.push(xi, ("lnx", tt % 2), tt % 2, tt, not last, last, not last,
                             hook=(lambda tt=tt, tb=tb: load_h(tt + 2, (tb + 1) * 8)))
                lnp.flush()
            cnt["trbase"] = 2
            if not last:
                A("dve", lambda e: e.memset(RB[:, :, :], 1.0), w=["RB"])
            if stop_after == ("D", l):
                return True

            return False

        for l_ in range(DEPTH):
            if do_layer(l_):
                break

        fin_keys = [k for k in p.res.keys() if isinstance(k, tuple) and k[0] in ("out", "hres", "qkT", "catT", "hTd")]
        A("sp", lambda e: None, r=fin_keys)
        p.emit()
    return nc


_NC = {}


def _get_nc():
    if "nc" not in _NC:
        _NC["nc"] = build()
    return _NC["nc"]


def _in_maps(inputs):
    c = _consts()
    f32 = lambda a: np.ascontiguousarray(np.asarray(a, dtype=np.float32))
    shared = {
        "ln_in_g": f32(inputs["ln_in_g"]), "ln_in_b": f32(inputs["ln_in_b"]),
        "w_in": f32(inputs["w_in"]), "lam_params": f32(inputs["lam_params"]).reshape(-1),
        "subln_g": f32(inputs["subln_g"]), "w_f": f32(inputs["w_f"]), "b_f": f32(inputs["b_f"]),
        "w_o": f32(inputs["w_o"]), "ln1_g": f32(inputs["ln1_g"]).reshape(-1), "ln1_b": f32(inputs["ln1_b"]).reshape(-1),
        "w_gu": f32(inputs["w_gu"]), "w_down": f32(inputs["w_down"]),
        "ln2_g": f32(inputs["ln2_g"]).reshape(-1), "ln2_b": f32(inputs["ln2_b"]).reshape(-1),
    }
    shared.update(c)
    x = f32(inputs["x"])
    return [dict(shared, x=x[b]) for b in range(x.shape[0])]


def kernel(**inputs):
    nc = _get_nc()
    maps = _in_maps(inputs)
    res = run_bass_kernel_spmd(nc, maps, core_ids=list(range(8)))
    return np.stack([np.asarray(r["out"], dtype=np.float32) for r in res.results], axis=0)
```
